# Optimizing a Trainium2 kernel written in Bass

```python
import math
import jax, jax.numpy as jnp
from jax import lax
import numpy as np

D_MODEL = 1024
BATCH = 8
SEQ = 2048
DEPTH = 1
DEC_BATCH = 128
DEC_SEQ = 1
PAST_LEN = 16384
PAGE_SIZE = 128

MIX_WIDTH = D_MODEL
MLSTM_WIDTH = MIX_WIDTH // 2
POOL_WIDTH = MIX_WIDTH - MLSTM_WIDTH
N_HEADS = 4
HEAD_DIM = MLSTM_WIDTH // N_HEADS
POOL_WINDOWS = (2, 4, 8, 16)
N_POOL_GROUPS = len(POOL_WINDOWS)
POOL_GROUP_DIM = POOL_WIDTH // N_POOL_GROUPS
POOL_BUF = max(POOL_WINDOWS) - 1
D_FF = 4 * D_MODEL
CHUNK = 64
EPS = 1e-6
IN_COLS = 4 * MLSTM_WIDTH + 2 * N_HEADS + POOL_WIDTH

kernel_name = "hymba_mlstm_pool_adaln_step"


def _rmsnorm(x, g):
    xf = x.astype(jnp.float32)
    y = xf * lax.rsqrt(jnp.mean(xf * xf, axis=-1, keepdims=True) + EPS)
    return y * g.astype(jnp.float32)


def _mlstm_chunk(carry, xs):
    C0, n0, m0 = carry
    q, k, v, logf, ig = xs
    L = q.shape[2]
    b = jnp.cumsum(logf, axis=-1)
    causal = jnp.tril(jnp.ones((L, L), dtype=bool))
    dmat = b[..., :, None] - b[..., None, :] + ig[..., None, :]
    dmat = jnp.where(causal, dmat, -jnp.inf)
    g = b + m0[..., None]
    m = jnp.maximum(g, jnp.max(dmat, axis=-1))
    w = jnp.exp(dmat - m[..., None])
    a = jnp.exp(g - m)
    s = jnp.einsum('bhtd,bhsd->bhts', q, k) * w
    num = a[..., None] * jnp.einsum('bhtd,bhde->bhte', q, C0) + jnp.einsum('bhts,bhse->bhte', s, v)
    den = a * jnp.einsum('bhtd,bhd->bht', q, n0) + jnp.sum(s, axis=-1)
    h = num / jnp.maximum(jnp.abs(den), jnp.exp(-m))[..., None]
    mL = m[..., -1]
    aL = jnp.exp(g[..., -1] - mL)
    wL = jnp.exp(b[..., -1:] - b + ig - mL[..., None])
    C1 = aL[..., None, None] * C0 + jnp.einsum('bhs,bhsd,bhse->bhde', wL, k, v)
    n1 = aL[..., None] * n0 + jnp.einsum('bhs,bhsd->bhd', wL, k)
    return (C1, n1, mL), h


def _mlstm_sequence(q, k, v, logf, ig, C0, n0, m0):
    B, H, L, Dh = q.shape
    lc = CHUNK if L % CHUNK == 0 else L
    nc = L // lc

    def to_chunks(t):
        t = t.reshape(t.shape[:2] + (nc, lc) + t.shape[3:])
        return jnp.moveaxis(t, 2, 0)

    xs = (to_chunks(q), to_chunks(k), to_chunks(v), to_chunks(logf), to_chunks(ig))
    (C1, n1, m1), h = lax.scan(_mlstm_chunk, (C0, n0, m0), xs)
    h = jnp.moveaxis(h, 0, 2).reshape(B, H, L, Dh)
    return h, C1, n1, m1


def _pool_mixer(u, buf, pos0, w_pool, pool_scale):
    B, L, P = u.shape
    ext = jnp.concatenate([buf.astype(jnp.float32), u], axis=1)
    cs = jnp.cumsum(ext, axis=1)
    cs = jnp.pad(cs, ((0, 0), (1, 0), (0, 0)))
    pos = (pos0 + jnp.arange(L)).astype(jnp.float32)
    means = []
    for gi, win in enumerate(POOL_WINDOWS):
        sl = slice(gi * POOL_GROUP_DIM, (gi + 1) * POOL_GROUP_DIM)
        end = cs[:, POOL_BUF + 1:, sl]
        start = cs[:, POOL_BUF + 1 - win:POOL_BUF + 1 - win + L, sl]
        cnt = jnp.minimum(pos + 1.0, float(win))[None, :, None]
        means.append((end - start) / cnt)
    pooled = jnp.concatenate(means, axis=-1) - u
    pooled = pooled.reshape(B, L, N_POOL_GROUPS, POOL_GROUP_DIM)
    out = jnp.einsum('blgc,gcd->blgd', pooled, w_pool.astype(jnp.float32)).reshape(B, L, P)
    return out * pool_scale.astype(jnp.float32), ext[:, -POOL_BUF:]


def _layer(x, c, C0, n0, m0, buf, pos0, w_ada, b_ada, g_pre1, g_post1, w_in, b_ig, b_fg,
           g_head, w_pool, pool_scale, w_out, g_pre2, g_post2, w_up, w_down):
    dt = x.dtype
    B, L, _ = x.shape
    f32 = jnp.float32
    mod = jnp.einsum('bd,de->be', jax.nn.silu(c.astype(f32)), w_ada.astype(f32)) + b_ada.astype(f32)
    sh1, sc1, ga1, sh2, sc2, ga2 = jnp.split(mod[:, None, :], 6, axis=-1)

    hn = (_rmsnorm(x, g_pre1) * (1.0 + sc1) + sh1).astype(dt)
    z = jnp.einsum('bld,de->ble', hn, w_in).astype(f32)
    Wm = MLSTM_WIDTH
    q, k, v, o = z[..., :Wm], z[..., Wm:2 * Wm], z[..., 2 * Wm:3 * Wm], z[..., 3 * Wm:4 * Wm]
    ig = z[..., 4 * Wm:4 * Wm + N_HEADS] + b_ig.astype(f32)
    fg = z[..., 4 * Wm + N_HEADS:4 * Wm + 2 * N_HEADS] + b_fg.astype(f32)
    u = z[..., 4 * Wm + 2 * N_HEADS:]

    def heads(t):
        return t.reshape(B, L, N_HEADS, HEAD_DIM).transpose(0, 2, 1, 3)

    qh, kh, vh = heads(q), heads(k) * (HEAD_DIM ** -0.5), heads(v)
    logf = jax.nn.log_sigmoid(fg).transpose(0, 2, 1)
    igh = ig.transpose(0, 2, 1)
    h, C1, n1, m1 = _mlstm_sequence(qh, kh, vh, logf, igh, C0.astype(f32), n0.astype(f32), m0.astype(f32))
    h = h.transpose(0, 2, 1, 3)
    h = _rmsnorm(h, g_head) * jax.nn.sigmoid(heads(o).transpose(0, 2, 1, 3))
    h = h.reshape(B, L, Wm)

    p_out, buf1 = _pool_mixer(u, buf, pos0, w_pool, pool_scale)
    mix = jnp.concatenate([h, p_out], axis=-1).astype(dt)
    mix = jnp.einsum('blm,md->bld', mix, w_out)
    x = (x.astype(f32) + ga1 * _rmsnorm(mix, g_post1)).astype(dt)

    hn2 = (_rmsnorm(x, g_pre2) * (1.0 + sc2) + sh2).astype(dt)
    f = jnp.square(jax.nn.relu(jnp.einsum('bld,df->blf', hn2, w_up)))
    f = jnp.einsum('blf,fd->bld', f, w_down)
    x = (x.astype(f32) + ga2 * _rmsnorm(f, g_post2)).astype(dt)
    return x, C1, n1, m1, buf1


def setup_inputs(seed: int = 0) -> dict:
    key = jax.random.key(seed)
    ks = jax.random.split(key, 24)
    nrm = jax.random.normal
    f32 = jnp.float32
    d = {}
    d['x_prompt'] = nrm(ks[0], (BATCH, SEQ, D_MODEL), f32)
    d['x_sample'] = nrm(ks[1], (DEC_BATCH, DEC_SEQ, D_MODEL), f32)
    d['c_prompt'] = nrm(ks[2], (BATCH, D_MODEL), f32)
    d['c_sample'] = nrm(ks[3], (DEC_BATCH, D_MODEL), f32)
    d['state_C'] = nrm(ks[4], (DEPTH, DEC_BATCH, N_HEADS, HEAD_DIM, HEAD_DIM), f32) * HEAD_DIM ** -0.5
    d['state_n'] = nrm(ks[5], (DEPTH, DEC_BATCH, N_HEADS, HEAD_DIM), f32) * 0.5
    d['state_m'] = nrm(ks[6], (DEPTH, DEC_BATCH, N_HEADS), f32)
    d['state_pool'] = nrm(ks[7], (DEPTH, DEC_BATCH, POOL_BUF, POOL_WIDTH), f32)
    d['w_ada'] = nrm(ks[8], (DEPTH, D_MODEL, 6 * D_MODEL), f32) * D_MODEL ** -0.5
    d['b_ada'] = nrm(ks[9], (DEPTH, 6 * D_MODEL), f32) * 0.02
    d['g_pre1'] = 1.0 + 0.05 * nrm(ks[10], (DEPTH, D_MODEL), f32)
    d['g_post1'] = 1.0 + 0.05 * nrm(ks[11], (DEPTH, D_MODEL), f32)
    d['w_in'] = nrm(ks[12], (DEPTH, D_MODEL, IN_COLS), f32) * D_MODEL ** -0.5
    d['b_ig'] = 0.1 * nrm(ks[13], (DEPTH, N_HEADS), f32)
    d['b_fg'] = jnp.linspace(3.0, 6.0, N_HEADS, dtype=f32)[None, :] + 0.1 * nrm(ks[14], (DEPTH, N_HEADS), f32)
    d['g_head'] = 1.0 + 0.05 * nrm(ks[15], (DEPTH, HEAD_DIM), f32)
    d['w_pool'] = nrm(ks[16], (DEPTH, N_POOL_GROUPS, POOL_GROUP_DIM, POOL_GROUP_DIM), f32) * POOL_GROUP_DIM ** -0.5
    d['pool_scale'] = 0.5 + 0.1 * nrm(ks[17], (DEPTH, POOL_WIDTH), f32)
    d['w_out'] = nrm(ks[18], (DEPTH, MIX_WIDTH, D_MODEL), f32) * MIX_WIDTH ** -0.5
    d['g_pre2'] = 1.0 + 0.05 * nrm(ks[19], (DEPTH, D_MODEL), f32)
    d['g_post2'] = 1.0 + 0.05 * nrm(ks[20], (DEPTH, D_MODEL), f32)
    d['w_up'] = nrm(ks[21], (DEPTH, D_MODEL, D_FF), f32) * D_MODEL ** -0.5
    d['w_down'] = nrm(ks[22], (DEPTH, D_FF, D_MODEL), f32) * D_FF ** -0.5
    return d


def reference(x_prompt, x_sample, c_prompt, c_sample, state_C, state_n, state_m, state_pool,
              w_ada, b_ada, g_pre1, g_post1, w_in, b_ig, b_fg, g_head, w_pool, pool_scale,
              w_out, g_pre2, g_post2, w_up, w_down):
    f32 = jnp.float32
    xp, xs = x_prompt, x_sample
    Cp_l, np_l, mp_l, pp_l = [], [], [], []
    Cs_l, ns_l, ms_l, ps_l = [], [], [], []
    for l in range(DEPTH):
        w = (w_ada[l], b_ada[l], g_pre1[l], g_post1[l], w_in[l], b_ig[l], b_fg[l], g_head[l],
             w_pool[l], pool_scale[l], w_out[l], g_pre2[l], g_post2[l], w_up[l], w_down[l])
        C0 = jnp.zeros((BATCH, N_HEADS, HEAD_DIM, HEAD_DIM), f32)
        n0 = jnp.zeros((BATCH, N_HEADS, HEAD_DIM), f32)
        m0 = jnp.zeros((BATCH, N_HEADS), f32)
        b0 = jnp.zeros((BATCH, POOL_BUF, POOL_WIDTH), f32)
        xp, Cp, npv, mp, pp = _layer(xp, c_prompt, C0, n0, m0, b0, 0, *w)
        xs, Cs, nsv, ms, ps = _layer(xs, c_sample, state_C[l], state_n[l], state_m[l], state_pool[l], PAST_LEN, *w)
        Cp_l.append(Cp.astype(x_prompt.dtype)); np_l.append(npv.astype(x_prompt.dtype))
        mp_l.append(mp.astype(x_prompt.dtype)); pp_l.append(pp.astype(x_prompt.dtype))
        Cs_l.append(Cs.astype(state_C.dtype)); ns_l.append(nsv.astype(state_n.dtype))
        ms_l.append(ms.astype(state_m.dtype)); ps_l.append(ps.astype(state_pool.dtype))
    return (xp, xs,
            jnp.stack(Cp_l), jnp.stack(np_l), jnp.stack(mp_l), jnp.stack(pp_l),
            jnp.stack(Cs_l), jnp.stack(ns_l), jnp.stack(ms_l), jnp.stack(ps_l))
```

```python
import contextlib
import numpy as np
import concourse.bass as bass
import concourse.mybir as mybir
from concourse.bass_utils import run_bass_kernel_spmd

F32 = mybir.dt.float32
BF16 = mybir.dt.bfloat16
AF = mybir.ActivationFunctionType
ALU = mybir.AluOpType
AX = mybir.AxisListType

ENGS = ("pe", "act", "dve", "pool", "sp")
NCORES = 8
D = 1024
SEQ = 2048
NS = 16
TT = 512
NTILE = SEQ // TT
EPS = 1e-6
SLOT = 8192
NSLOT = 3


class Prog:
    def __init__(self, nc, stack):
        self.nc = nc
        self.stack = stack
        self.streams = {e: [] for e in ENGS}
        self.esem = {e: stack.enter_context(nc.semaphore("S_" + e)) for e in ENGS if e != "sp"}
        self.ecount = {e: 0 for e in ENGS}
        self.slot_sem = {}
        self.slot_count = {}
        self.last_writer = {}
        self.readers = {}
        self.waited = {e: {} for e in ENGS}
        self.out_tokens = []
        self.off = False

    def _deps(self, eng, reads, writes):
        toks = []
        for b in reads:
            t = self.last_writer.get(b)
            if t is not None:
                toks.append(t)
        for b in writes:
            t = self.last_writer.get(b)
            if t is not None:
                toks.append(t)
            toks.extend(self.readers.get(b, ()))
        w = self.waited[eng]
        best = {}
        for (sem, val, key) in toks:
            if w.get(key, 0) >= val:
                continue
            if key not in best or best[key][1] < val:
                best[key] = (sem, val)
        waits = []
        for key, (sem, val) in best.items():
            w[key] = val
            waits.append((sem, val))
        return waits

    def _record(self, tok, reads, writes):
        for b in reads:
            self.readers.setdefault(b, []).append(tok)
        for b in writes:
            self.last_writer[b] = tok
            self.readers[b] = []

    def op(self, eng, fn, reads=(), writes=()):
        if self.off:
            return None
        waits = self._deps(eng, reads, writes)
        self.ecount[eng] += 1
        tok = (self.esem[eng], self.ecount[eng], "E" + eng)
        self.streams[eng].append((fn, waits, (self.esem[eng], 1)))
        self._record(tok, reads, writes)
        return tok

    def dma(self, queue, fn, slot, reads=(), writes=(), is_output=False):
        if self.off:
            return None
        waits = self._deps(queue, reads, writes)
        if slot not in self.slot_sem:
            self.slot_sem[slot] = self.stack.enter_context(self.nc.semaphore("D_" + str(slot)))
            self.slot_count[slot] = 0
        self.slot_count[slot] += 16
        tok = (self.slot_sem[slot], self.slot_count[slot], "D" + str(slot))
        self.streams[queue].append((fn, waits, (self.slot_sem[slot], 16)))
        self._record(tok, reads, writes)
        if is_output:
            self.out_tokens.append(tok)
        return tok

    def emit(self):
        nc = self.nc
        fin = {}
        for (sem, val, key) in self.out_tokens:
            if key not in fin or fin[key][1] < val:
                fin[key] = (sem, val)
        final_waits = list(fin.values())

        def run(engine, name):
            for (fn, waits, inc) in self.streams[name]:
                for (sem, val) in waits:
                    engine.wait_ge(sem, val)
                ins = fn(engine)
                ins.then_inc(inc[0], inc[1])
            if name == "sp":
                for (sem, val) in final_waits:
                    engine.wait_ge(sem, val)

        with nc.allow_non_contiguous_dma(reason="small strided state/param transfers"), nc.Block() as block:
            @block.tensor
            def _(e):
                run(e, "pe")

            @block.scalar
            def _(e):
                run(e, "act")

            @block.vector
            def _(e):
                run(e, "dve")

            @block.gpsimd
            def _(e):
                run(e, "pool")

            @block.sync
            def _(e):
                run(e, "sp")


WBLOCKS = [("w_in", 0, 1024, 8), ("w_in", 1024, 1024, 8), ("w_in", 2048, 520, 8),
           ("w_out", 0, 1024, 8),
           ("w_up", 0, 1024, 8), ("w_up", 1024, 1024, 8), ("w_up", 2048, 1024, 8), ("w_up", 3072, 1024, 8),
           ("w_down", 0, 256, 32), ("w_down", 256, 256, 32), ("w_down", 512, 256, 32), ("w_down", 768, 256, 32)]


def build_nc(stop=99):
    nc = bass.Bass("TRN2", target_bir_lowering=False)
    dt_in = lambda name, shape: nc.dram_tensor(name, shape, F32, kind="ExternalInput").ap()
    dt_out = lambda name, shape: nc.dram_tensor(name, shape, F32, kind="ExternalOutput").ap()
    x_p = dt_in("x_p", [SEQ, D]); x_s = dt_in("x_s", [NS, D]); c_all = dt_in("c_all", [NS + 1, D])
    sC = dt_in("sC", [NS, 4, 128, 128]); sn = dt_in("sn", [NS, 512]); sm = dt_in("sm", [NS, 4])
    spool = dt_in("spool", [NS, 15, 512])
    w_ada = dt_in("w_ada", [D, 6 * D]); b_ada = dt_in("b_ada", [6 * D])
    gvec = {n: dt_in(n, [D]) for n in ("g_pre1", "g_post1", "g_pre2", "g_post2")}
    W = {"w_in": dt_in("w_in", [D, 2568]), "w_out": dt_in("w_out", [D, D]),
         "w_up": dt_in("w_up", [D, 4 * D]), "w_down": dt_in("w_down", [4 * D, D])}
    b_ig = dt_in("b_ig", [4]); b_fg = dt_in("b_fg", [4]); g_head = dt_in("g_head", [128])
    w_pool = dt_in("w_pool", [4, 128, 128]); pool_scale = dt_in("pool_scale", [512])
    y_p = dt_out("y_p", [SEQ, D]); y_s = dt_out("y_s", [NS, D])
    C_p = dt_out("C_p", [4, 128, 128]); n_p = dt_out("n_p", [4, 128]); m_p = dt_out("m_p", [4, 1])
    pool_p = dt_out("pool_p", [15, 512])
    C_s = dt_out("C_s", [NS, 4, 128, 128]); n_s = dt_out("n_s", [NS, 512]); m_s = dt_out("m_s", [NS, 4])
    pool_s = dt_out("pool_s", [NS, 15, 512])
    WB = {"w_in": nc.dram_tensor("wb_in", [D, 2568], BF16, kind="Internal").ap(),
          "w_out": nc.dram_tensor("wb_out", [D, D], BF16, kind="Internal").ap(),
          "w_up": nc.dram_tensor("wb_up", [D, 4 * D], BF16, kind="Internal").ap(),
          "w_down": nc.dram_tensor("wb_dn", [4 * D, D], BF16, kind="Internal").ap()}
    muscr = nc.dram_tensor("muscr", [2, 4, 513], F32, kind="Internal").ap()

    with contextlib.ExitStack() as st:
        P = Prog(nc, st)

        def stage(n):
            if n >= stop:
                P.off = True
        sb = lambda name, shape, dt=F32: st.enter_context(nc.sbuf_tensor(name, shape, dt))
        ps = lambda name, shape, dt=F32: st.enter_context(nc.psum_tensor(name, shape, dt))

        slots = [sb(f"slot{i}", [128, SLOT], BF16) for i in range(NSLOT)]
        xt = sb("xt", [128, 4, D]); ysb = sb("ysb", [128, 4, D])
        arena = sb("arena", [128, 32 * 512], BF16)
        fT = arena[:, :].rearrange("p (c t) -> p c t", c=32)
        hT = sb("hT", [128, 8, TT], BF16)
        qT = sb("qT", [128, 4, TT], BF16); kT = sb("kT", [128, 4, TT], BF16)
        vtok = sb("vtok", [128, 4, 4, 129], BF16); ktok = sb("ktok", [128, 4, 512], BF16)
        GO = sb("GO", [128, 4, 512], BF16); uext = sb("uext", [128, 4, 16 + TT])
        mubc = sb("mubc", [128, 4, 513]); abc = sb("abc", [128, 4, 128])
        tmpA = sb("tmpA", [128, 528]); tmpB = sb("tmpB", [128, 528])
        junk = sb("junk", [128, 512], BF16)
        GA = [sb("GA1", [128, D]), sb("GA2", [128, D])]
        GAs = [sb("GA1s", [NS, D]), sb("GA2s", [NS, D])]
        modT = [sb(f"modT{i}", [128, 8, NS + 1]) for i in range(4)]
        Cst = sb("Cst", [128, 4, 129]); Cbf = sb("Cbf", [128, 4, 129], BF16)
        kw = sb("kw", [128, 4, 128], BF16); STb = sb("STb", [128, 4, 128], BF16); qaT = sb("qaT", [128, 4, 128], BF16)
        htok = sb("htok", [128, 512])
        identb = sb("identb", [NS, NS], BF16); ident = sb("ident", [128, 128]); mbig = sb("mbig", [128, 128]); ghbc = sb("ghbc", [128, 128])
        wpool_b = sb("wpool_b", [128, 4, 128], BF16); pscol = sb("pscol", [128, 4])
        invcnt = sb("invcnt", [128, 4, 16])
        ones4 = sb("ones4", [4, 512]); epsc = sb("epsc", [128, 1]); onec = sb("onec", [128, 1])
        gb = sb("gb", [4, 2])
        negB1 = sb("negB", [4, 513]); Mu1 = sb("Mu", [4, 513]); Ug = sb("Ug", [4, 512])
        ucol = sb("ucol", [128, 4, 2, 4])
        sm4 = sb("sm4", [128, 80])
        mfin = sb("mfin", [4, 1])
        xt_s = sb("xt_s", [NS, D])
        hT_s = sb("hT_s", [128, 8, NS], BF16); fT_s = sb("fT_s", [128, 32, NS], BF16)
        pooledT_s = sb("pooledT_s", [128, 4, NS], BF16)
        sms = sb("sms", [NS, 80])
        ar32 = arena[:, :].bitcast(F32)
        mm = [ps(f"mm{i}", [128, 512]) for i in range(3)]
        stp = ps("stp", [128, 512]); nump = ps("nump", [128, 1024]); cup = ps("cup", [128, 1024])
        numv = nump[:, :].rearrange("p (h c) -> p h c", h=4)
        cupv = cup[:, :].rearrange("p (h c) -> p h c", h=4)
        stv = stp[:, :].rearrange("p (h c) -> p h c", h=4)
        mmctr = [0]

        mmmode = {"dense": True}
        mmbanks = [(mm[0][:, :], "mm0"), (mm[1][:, :], "mm1"), (mm[2][:, :], "mm2"),
                   (stp[:, :], "stp"), (nump[:, 0:512], "nump"), (cup[:, 0:512], "cup")]

        def nextmm():
            n = 6 if mmmode["dense"] else 3
            i = mmctr[0] % n
            mmctr[0] += 1
            return mmbanks[i]

        stage(0)
        wl = {"n": 0}

        def load_block(b):
            s = wl["n"] % NSLOT
            wl["n"] += 1
            src, c0, ncol, K = WBLOCKS[b]
            view = slots[s][:, 0:K * ncol].rearrange("p (k n) -> p k n", k=K)
            P.dma("sp", lambda e: e.dma_start(out=view, in_=WB[src][:, c0:c0 + ncol].rearrange("(k p) n -> p k n", p=128)),
                  slot=f"wl{s}", reads=CASTNAME[b], writes=[f"slot{s}"])
            return view, f"slot{s}"

        P.op("pool", lambda e: e.memset(ident[:, :], 1.0), writes=["ident"])
        P.op("pool", lambda e: e.affine_select(out=ident[:, :], in_=ident[:, :], pattern=[[-1, 128]],
                                               compare_op=ALU.is_equal, fill=0.0, base=0, channel_multiplier=1),
             reads=["ident"], writes=["ident"])
        P.op("pool", lambda e: e.tensor_copy(out=identb[:, :], in_=ident[0:NS, 0:NS]), reads=["ident"], writes=["identb"])
        P.op("pool", lambda e: e.memset(mbig[:, :], 0.0), writes=["mbig"])
        P.op("pool", lambda e: e.affine_select(out=mbig[:, :], in_=mbig[:, :], pattern=[[1, 128]],
                                               compare_op=ALU.is_ge, fill=1.0e4, base=0, channel_multiplier=-1),
             reads=["mbig"], writes=["mbig"])
        P.op("pool", lambda e: e.memset(ones4[:, :], 1.0), writes=["ones4"])
        P.op("pool", lambda e: e.memset(epsc[:, :], EPS), writes=["epsc"])
        P.op("pool", lambda e: e.memset(onec[:, :], 1.0), writes=["onec"])
        P.op("pool", lambda e: e.memset(Cst[:, :, :], 0.0), writes=["Cst"])
        P.op("pool", lambda e: e.memset(Cbf[:, :, :], 0.0), writes=["Cbf"])
        P.op("pool", lambda e: e.memset(vtok[:, :, :, :], 1.0), writes=["vtok"])
        P.op("pool", lambda e: e.memset(uext[:, :, 0:16], 0.0), writes=["uext"])
        P.op("pool", lambda e: e.memset(Mu1[:, :], 0.0), writes=["Mu"])
        P.op("pool", lambda e: e.memset(negB1[:, :], 0.0), writes=["negB"])
        P.op("pool", lambda e: e.iota(invcnt[:, 0, :], [[1, 16]], base=1, channel_multiplier=0,
                                      allow_small_or_imprecise_dtypes=True), writes=["invcnt"])
        for g in range(1, 4):
            P.op("pool", lambda e, g=g: e.tensor_copy(out=invcnt[:, g, :], in_=invcnt[:, 0, :]),
                 reads=["invcnt"], writes=["invcnt"])
        for g in range(4):
            P.op("pool", lambda e, g=g: e.tensor_scalar(out=invcnt[:, g, :], in0=invcnt[:, g, :],
                                                        scalar1=float(2 ** (g + 1)), scalar2=None, op0=ALU.min),
                 reads=["invcnt"], writes=["invcnt"])
        P.op("dve", lambda e: e.reciprocal(out=invcnt[:, :, :], in_=invcnt[:, :, :]), reads=["invcnt"], writes=["invcnt"])
        P.dma("sp", lambda e: e.dma_start(out=ghbc[:, :], in_=g_head.partition_broadcast(128)), slot="c0", writes=["ghbc"])
        P.dma("sp", lambda e: e.dma_start(out=gb[:, 0:1], in_=b_ig.rearrange("(h o) -> h o", o=1)), slot="c1", writes=["gb0"])
        P.dma("sp", lambda e: e.dma_start(out=gb[:, 1:2], in_=b_fg.rearrange("(h o) -> h o", o=1)), slot="c2", writes=["gb1"])
        P.op("dve", lambda e: e.tensor_scalar(out=gb[:, 1:2], in0=gb[:, 1:2], scalar1=-1.0, scalar2=None, op0=ALU.mult),
             reads=["gb1"], writes=["gb1"])
        P.dma("sp", lambda e: e.dma_start(out=pscol[:, :], in_=pool_scale.rearrange("(g c) -> c g", c=128)),
              slot="c3", writes=["pscol"])

        NM = NS + 1
        call = ar32[0:NM, 0:1024]; sil = ar32[0:NM, 1024:2048]
        sel16_t = sb("sel16_t", [NM, 128]); siluT_t = sb("siluT_t", [128, 8, NM])
        modt = [htok[0:NM, :]] * 2; badab = [abc[0:NM, :, :].rearrange("p h c -> p (h c)")] * 2; gvb = [tmpB[0:NM, 0:512]] * 2
        sel16 = sel16_t[:, :]
        siluT = siluT_t[:, :, :]
        P.dma("sp", lambda e: e.dma_start(out=call, in_=c_all[:, :]), slot="c4", writes=["call"])
        P.op("act", lambda e: e.activation(out=sil, in_=call, func=AF.Exp, scale=-1.0), reads=["call"], writes=["sil"])
        P.op("dve", lambda e: e.tensor_scalar(out=sil, in0=sil, scalar1=1.0, scalar2=None, op0=ALU.add), reads=["sil"], writes=["sil"])
        P.op("dve", lambda e: e.reciprocal(out=sil, in_=sil), reads=["sil"], writes=["sil"])
        P.op("dve", lambda e: e.tensor_tensor(out=sil, in0=sil, in1=call, op=ALU.mult), reads=["sil", "call"], writes=["sil"])
        P.op("pool", lambda e: e.memset(sel16, 1.0), writes=["sel16"])
        P.op("pool", lambda e: e.affine_select(out=sel16, in_=sel16, pattern=[[0, 128]], compare_op=ALU.is_equal,
                                               fill=0.0, base=-NS, channel_multiplier=1), reads=["sel16"], writes=["sel16"])
        bk, bkn = nextmm()
        def _silT(e):
            for k in range(8):
                ins = e.transpose(out=bk[:, k * NM:(k + 1) * NM], in_=sil[:, k * 128:(k + 1) * 128], identity=ident[0:NM, 0:NM])
            return ins
        P.op("pe", _silT, reads=["sil", "ident"], writes=[bkn])
        P.op("dve", lambda e: e.tensor_copy(out=siluT, in_=bk[:, 0:8 * NM].rearrange("p (k m) -> p k m", k=8)),
             reads=[bkn], writes=["siluT"])
        def do_mod(t):
            kind, half = t // 2, t % 2
            pp = 0
            wv, wn = None, None
            s = wl["n"] % NSLOT
            wl["n"] += 1
            wv = slots[s][:, :].bitcast(F32).rearrange("p (k n) -> p k n", k=8)
            P.dma("sp", lambda e, s=s, t=t, wv=wv: e.dma_start(
                out=wv, in_=w_ada[:, t * 512:(t + 1) * 512].rearrange("(k p) n -> p k n", p=128)),
                slot=f"wl{s}", writes=[f"slot{s}"] + (["adaload3"] if t == 3 else []))
            P.dma("sp", lambda e, t=t, pp=pp: e.dma_start(out=badab[pp], in_=b_ada[t * 512:(t + 1) * 512].partition_broadcast(NM)),
                  slot=f"bad{pp}", writes=["abc"])
            if kind in (1, 2, 4, 5):
                gname = {1: "g_pre1", 2: "g_post1", 4: "g_pre2", 5: "g_post2"}[kind]
                P.dma("sp", lambda e, pp=pp, gname=gname, half=half: e.dma_start(
                    out=gvb[pp], in_=gvec[gname][half * 512:(half + 1) * 512].partition_broadcast(NM)),
                    slot="tmpB", writes=["tmpB"])
            bk, bkn = nextmm()
            def _modmm(e, wv=wv, bk=bk):
                for k in range(8):
                    ins = e.matmul(bk[0:NM, :], lhsT=siluT[:, k, :], rhs=wv[:, k, :], start=(k == 0), stop=(k == 7))
                return ins
            P.op("pe", _modmm, reads=["siluT", f"slot{s}"], writes=[bkn])
            P.op("dve", lambda e, pp=pp, bk=bk: e.tensor_tensor(out=modt[pp], in0=bk[0:NM, :], in1=badab[pp], op=ALU.add),
                 reads=[bkn, "abc"], writes=["htok"])
            if kind in (1, 4):
                P.op("dve", lambda e, pp=pp: e.scalar_tensor_tensor(out=modt[pp], in0=modt[pp], scalar=1.0, in1=gvb[pp],
                                                                    op0=ALU.add, op1=ALU.mult),
                     reads=["htok", "tmpB"], writes=["htok"])
            if kind in (2, 5):
                P.op("dve", lambda e, pp=pp: e.tensor_tensor(out=modt[pp], in0=modt[pp], in1=gvb[pp], op=ALU.mult),
                     reads=["htok", "tmpB"], writes=["htok"])
                gi = 0 if kind == 2 else 1
                P.op("act", lambda e, pp=pp, gi=gi, half=half: e.activation(
                    out=GAs[gi][:, half * 512:(half + 1) * 512], in_=modt[pp][0:NS, :], func=AF.Copy),
                    reads=["htok"], writes=[f"GAs{gi}"])
                bk2, bk2n = nextmm()
                P.op("pe", lambda e, pp=pp, bk2=bk2: e.matmul(bk2[:, :], lhsT=sel16, rhs=modt[pp], start=True, stop=True),
                     reads=["sel16", "htok"], writes=[bk2n])
                P.op("dve", lambda e, gi=gi, half=half, bk2=bk2: e.tensor_copy(out=GA[gi][:, half * 512:(half + 1) * 512], in_=bk2[:, :]),
                     reads=[bk2n], writes=[f"GA{gi}"])
            else:
                mi = {0: 0, 1: 1, 3: 2, 4: 3}[kind]
                bk2, bk2n = nextmm()
                def _modT(e, pp=pp, bk2=bk2):
                    for c in range(4):
                        ins = e.transpose(out=bk2[:, c * NM:(c + 1) * NM], in_=modt[pp][:, c * 128:(c + 1) * 128],
                                          identity=ident[0:NM, 0:NM])
                    return ins
                P.op("pe", _modT, reads=["htok", "ident"], writes=[bk2n])
                P.op("dve", lambda e, mi=mi, half=half, bk2=bk2: e.tensor_copy(
                    out=modT[mi][:, half * 4:(half + 1) * 4, :], in_=bk2[:, 0:4 * NM].rearrange("p (c m) -> p c m", c=4)),
                    reads=[bk2n], writes=[f"modT{mi}"])

        stage(1)
        P.dma("pool", lambda e: e.dma_start(out=WB["w_in"][:, 0:2048], in_=W["w_in"][:, 0:2048], max_dma_last_dim=2048), slot="cast_c_in0", writes=["c_in0"])
        P.dma("pool", lambda e: e.dma_start(out=WB["w_in"][:, 2048:2568], in_=W["w_in"][:, 2048:2568], max_dma_last_dim=2048), slot="cast_c_in1", writes=["c_in1"])
        for t in range(4):
            do_mod(t)
        stage(2)
        def cast(name, dst, src, after="adaload3"):
            P.dma("pool", lambda e: e.dma_start(out=dst, in_=src, max_dma_last_dim=2048), slot="cast_" + name, reads=[after], writes=[name])
        two = lambda ap: ap.rearrange("(a b) n -> a (b n)", b=2)
        P.dma("pool", lambda e: e.dma_start(out=wpool_b[:, :, :], in_=w_pool.rearrange("g c d -> c g d")),
              slot="castwp", writes=["wpool_b"])
        cast("c_out", two(WB["w_out"]), two(W["w_out"]))
        cast("c_up0", WB["w_up"][:, 0:2048], W["w_up"][:, 0:2048])
        cast("c_up1", WB["w_up"][:, 2048:4096], W["w_up"][:, 2048:4096])
        cast("c_dn0", two(WB["w_down"])[0:1024, :], two(W["w_down"])[0:1024, :])
        cast("c_dn1", two(WB["w_down"])[1024:2048, :], two(W["w_down"])[1024:2048, :])
        CASTNAME = {0: ["c_in0"], 1: ["c_in0"], 2: ["c_in1"], 3: ["c_out"], 4: ["c_up0"], 5: ["c_up0"], 6: ["c_up1"], 7: ["c_up1"],
                    8: ["c_dn0", "c_dn1"], 9: ["c_dn0", "c_dn1"], 10: ["c_dn0", "c_dn1"], 11: ["c_dn0", "c_dn1"]}


        class Ctx:
            pass

        def mkctx(kind, i=0):
            c = Ctx()
            c.kind = kind
            if kind == "p":
                c.ng, c.Pn, c.nt = 4, 128, TT
                c.xt = lambda j: xt[:, j, :]
                c.ysb = lambda j: ysb[:, j, :]
                c.xname = lambda j: f"xt{j}"
                c.yname = lambda j: f"ysb{j}"
                c.hT, c.hname = hT, "hT"
                c.fT, c.fname = fT, "fT"
                c.GA = GA
                c.ganame = ["GA0", "GA1"]
                c.src = lambda j: x_p[i * TT + j * 128: i * TT + (j + 1) * 128, :]
                c.dst = lambda j: y_p[i * TT + j * 128: i * TT + (j + 1) * 128, :]
                c.sm = sm4
                c.ssb = 32
                c.pfx = "p"
            else:
                c.ng, c.Pn, c.nt = 1, NS, NS
                c.xt = lambda j: xt_s[:, :]
                c.ysb = lambda j: mubc[:, :, :].rearrange("p h c -> p (h c)")[0:NS, 0:1024]
                c.xname = lambda j: "xt_s"
                c.yname = lambda j: "mubc"
                c.hT, c.hname = hT_s, "hT_s"
                c.fT, c.fname = fT_s, "fT_s"
                c.GA = GAs
                c.ganame = ["GAs0", "GAs1"]
                c.src = lambda j: x_s[:, :]
                c.dst = lambda j: y_s[:, :]
                c.sm = sms
                c.ssb = 64
                c.pfx = "s"
            return c

        def rstd_from_ss(c, col_ss, col_out, n, tag):
            Pn = c.Pn
            P.op("act", lambda e: e.activation(out=c.sm[0:Pn, col_out:col_out + n], in_=c.sm[0:Pn, col_ss:col_ss + n],
                                               func=AF.Ln, scale=1.0 / D, bias=epsc[0:Pn, :]),
                 reads=[c.pfx + tag + "ss"], writes=[c.pfx + tag + "rs"])
            P.op("act", lambda e: e.activation(out=c.sm[0:Pn, col_out:col_out + n], in_=c.sm[0:Pn, col_out:col_out + n],
                                               func=AF.Exp, scale=-0.5),
                 reads=[c.pfx + tag + "rs"], writes=[c.pfx + tag + "rs"])

        def prenorm(c, which):
            Pn = c.Pn
            mSH, mG = modT[2 * which], modT[2 * which + 1]
            tag = f"pn{which}"
            for j in range(c.ng):
                P.op("act", lambda e, j=j: e.activation(out=junk[0:Pn, :], in_=c.xt(j)[:, 0:512], func=AF.Square,
                                                        accum_out=c.sm[0:Pn, j:j + 1]),
                     reads=[c.xname(j)], writes=["junk", c.pfx + tag + "ss"])
                P.op("act", lambda e, j=j: e.activation(out=junk[0:Pn, :], in_=c.xt(j)[:, 512:1024], func=AF.Square,
                                                        accum_out=c.sm[0:Pn, c.ssb + j:c.ssb + 1 + j]),
                     reads=[c.xname(j)], writes=["junk", c.pfx + tag + "ss"])
            P.op("dve", lambda e: e.tensor_tensor(out=c.sm[0:Pn, 0:c.ng], in0=c.sm[0:Pn, 0:c.ng],
                                                  in1=c.sm[0:Pn, c.ssb:c.ssb + c.ng], op=ALU.add),
                 reads=[c.pfx + tag + "ss"], writes=[c.pfx + tag + "ss"])
            rstd_from_ss(c, 0, 4, c.ng, tag)
            for j in range(c.ng):
                P.op("dve", lambda e, j=j: e.tensor_scalar(out=c.ysb(j), in0=c.xt(j), scalar1=c.sm[0:Pn, 4 + j:5 + j],
                                                           scalar2=None, op0=ALU.mult),
                     reads=[c.xname(j), c.pfx + tag + "rs"], writes=[c.yname(j)])
            if c.kind == "p":
                for k in range(8):
                    bk, bkn = nextmm()
                    def _tr(e, k=k, bk=bk):
                        for j in range(4):
                            ins = e.transpose(out=bk[:, j * 128:(j + 1) * 128], in_=c.ysb(j)[:, k * 128:(k + 1) * 128],
                                              identity=ident[:, :])
                        return ins
                    P.op("pe", _tr, reads=[c.yname(j) for j in range(4)] + ["ident"], writes=[bkn])
                    if k % 2 == 0:
                        P.op("dve", lambda e, k=k, bk=bk: e.tensor_scalar(
                            out=c.hT[:, k, :], in0=bk[:, :], scalar1=mG[:, k, NS:NS + 1], scalar2=mSH[:, k, NS:NS + 1],
                            op0=ALU.mult, op1=ALU.add),
                            reads=[bkn, f"modT{2 * which}", f"modT{2 * which + 1}"], writes=[c.hname])
                    else:
                        P.op("act", lambda e, k=k, bk=bk: e.activation(
                            out=c.hT[:, k, :], in_=bk[:, :], func=AF.Identity, scale=mG[:, k, NS:NS + 1], bias=mSH[:, k, NS:NS + 1]),
                            reads=[bkn, f"modT{2 * which}", f"modT{2 * which + 1}"], writes=[c.hname])
            else:
                for hb in range(2):
                    bk, bkn = nextmm()
                    def _tr(e, hb=hb, bk=bk):
                        for kk in range(4):
                            k = hb * 4 + kk
                            ins = e.transpose(out=bk[:, kk * Pn:(kk + 1) * Pn], in_=c.ysb(0)[:, k * 128:(k + 1) * 128],
                                              identity=ident[0:Pn, 0:Pn])
                        return ins
                    P.op("pe", _tr, reads=[c.yname(0), "ident"], writes=[bkn])
                    tv = tmpA[:, 0:4 * NS].rearrange("p (k m) -> p k m", k=4)
                    P.op("dve", lambda e, hb=hb, bk=bk, tv=tv: e.tensor_tensor(
                        out=tv, in0=bk[:, 0:4 * NS].rearrange("p (k m) -> p k m", k=4),
                        in1=mG[:, hb * 4:(hb + 1) * 4, 0:NS], op=ALU.mult),
                        reads=[bkn, f"modT{2 * which + 1}"], writes=["tmpA"])
                    P.op("dve", lambda e, hb=hb, tv=tv: e.tensor_tensor(
                        out=c.hT[:, hb * 4:(hb + 1) * 4, :], in0=tv, in1=mSH[:, hb * 4:(hb + 1) * 4, 0:NS], op=ALU.add),
                        reads=["tmpA", f"modT{2 * which}"], writes=[c.hname])

        def fm_group(c, wv, wn, col0, rhs_hT, rhs_name, K=8, M=128):
            bk, bkn = nextmm()
            def _mm(e):
                for k in range(K):
                    ins = e.matmul(bk[0:M, 0:c.nt], lhsT=wv[:, k, col0:col0 + M], rhs=rhs_hT[:, k, 0:c.nt],
                                   start=(k == 0), stop=(k == K - 1))
                return ins
            P.op("pe", _mm, reads=[wn, rhs_name], writes=[bkn])
            return bk, bkn

        def tm_group(c, j, wv, wn, col0, ncol, lhs_hT, lhs_name, K=8):
            bk, bkn = nextmm()
            Pn = c.Pn
            def _mm(e):
                for k in range(K):
                    ins = e.matmul(bk[0:Pn, 0:ncol], lhsT=lhs_hT[:, k, j * Pn:(j + 1) * Pn], rhs=wv[:, k, col0:col0 + ncol],
                                   start=(k == 0), stop=(k == K - 1))
                return ins
            P.op("pe", _mm, reads=[wn, lhs_name], writes=[bkn])
            return bk, bkn

        def postnorm_all(c, gi, tag, to_ysb=False):
            Pn, ng = c.Pn, c.ng
            for j in range(ng):
                P.op("act", lambda e, j=j: e.activation(out=junk[0:Pn, :], in_=c.ysb(j)[:, 0:512], func=AF.Square,
                                                        accum_out=c.sm[0:Pn, 8 + j:9 + j]),
                     reads=[c.yname(j)], writes=["junk", c.pfx + tag + "ss"])
                P.op("act", lambda e, j=j: e.activation(out=junk[0:Pn, :], in_=c.ysb(j)[:, 512:1024], func=AF.Square,
                                                        accum_out=c.sm[0:Pn, c.ssb + j:c.ssb + 1 + j]),
                     reads=[c.yname(j)], writes=["junk", c.pfx + tag + "ssb"])
                P.op("dve", lambda e, j=j: e.tensor_tensor(out=c.ysb(j), in0=c.ysb(j), in1=c.GA[gi][0:Pn, :], op=ALU.mult),
                     reads=[c.yname(j), c.ganame[gi]], writes=[c.yname(j)])
            P.op("dve", lambda e: e.tensor_tensor(out=c.sm[0:Pn, 8:8 + ng], in0=c.sm[0:Pn, 8:8 + ng],
                                                  in1=c.sm[0:Pn, c.ssb:c.ssb + ng], op=ALU.add),
                 reads=[c.pfx + tag + "ss", c.pfx + tag + "ssb"], writes=[c.pfx + tag + "ss"])
            P.op("act", lambda e: e.activation(out=c.sm[0:Pn, 12:12 + ng], in_=c.sm[0:Pn, 8:8 + ng],
                                               func=AF.Ln, scale=1.0 / D, bias=epsc[0:Pn, :]),
                 reads=[c.pfx + tag + "ss"], writes=[c.pfx + tag + "rs"])
            P.op("act", lambda e: e.activation(out=c.sm[0:Pn, 12:12 + ng], in_=c.sm[0:Pn, 12:12 + ng],
                                               func=AF.Exp, scale=-0.5),
                 reads=[c.pfx + tag + "rs"], writes=[c.pfx + tag + "rs"])
            for j in range(ng):
                if to_ysb:
                    P.op("dve", lambda e, j=j: e.scalar_tensor_tensor(out=c.ysb(j), in0=c.ysb(j), scalar=c.sm[0:Pn, 12 + j:13 + j],
                                                                      in1=c.xt(j), op0=ALU.mult, op1=ALU.add),
                         reads=[c.yname(j), c.pfx + tag + "rs", c.xname(j)], writes=[c.yname(j)])
                else:
                    P.op("dve", lambda e, j=j: e.scalar_tensor_tensor(out=c.xt(j), in0=c.ysb(j), scalar=c.sm[0:Pn, 12 + j:13 + j],
                                                                      in1=c.xt(j), op0=ALU.mult, op1=ALU.add),
                         reads=[c.yname(j), c.pfx + tag + "rs", c.xname(j)], writes=[c.xname(j)])

        cs = mkctx("s")
        XS = [(qT[:, :, :].rearrange("p h t -> p (h t)").bitcast(F32), "qT"),
              (kT[:, :, :].rearrange("p h t -> p (h t)").bitcast(F32), "kT"),
              (ktok[:, :, :].rearrange("p j c -> p (j c)").bitcast(F32), "ktok"),
              (GO[:, :, :].rearrange("p j c -> p (j c)").bitcast(F32), "GO")]

        def early_prenorm_a(inext):
            for j in range(4):
                xs, xn = XS[j]
                P.dma("sp", lambda e, j=j, xs=xs: e.dma_start(out=xs, in_=x_p[inext * TT + j * 128: inext * TT + (j + 1) * 128, :]),
                      slot=f"ldxe{j}", writes=[xn])
            for j in range(4):
                xs, xn = XS[j]
                P.op("act", lambda e, j=j, xs=xs: e.activation(out=junk[:, :], in_=xs[:, 0:512], func=AF.Square,
                                                               accum_out=sm4[:, 56 + j:57 + j]),
                     reads=[xn], writes=["junk", "epn_ss"])
                P.op("act", lambda e, j=j, xs=xs: e.activation(out=junk[:, :], in_=xs[:, 512:1024], func=AF.Square,
                                                               accum_out=sm4[:, 60 + j:61 + j]),
                     reads=[xn], writes=["junk", "epn_ss"])
            P.op("dve", lambda e: e.tensor_tensor(out=sm4[:, 56:60], in0=sm4[:, 56:60], in1=sm4[:, 60:64], op=ALU.add),
                 reads=["epn_ss"], writes=["epn_ss"])
            P.op("act", lambda e: e.activation(out=sm4[:, 64:68], in_=sm4[:, 56:60], func=AF.Ln, scale=1.0 / D, bias=epsc[:, :]),
                 reads=["epn_ss"], writes=["epn_rs"])
            P.op("act", lambda e: e.activation(out=sm4[:, 64:68], in_=sm4[:, 64:68], func=AF.Exp, scale=-0.5),
                 reads=["epn_rs"], writes=["epn_rs"])
            for j in range(4):
                xs, xn = XS[j]
                P.op("dve", lambda e, j=j, xs=xs: e.tensor_scalar(out=xs, in0=xs, scalar1=sm4[:, 64 + j:65 + j], scalar2=None, op0=ALU.mult),
                     reads=[xn, "epn_rs"], writes=[xn])

        def early_prenorm_b():
            mSH, mG = modT[0], modT[1]
            for k in range(8):
                bk, bkn = nextmm()
                def _tr(e, k=k, bk=bk):
                    for j in range(4):
                        ins = e.transpose(out=bk[:, j * 128:(j + 1) * 128], in_=XS[j][0][:, k * 128:(k + 1) * 128], identity=ident[:, :])
                    return ins
                P.op("pe", _tr, reads=[XS[j][1] for j in range(4)] + ["ident"], writes=[bkn])
                if k % 2 == 0:
                    P.op("dve", lambda e, k=k, bk=bk: e.tensor_scalar(
                        out=hT[:, k, :], in0=bk[:, :], scalar1=mG[:, k, NS:NS + 1], scalar2=mSH[:, k, NS:NS + 1],
                        op0=ALU.mult, op1=ALU.add), reads=[bkn, "modT0", "modT1"], writes=["hT"])
                else:
                    P.op("act", lambda e, k=k, bk=bk: e.activation(
                        out=hT[:, k, :], in_=bk[:, :], func=AF.Identity, scale=mG[:, k, NS:NS + 1], bias=mSH[:, k, NS:NS + 1]),
                        reads=[bkn, "modT0", "modT1"], writes=["hT"])

        SAMPLE_ARENA = ["vb", "zs", "hist0", "hist1", "hist2", "hist3", "qTs", "wkTs", "qCs", "abcs", "n0s", "call", "sil"]
        zs = ar32[0:NS, 0:2568]
        hist = ar32[0:NS, 2688:2688 + 26 * 128].rearrange("p (r c) -> p r c", c=128)
        C0s = [ysb[:, i, 0:512].rearrange("p (h e) -> p h e", h=4) for i in range(4)]
        qTs = ar32[:, 7168:7168 + 64].rearrange("p (h b) -> p h b", h=4)
        wkTs = ar32[:, 7232:7232 + 64].rearrange("p (h b) -> p h b", h=4)
        qCs = ar32[:, 7296:7296 + 64]
        abcs = ar32[:, 7360:7360 + 64].rearrange("p (b h) -> p b h", h=4)
        n0s = ar32[0:NS, 7680:7680 + 512]

        def pool_group(g, bufs, tgt, tname, first_tile):
            win = 2 ** (g + 1)
            cur = None
            sh = 1
            bi = 0
            while sh < win:
                dstt, dn = bufs[bi]
                if cur is None:
                    P.op("pool", lambda e, sh=sh, dstt=dstt: e.tensor_tensor(
                        out=dstt[:, sh:528], in0=uext[:, g, sh:528], in1=uext[:, g, 0:528 - sh], op=ALU.add),
                        reads=["uext"], writes=[dn])
                else:
                    ct, cn = cur
                    P.op("pool", lambda e, sh=sh, dstt=dstt, ct=ct: e.tensor_tensor(
                        out=dstt[:, 2 * sh - 1:528], in0=ct[:, 2 * sh - 1:528], in1=ct[:, sh - 1:528 - sh], op=ALU.add),
                        reads=[cn], writes=[dn])
                cur = bufs[bi]
                bi ^= 1
                sh *= 2
            ct, cn = cur
            P.op("dve", lambda e, ct=ct: e.scalar_tensor_tensor(
                out=tgt(g), in0=ct[:, 16:528], scalar=1.0 / win, in1=uext[:, g, 16:528], op0=ALU.mult, op1=ALU.subtract),
                reads=[cn, "uext"], writes=[tname])
            if first_tile:
                ot, on = bufs[bi]
                P.op("dve", lambda e, ct=ct, ot=ot: e.tensor_tensor(out=ot[:, 0:16], in0=ct[:, 16:32], in1=invcnt[:, g, :], op=ALU.mult),
                     reads=[cn, "invcnt"], writes=[on])
                P.op("dve", lambda e, ot=ot: e.tensor_tensor(out=tgt(g)[:, 0:16], in0=ot[:, 0:16], in1=uext[:, g, 16:32], op=ALU.subtract),
                     reads=[on, "uext"], writes=[tname])
            bk, bkn = nextmm()
            P.op("pe", lambda e, bk=bk: e.matmul(bk[:, :], lhsT=wpool_b[:, g, :], rhs=tgt(g), start=True, stop=True),
                 reads=["wpool_b", tname], writes=[bkn])
            P.op("act", lambda e, bk=bk: e.activation(out=hT[:, 4 + g, :], in_=bk[:, :], func=AF.Identity, scale=pscol[:, g:g + 1]),
                 reads=[bkn, "pscol"], writes=["hT"])

        pref = None
        pending_G = []
        pending_post = None
        for i in range(NTILE):
            par = i % 2
            cp = mkctx("p", i)
            ctxs = [cp] + ([cs] if i == 0 else [])
            def load_x(c):
                for j in range(c.ng):
                    P.dma("sp", lambda e, c=c, j=j: e.dma_start(out=c.xt(j), in_=c.src(j)),
                          slot=f"ldx{c.pfx}{j}", writes=[c.xname(j)])
            for c in ctxs:
                if c.kind == "p" and i >= 1:
                    continue
                load_x(c)
                prenorm(c, 0)
            stage(3 + 10 * i)
            wv, wn = pref[0] if pref else load_block(0)
            for c in ctxs:
                if c.kind == "p":
                    for m in range(8):
                        bk, bkn = fm_group(c, wv, wn, m * 128, c.hT, c.hname)
                        if m < 4:
                            P.op("act", lambda e, m=m, bk=bk: e.activation(out=qT[:, m, :], in_=bk[:, :], func=AF.Copy),
                                 reads=[bkn], writes=["qT"])
                        else:
                            P.op("act", lambda e, m=m, bk=bk: e.activation(out=kT[:, m - 4, :], in_=bk[:, :], func=AF.Copy,
                                                                           scale=128.0 ** -0.5),
                                 reads=[bkn], writes=["kT"])
                    for j in range(4):
                        bk, bkn = tm_group(c, j, wv, wn, 512, 512, c.hT, c.hname)
                        P.op("dve", lambda e, j=j, bk=bk: e.tensor_scalar(out=ktok[:, j, :], in0=bk[:, :], scalar1=128.0 ** -0.5,
                                                                          scalar2=None, op0=ALU.mult),
                             reads=[bkn], writes=["ktok"])
                else:
                    for half in range(2):
                        bk, bkn = tm_group(c, 0, wv, wn, half * 512, 512, c.hT, c.hname)
                        P.op("act", lambda e, half=half, bk=bk: e.activation(out=zs[:, half * 512:(half + 1) * 512],
                                                                             in_=bk[0:NS, :], func=AF.Copy),
                             reads=[bkn], writes=["zs", "call", "sil"])
            wv, wn = pref[1] if pref else load_block(1)
            for c in ctxs:
                if c.kind == "p":
                    for j in range(4):
                        bk, bkn = tm_group(c, j, wv, wn, 0, 512, c.hT, c.hname)
                        P.op("act", lambda e, j=j, bk=bk: e.activation(
                            out=vtok[:, j, :, 0:128], in_=bk[:, :].rearrange("p (h c) -> p h c", h=4), func=AF.Copy),
                            reads=[bkn], writes=["vtok"])
                        bk, bkn = tm_group(c, j, wv, wn, 512, 512, c.hT, c.hname)
                        gt, gtn = (tmpA, "tmpA") if j % 2 == 0 else (tmpB, "tmpB")
                        P.op("act", lambda e, bk=bk, gt=gt: e.activation(out=gt[:, 0:512], in_=bk[:, :], func=AF.Exp, scale=-1.0),
                             reads=[bkn], writes=[gtn])
                        P.op("dve", lambda e, gt=gt: e.tensor_scalar(out=gt[:, 0:512], in0=gt[:, 0:512], scalar1=1.0, scalar2=None,
                                                                     op0=ALU.add), reads=[gtn], writes=[gtn])
                        P.op("dve", lambda e, gt=gt: e.reciprocal(out=gt[:, 0:512], in_=gt[:, 0:512]), reads=[gtn], writes=[gtn])
                        P.op("pool", lambda e, j=j, gt=gt: e.tensor_tensor(
                            out=GO[:, j, :].rearrange("p (h c) -> p h c", h=4), in0=gt[:, 0:512].rearrange("p (h c) -> p h c", h=4),
                            in1=ghbc[:, :].unsqueeze(1).to_broadcast([128, 4, 128]), op=ALU.mult),
                            reads=[gtn, "ghbc"], writes=["GO"])
                else:
                    for half in range(2):
                        bk, bkn = tm_group(c, 0, wv, wn, half * 512, 512, c.hT, c.hname)
                        P.op("act", lambda e, half=half, bk=bk: e.activation(out=zs[:, 1024 + half * 512:1024 + (half + 1) * 512],
                                                                             in_=bk[0:NS, :], func=AF.Copy),
                             reads=[bkn], writes=["zs", "call", "sil"])
            wv, wn = pref[2] if pref else load_block(2)
            igb = fgb = None
            for c in ctxs:
                if c.kind == "p":
                    igb, ign = fm_group(c, wv, wn, 0, c.hT, c.hname, M=4)
                    fgb, fgn = fm_group(c, wv, wn, 4, c.hT, c.hname, M=4)
                    P.op("act", lambda e, fgb=fgb: e.activation(out=Ug[:, :], in_=fgb[0:4, :], func=AF.Exp, scale=-1.0, bias=gb[:, 1:2]),
                         reads=[fgn, "gb1"], writes=["Ug"])
                    P.op("act", lambda e: e.activation(out=Ug[:, :], in_=Ug[:, :], func=AF.Ln, bias=onec[0:4, :]),
                         reads=["Ug", "onec"], writes=["Ug"])
                    P.op("dve", lambda e: e.tensor_tensor_scan(out=negB1[:, 1:513], data0=ones4[:, :], data1=Ug[:, :],
                                                               initial=negB1[:, 0:1], op0=ALU.mult, op1=ALU.add),
                         reads=["ones4", "Ug", "negB"], writes=["negB"])
                    P.op("dve", lambda e: e.tensor_copy(out=negB1[:, 0:1], in_=negB1[:, 512:513]), reads=["negB"], writes=["negB"])
                    P.op("dve", lambda e, igb=igb: e.scalar_tensor_tensor(out=Ug[:, :], in0=igb[0:4, :], scalar=gb[:, 0:1],
                                                                          in1=negB1[:, 1:513], op0=ALU.add, op1=ALU.add),
                         reads=[ign, "gb0", "negB"], writes=["Ug"])
                    P.op("dve", lambda e: e.tensor_tensor_scan(out=Mu1[:, 1:513], data0=ones4[:, :], data1=Ug[:, :],
                                                               initial=Mu1[:, 0:1], op0=ALU.mult, op1=ALU.max),
                         reads=["ones4", "Ug", "Mu"], writes=["Mu"])
                    P.op("dve", lambda e: e.tensor_tensor(out=negB1[:, 1:513], in0=negB1[:, 1:513], in1=Mu1[:, 1:513],
                                                          op=ALU.subtract),
                         reads=["negB", "Mu"], writes=["negB"])
                    P.dma("sp", lambda e, par=par: e.dma_start(out=muscr[par], in_=Mu1[:, :]), slot="mus",
                          reads=["Mu"], writes=[f"muscr{par}"])
                    P.dma("sp", lambda e, par=par: e.dma_start(out=mubc[:, :, :].rearrange("p h c -> p (h c)"),
                                                               in_=muscr[par].rearrange("h c -> (h c)").partition_broadcast(128)),
                          slot="mub", reads=[f"muscr{par}"], writes=["mubc"])
                    P.op("dve", lambda e: e.tensor_copy(out=Mu1[:, 0:1], in_=Mu1[:, 512:513]), reads=["Mu"], writes=["Mu"])
                    bk, bkn = nextmm()
                    def _gtr(e, bk=bk):
                        for j in range(4):
                            e.transpose(out=bk[:, j * 8:j * 8 + 4], in_=Ug[:, j * 128:(j + 1) * 128], identity=ident[0:4, 0:4])
                            ins = e.transpose(out=bk[:, j * 8 + 4:j * 8 + 8], in_=negB1[:, 1 + j * 128:1 + (j + 1) * 128], identity=ident[0:4, 0:4])
                        return ins
                    P.op("pe", _gtr, reads=["Ug", "negB", "ident"], writes=[bkn])
                    P.op("dve", lambda e, bk=bk: e.tensor_copy(out=ucol[:, :, :, :].rearrange("p j w h -> p (j w h)"), in_=bk[:, 0:32]),
                         reads=[bkn], writes=["ucol"])
                    P.op("act", lambda e: e.activation(out=ucol[:, :, 1, :], in_=ucol[:, :, 1, :], func=AF.Exp),
                         reads=["ucol"], writes=["ucol"])
                    for g in range(4):
                        bk, bkn = fm_group(c, wv, wn, 8 + g * 128, c.hT, c.hname)
                        P.op("act", lambda e, g=g, bk=bk: e.activation(out=uext[:, g, 16:16 + TT], in_=bk[:, :], func=AF.Copy),
                             reads=[bkn], writes=["uext"])
                else:
                    bk, bkn = tm_group(c, 0, wv, wn, 8, 512, c.hT, c.hname)
                    P.op("act", lambda e, bk=bk: e.activation(out=zs[:, 2048:2560], in_=bk[0:NS, :], func=AF.Copy),
                         reads=[bkn], writes=["zs", "call", "sil"])
                    bk, bkn = tm_group(c, 0, wv, wn, 0, 8, c.hT, c.hname)
                    P.op("act", lambda e, bk=bk: e.activation(out=zs[:, 2560:2568], in_=bk[0:NS, 0:8], func=AF.Copy),
                         reads=[bkn], writes=["zs", "call", "sil"])

            stage(4 + 10 * i)
            mmmode["dense"] = False
            ux3b = uext[:, 3, 16:528].bitcast(BF16)
            STb2 = [(STb[:, :, :], "STb"), (ux3b[:, 0:512].rearrange("p (h c) -> p h c", h=4), "ux3a")]
            qaT2 = [(qaT[:, :, :], "qaT"), (ux3b[:, 512:1024].rearrange("p (h c) -> p h c", h=4), "ux3b")]
            kw2 = [(kw[:, :, :], "kw"), (junk[:, :].rearrange("p (h c) -> p h c", h=4), "junk")]
            nsb2 = [(uext[:, 0, 16:528].rearrange("p (h c) -> p h c", h=4), "ux0"),
                    (uext[:, 1, 16:528].rearrange("p (h c) -> p h c", h=4), "ux1")]
            htok2 = [(htok[:, :], "htok"), (uext[:, 2, 16:528], "ux2")]
            tA = tmpA[:, 0:512].rearrange("p (h c) -> p h c", h=4)
            tB = tmpB[:, 0:512].rearrange("p (h c) -> p h c", h=4)

            def pre(j):
                c0 = j * 128
                STj, STn = STb2[j % 2]
                qaj, qan = qaT2[j % 2]
                kwj, kwn = kw2[j % 2]
                def _qk(e):
                    for h in range(4):
                        ins = e.matmul(stv[:, h, :], lhsT=kT[:, h, j * 128:(j + 1) * 128], rhs=qT[:, h, j * 128:(j + 1) * 128],
                                       start=True, stop=True)
                    return ins
                P.op("pe", _qk, reads=["kT", "qT"], writes=["stp"])
                for h in range(4):
                    P.op("dve", lambda e, h=h: e.scalar_tensor_tensor(
                        out=tA[:, h, :], in0=mubc[:, h, 1 + c0:1 + c0 + 128], scalar=ucol[:, j, 0, h:h + 1], in1=mbig[:, :],
                        op0=ALU.subtract, op1=ALU.max), reads=["mubc", "ucol", "mbig"], writes=["tmpA"])
                P.op("act", lambda e: e.activation(out=tA, in_=tA, func=AF.Exp, scale=-1.0), reads=["tmpA"], writes=["tmpA"])
                P.op("dve", lambda e: e.tensor_tensor(out=STj, in0=stv[:, :, :], in1=tA, op=ALU.mult),
                     reads=["stp", "tmpA"], writes=[STn])
                for h in range(4):
                    P.op("act", lambda e, h=h: e.activation(out=abc[:, h, :], in_=mubc[:, h, 1 + c0:1 + c0 + 128], func=AF.Exp,
                                                            scale=-1.0, bias=mubc[:, h, c0:c0 + 1]),
                         reads=["mubc"], writes=["abc"])
                P.op("pool", lambda e: e.tensor_tensor(out=qaj, in0=qT[:, :, j * 128:(j + 1) * 128], in1=abc[:, :, :], op=ALU.mult),
                     reads=["qT", "abc"], writes=[qan])
                P.op("dve", lambda e: e.tensor_copy(out=sm4[:, 40 + 4 * j:44 + 4 * j], in_=abc[:, :, 127]),
                     reads=["abc"], writes=[f"aLs{j}"])
                P.op("dve", lambda e: e.tensor_tensor(out=sm4[:, 28:32], in0=ucol[:, j, 0, :], in1=mubc[:, :, c0 + 128],
                                                      op=ALU.subtract), reads=["ucol", "mubc"], writes=["wL"])
                P.op("act", lambda e: e.activation(out=sm4[:, 28:32], in_=sm4[:, 28:32], func=AF.Exp), reads=["wL"], writes=["wL"])
                P.op("pool", lambda e: e.tensor_tensor(out=kwj, in0=ktok[:, j, :].rearrange("p (h c) -> p h c", h=4),
                                                       in1=sm4[:, 28:32].unsqueeze(2).to_broadcast([128, 4, 128]), op=ALU.mult),
                     reads=["ktok", "wL"], writes=[kwn])

            def num(j):
                STj, STn = STb2[j % 2]
                qaj, qan = qaT2[j % 2]
                nsj, nsn = nsb2[j % 2]
                def _num(e):
                    for h in range(4):
                        e.matmul(numv[:, h, 0:129], lhsT=qaj[:, h, :], rhs=Cbf[:, h, :], start=True, stop=False)
                        ins = e.matmul(numv[:, h, 0:129], lhsT=STj[:, h, :], rhs=vtok[:, j, h, :], start=False, stop=True)
                    return ins
                P.op("pe", _num, reads=[qan, "Cbf", STn, "vtok"], writes=["nump"])
                P.op("act", lambda e: e.activation(out=nsj, in_=numv[:, :, 0:128], func=AF.Copy), reads=["nump"], writes=[nsn])
                P.op("dve", lambda e: e.tensor_copy(out=sm4[:, 72:76], in_=numv[:, :, 128]), reads=["nump"], writes=["hdenraw"])

            def upd(j):
                kwj, kwn = kw2[j % 2]
                def _cu(e):
                    for h in range(4):
                        ins = e.matmul(cupv[:, h, 0:129], lhsT=kwj[:, h, :], rhs=vtok[:, j, h, :], start=True, stop=True)
                    return ins
                P.op("pe", _cu, reads=[kwn, "vtok"], writes=["cup"])
                for h in range(4):
                    P.op("dve", lambda e, h=h: e.scalar_tensor_tensor(out=Cst[:, h, :], in0=Cst[:, h, :],
                                                                      scalar=sm4[:, 40 + 4 * j + h:41 + 4 * j + h],
                                                                      in1=cupv[:, h, 0:129], op0=ALU.mult, op1=ALU.add),
                         reads=["Cst", f"aLs{j}", "cup"], writes=["Cst"])
                P.op("act", lambda e: e.activation(out=Cbf[:, :, :], in_=Cst[:, :, :], func=AF.Copy), reads=["Cst"], writes=["Cbf"])

            def post_a(j):
                nsj, nsn = nsb2[j % 2]
                htj, htn = htok2[j % 2]
                P.op("act", lambda e: e.activation(out=tB, in_=nsj[:, :, 0:128], func=AF.Square), reads=[nsn], writes=["tmpB"])
                P.op("dve", lambda e: e.tensor_reduce(out=sm4[:, 16:20], in_=tB, axis=AX.X, op=ALU.add), reads=["tmpB"], writes=["hss"])
                P.op("dve", lambda e: e.tensor_scalar(out=sm4[:, 36:40], in0=sm4[:, 72:76], scalar1=-1.0, scalar2=None, op0=ALU.mult),
                     reads=["hdenraw"], writes=["hden2"])
                P.op("dve", lambda e: e.tensor_tensor(out=sm4[:, 20:24], in0=sm4[:, 36:40], in1=sm4[:, 72:76], op=ALU.max),
                     reads=["hdenraw", "hden2"], writes=["hden"])
                P.op("dve", lambda e: e.tensor_tensor(out=sm4[:, 20:24], in0=sm4[:, 20:24], in1=ucol[:, j, 1, :], op=ALU.max),
                     reads=["hden", "ucol"], writes=["hden"])
                P.op("dve", lambda e: e.reciprocal(out=sm4[:, 20:24], in_=sm4[:, 20:24]), reads=["hden"], writes=["hden"])
                P.op("dve", lambda e: e.tensor_tensor(out=sm4[:, 24:28], in0=sm4[:, 20:24], in1=sm4[:, 20:24], op=ALU.mult),
                     reads=["hden"], writes=["hrs"])
                P.op("dve", lambda e: e.tensor_tensor(out=sm4[:, 24:28], in0=sm4[:, 24:28], in1=sm4[:, 16:20], op=ALU.mult),
                     reads=["hrs", "hss"], writes=["hrs"])
                P.op("act", lambda e: e.activation(out=sm4[:, 24:28], in_=sm4[:, 24:28], func=AF.Ln, scale=1.0 / 128, bias=epsc[:, :]),
                     reads=["hrs"], writes=["hrs"])
                P.op("act", lambda e: e.activation(out=sm4[:, 24:28], in_=sm4[:, 24:28], func=AF.Exp, scale=-0.5),
                     reads=["hrs"], writes=["hrs"])
                P.op("dve", lambda e: e.tensor_tensor(out=sm4[:, 24:28], in0=sm4[:, 24:28], in1=sm4[:, 20:24], op=ALU.mult),
                     reads=["hrs", "hden"], writes=["hrs"])
                for h in range(4):
                    eng = "dve" if h % 2 == 0 else "pool"
                    if eng == "dve":
                        P.op("dve", lambda e, h=h: e.scalar_tensor_tensor(
                            out=htj[:, h * 128:(h + 1) * 128], in0=nsj[:, h, 0:128], scalar=sm4[:, 24 + h:25 + h],
                            in1=GO[:, j, h * 128:(h + 1) * 128], op0=ALU.mult, op1=ALU.mult),
                            reads=[nsn, "hrs", "GO"], writes=[htn])
                    else:
                        P.op("dve", lambda e, h=h: e.scalar_tensor_tensor(
                            out=htj[:, h * 128:(h + 1) * 128], in0=nsj[:, h, 0:128], scalar=sm4[:, 24 + h:25 + h],
                            in1=GO[:, j, h * 128:(h + 1) * 128], op0=ALU.mult, op1=ALU.mult),
                            reads=[nsn, "hrs", "GO"], writes=[htn])

            def post_b(j):
                htj, htn = htok2[j % 2]
                bk, bkn = nextmm()
                def _htr(e):
                    for h in range(4):
                        ins = e.transpose(out=bk[:, h * 128:(h + 1) * 128], in_=htj[:, h * 128:(h + 1) * 128], identity=ident[:, :])
                    return ins
                P.op("pe", _htr, reads=[htn, "ident"], writes=[bkn])
                P.op("act", lambda e: e.activation(out=hT[:, 0:4, j * 128:(j + 1) * 128],
                                                   in_=bk[:, :].rearrange("p (h c) -> p h c", h=4), func=AF.Copy),
                     reads=[bkn], writes=["hT"])

            for g in range(4):
                pool_group(g, [(tmpA, "tmpA"), (tmpB, "tmpB")], lambda g: hT[:, 4 + g, :], "hT", i == 0)
            if i == NTILE - 1:
                bk, bkn = nextmm()
                def _ptr(e, bk=bk):
                    for g in range(4):
                        ins = e.transpose(out=bk[0:15, g * 128:(g + 1) * 128], in_=uext[:, g, 513:528], identity=ident[:, :])
                    return ins
                P.op("pe", _ptr, reads=["uext", "ident"], writes=[bkn])
                P.op("dve", lambda e, bk=bk: e.tensor_copy(out=tmpA[0:15, 0:512], in_=bk[0:15, :]), reads=[bkn], writes=["tmpA"])
                P.dma("sp", lambda e: e.dma_start(out=pool_p[:, :], in_=tmpA[0:15, 0:512]), slot="o_pool", reads=["tmpA"], is_output=True)
            else:
                P.op("pool", lambda e: e.tensor_copy(out=uext[:, :, 0:16], in_=uext[:, :, 512:528]), reads=["uext"], writes=["uext"])
            UXN = ["uext", "ux0", "ux1", "ux2", "ux3a", "ux3b"]
            P.op("dve", lambda e: e.memset(sm4[:, 79:80], 0.0), reads=[], writes=UXN)
            steps = [lambda: pre(0), lambda: pre(1),
                     lambda: (num(0), upd(0), post_a(0)), lambda: pre(2),
                     lambda: (num(1), upd(1), post_a(1), post_b(0)), lambda: pre(3),
                     lambda: (num(2), upd(2), post_a(2), post_b(1)),
                     lambda: (num(3), upd(3), post_a(3), post_b(2)), lambda: post_b(3)]
            gq = list(pending_G)
            pending_G = []
            for k, st_ in enumerate(steps):
                st_()
                for _ in range(1 if k == 0 else 2):
                    if gq:
                        gq.pop(0)()
            while gq:
                gq.pop(0)()
            P.op("dve", lambda e: e.memset(sm4[:, 79:80], 0.0), reads=[], writes=UXN)
            if pending_post is not None:
                pending_post()
                pending_post = None
            if i >= 1:
                load_x(cp)

            stage(5 + 10 * i)
            if i == NTILE - 1:
                P.dma("sp", lambda e: e.dma_start(out=C_p.rearrange("h d e -> d h e"), in_=Cst[:, :, 0:128]), slot="o_C",
                      reads=["Cst"], is_output=True)
                P.dma("sp", lambda e: e.dma_start(out=n_p.rearrange("h d -> d h"), in_=Cst[:, :, 128]), slot="o_n",
                      reads=["Cst"], is_output=True)
                P.op("dve", lambda e: e.tensor_scalar(out=mfin[:, :], in0=negB1[:, 512:513], scalar1=-1.0, scalar2=None, op0=ALU.mult),
                     reads=["negB"], writes=["mfin"])
                P.dma("sp", lambda e: e.dma_start(out=m_p[:, :], in_=mfin[:, :]), slot="o_m", reads=["mfin"], is_output=True)

            stage(6 + 10 * i)
            if i == 0:
                sample_mixer(P, nc, locals())
            stage(7 + 10 * i)
            mmmode["dense"] = True

            if i == 0:
                do_mod(4)
                do_mod(5)
            wv, wn = load_block(3)
            for c in ctxs:
                for j in range(c.ng):
                    for half in range(2):
                        bk, bkn = tm_group(c, j, wv, wn, half * 512, 512, c.hT, c.hname)
                        extra3 = []
                        P.op("act", lambda e, c=c, j=j, half=half, bk=bk: e.activation(
                            out=c.ysb(j)[:, half * 512:(half + 1) * 512], in_=bk[0:c.Pn, :], func=AF.Copy),
                            reads=[bkn], writes=[c.yname(j)] + extra3)
                postnorm_all(c, 0, "po1")
            stage(8 + 10 * i)
            if i == 0:
                for t in (6, 7, 8, 9):
                    do_mod(t)
            for c in ctxs:
                prenorm(c, 1)
            stage(9 + 10 * i)
            if i + 1 < NTILE:
                early_prenorm_a(i + 1)
            for fb in range(4):
                wv, wn = load_block(4 + fb)
                for c in ctxs:
                    for m in range(8):
                        bk, bkn = fm_group(c, wv, wn, m * 128, c.hT, c.hname)
                        rt = tmpA if (m % 2 == 0) else tmpB
                        rn = "tmpA" if (m % 2 == 0) else "tmpB"
                        P.op("act", lambda e, c=c, bk=bk, rt=rt: e.activation(out=rt[:, 0:c.nt], in_=bk[:, 0:c.nt], func=AF.Relu),
                             reads=[bkn], writes=[rn])
                        eng = "dve" if (m % 2 == 0) else "pool"
                        extra = SAMPLE_ARENA if (i == 0 and fb == 0 and m == 0 and c.kind == "p") else []
                        P.op(eng, lambda e, c=c, fb=fb, m=m, rt=rt: e.tensor_tensor(
                            out=c.fT[:, fb * 8 + m, 0:c.nt], in0=rt[:, 0:c.nt], in1=rt[:, 0:c.nt], op=ALU.mult),
                            reads=[rn], writes=[c.fname] + extra)
            stage(10 + 10 * i)
            if i == 0:
                do_mod(10)
                do_mod(11)
            if i + 1 < NTILE:
                early_prenorm_b()
            ctxs_i = list(ctxs)

            def make_G(ctxs_i):
                state = {}

                def emit_one(cb, c, j, first):
                    if first:
                        state["w"] = load_block(8 + cb)
                    wv, wn = state["w"]
                    bk, bkn = tm_group(c, j, wv, wn, 0, 256, c.fT, c.fname, K=32)
                    P.op("act", lambda e: e.activation(out=c.ysb(j)[:, cb * 256:(cb + 1) * 256], in_=bk[0:c.Pn, 0:256], func=AF.Copy),
                         reads=[bkn], writes=[c.yname(j)])
                out = []
                for cb in range(4):
                    for ci, c in enumerate(ctxs_i):
                        for j in range(c.ng):
                            out.append(lambda cb=cb, c=c, j=j, first=(ci == 0 and j == 0): emit_one(cb, c, j, first))
                return out

            def make_post(ctxs_i):
                def post2():
                    for c in ctxs_i:
                        postnorm_all(c, 1, "po2", to_ysb=True)
                        for j in range(c.ng):
                            P.dma("sp", lambda e, c=c, j=j: e.dma_start(out=c.dst(j), in_=c.ysb(j)), slot=f"sty{c.pfx}{j}",
                                  reads=[c.yname(j)], is_output=True)
                return post2
            pending_G = make_G(ctxs_i)
            pending_post = make_post(ctxs_i)
            pref = None
        for g_ in pending_G:
            g_()
        pending_post()
        P.emit()
    return nc


def sample_mixer(P, nc, L):
    g = lambda n: L[n]
    zs, hist, C0s, qTs, wkTs, qCs, abcs, n0s = g("zs"), g("hist"), g("C0s"), g("qTs"), g("wkTs"), g("qCs"), g("abcs"), g("n0s")
    sms, ident, ghbc, epsc, tmpA, tmpB, junk = g("sms"), g("ident"), g("ghbc"), g("epsc"), g("tmpA"), g("tmpB"), g("junk")
    onec, stp = g("onec"), g("stp")
    identb, arena = g("identb"), g("arena")
    vb = arena[0:NS, 14848:15360]
    P.op("act", lambda e: e.activation(out=vb, in_=zs[:, 1024:1536], func=AF.Copy), reads=["zs"], writes=["vb"])
    hT_s, pooledT_s, wpool_b, pscol = g("hT_s"), g("pooledT_s"), g("wpool_b"), g("pscol")
    sC, sn, sm, spool = g("sC"), g("sn"), g("sm"), g("spool")
    C_s, n_s, m_s, pool_s = g("C_s"), g("n_s"), g("m_s"), g("pool_s")
    b_ig, b_fg = g("b_ig"), g("b_fg")
    nextmm = g("nextmm")
    ar32 = g("ar32")
    v4 = lambda ap: ap.rearrange("p (h c) -> p h c", h=4)
    q, k, v, o = zs[:, 0:512], zs[:, 512:1024], zs[:, 1024:1536], zs[:, 1536:2048]
    u = zs[:, 2048:2560]
    gates = zs[:, 2560:2568]
    S = lambda a, b: sms[:, a:b]
    P.dma("sp", lambda e: e.dma_start(out=S(52, 56), in_=sm[:, :]), slot="s0", writes=["s_m0"])
    P.dma("sp", lambda e: e.dma_start(out=S(56, 60), in_=b_ig.partition_broadcast(NS)), slot="s1", writes=["s_big"])
    P.dma("sp", lambda e: e.dma_start(out=S(60, 64), in_=b_fg.partition_broadcast(NS)), slot="s2", writes=["s_bfg"])
    P.dma("sp", lambda e: e.dma_start(out=n0s, in_=sn[:, :]), slot="s3", writes=["n0s"])
    for gi in range(4):
        win = 2 ** (gi + 1)
        r0 = [0, 1, 4, 11][gi]
        P.dma("sp", lambda e, gi=gi, win=win, r0=r0: e.dma_start(out=hist[:, r0:r0 + win - 1, :],
                                                                 in_=spool[:, 16 - win:15, gi * 128:(gi + 1) * 128]),
              slot=f"s4{gi}", writes=[f"hist{gi}"])
    P.dma("sp", lambda e: e.dma_start(out=pool_s[:, 0:14, :], in_=spool[:, 1:15, :]), slot="o_ps0", is_output=True)
    P.dma("sp", lambda e: e.dma_start(out=pool_s[:, 14, :], in_=u), slot="o_ps1", reads=["zs"], is_output=True)
    P.op("dve", lambda e: e.tensor_tensor(out=S(16, 20), in0=gates[:, 0:4], in1=S(56, 60), op=ALU.add), reads=["zs", "s_big"], writes=["s_ig"])
    P.op("dve", lambda e: e.tensor_tensor(out=S(20, 24), in0=gates[:, 4:8], in1=S(60, 64), op=ALU.add), reads=["zs", "s_bfg"], writes=["s_fg"])
    P.op("act", lambda e: e.activation(out=S(20, 24), in_=S(20, 24), func=AF.Exp, scale=-1.0), reads=["s_fg"], writes=["s_fg"])
    P.op("act", lambda e: e.activation(out=S(20, 24), in_=S(20, 24), func=AF.Ln, bias=onec[0:NS, :]), reads=["s_fg", "onec"], writes=["s_fg"])
    P.op("dve", lambda e: e.tensor_tensor(out=S(20, 24), in0=S(52, 56), in1=S(20, 24), op=ALU.subtract), reads=["s_fg", "s_m0"], writes=["s_fg"])
    P.op("dve", lambda e: e.tensor_tensor(out=S(24, 28), in0=S(20, 24), in1=S(16, 20), op=ALU.max), reads=["s_fg", "s_ig"], writes=["s_m"])
    P.dma("sp", lambda e: e.dma_start(out=m_s[:, :], in_=S(24, 28)), slot="o_ms", reads=["s_m"], is_output=True)
    P.op("dve", lambda e: e.tensor_tensor(out=S(28, 32), in0=S(16, 20), in1=S(24, 28), op=ALU.subtract), reads=["s_ig", "s_m"], writes=["s_w"])
    P.op("dve", lambda e: e.tensor_tensor(out=S(32, 36), in0=S(20, 24), in1=S(24, 28), op=ALU.subtract), reads=["s_fg", "s_m"], writes=["s_a"])
    P.op("act", lambda e: e.activation(out=S(28, 36), in_=S(28, 36), func=AF.Exp), reads=["s_w", "s_a"], writes=["s_w", "s_a"])
    P.op("act", lambda e: e.activation(out=S(36, 40), in_=S(24, 28), func=AF.Exp, scale=-1.0), reads=["s_m"], writes=["s_e"])
    t512 = tmpA[0:NS, 0:512]
    P.op("dve", lambda e: e.tensor_tensor(out=t512, in0=q, in1=k, op=ALU.mult), reads=["zs"], writes=["tmpA"])
    P.op("dve", lambda e: e.tensor_reduce(out=S(40, 44), in_=v4(t512), axis=AX.X, op=ALU.add), reads=["tmpA"], writes=["s_qk"])
    P.op("dve", lambda e: e.scalar_tensor_tensor(out=S(40, 44), in0=S(40, 44), scalar=128.0 ** -0.5, in1=S(28, 32), op0=ALU.mult, op1=ALU.mult),
         reads=["s_qk", "s_w"], writes=["s_qk"])
    P.op("dve", lambda e: e.tensor_tensor(out=t512, in0=q, in1=n0s, op=ALU.mult), reads=["zs", "n0s", "tmpA"], writes=["tmpA"])
    P.op("dve", lambda e: e.tensor_reduce(out=S(44, 48), in_=v4(t512), axis=AX.X, op=ALU.add), reads=["tmpA"], writes=["s_den"])
    P.op("dve", lambda e: e.tensor_tensor(out=S(44, 48), in0=S(44, 48), in1=S(32, 36), op=ALU.mult), reads=["s_den", "s_a"], writes=["s_den"])
    P.op("dve", lambda e: e.tensor_tensor(out=S(44, 48), in0=S(44, 48), in1=S(40, 44), op=ALU.add), reads=["s_den", "s_qk"], writes=["s_den"])
    P.op("dve", lambda e: e.tensor_scalar(out=S(68, 72), in0=S(44, 48), scalar1=-1.0, scalar2=None, op0=ALU.mult), reads=["s_den"], writes=["s_den2"])
    P.op("dve", lambda e: e.tensor_tensor(out=S(44, 48), in0=S(44, 48), in1=S(68, 72), op=ALU.max), reads=["s_den", "s_den2"], writes=["s_den"])
    P.op("dve", lambda e: e.tensor_tensor(out=S(44, 48), in0=S(44, 48), in1=S(36, 40), op=ALU.max), reads=["s_den", "s_e"], writes=["s_den"])
    P.op("dve", lambda e: e.reciprocal(out=S(44, 48), in_=S(44, 48)), reads=["s_den"], writes=["s_den"])
    tB = tmpB[0:NS, 0:512]
    bc4 = lambda a: a.unsqueeze(2).to_broadcast([NS, 4, 128])
    P.op("dve", lambda e: e.tensor_tensor(out=v4(tB), in0=v4(k), in1=bc4(S(28, 32)), op=ALU.mult), reads=["zs", "s_w"], writes=["tmpB"])
    P.op("dve", lambda e: e.tensor_scalar(out=tB, in0=tB, scalar1=128.0 ** -0.5, scalar2=None, op0=ALU.mult), reads=["tmpB"], writes=["tmpB"])
    P.op("dve", lambda e: e.tensor_tensor(out=v4(n0s), in0=v4(n0s), in1=bc4(S(32, 36)), op=ALU.mult), reads=["n0s", "s_a", "tmpA"], writes=["n0s"])
    P.op("dve", lambda e: e.tensor_tensor(out=n0s, in0=n0s, in1=tB, op=ALU.add), reads=["n0s", "tmpB"], writes=["n0s"])
    P.dma("sp", lambda e: e.dma_start(out=n_s[:, :], in_=n0s), slot="o_ns", reads=["n0s"], is_output=True)
    bk, bkn = nextmm()
    def _tq(e):
        for h in range(4):
            e.transpose(out=bk[:, h * NS:(h + 1) * NS], in_=q[:, h * 128:(h + 1) * 128], identity=ident[0:NS, 0:NS])
        for h in range(4):
            ins = e.transpose(out=bk[:, 64 + h * NS:64 + (h + 1) * NS], in_=tB[:, h * 128:(h + 1) * 128], identity=ident[0:NS, 0:NS])
        return ins
    P.op("pe", _tq, reads=["zs", "tmpB", "ident"], writes=[bkn])
    P.op("dve", lambda e: e.tensor_copy(out=ar32[:, 7168:7168 + 128], in_=bk[:, 0:128]), reads=[bkn], writes=["qTs", "wkTs"])
    for b in range(NS):
        Cb = C0s[b % 4]
        cn = f"ysb{b % 4}"
        if b == 0:
            for bb in range(4):
                P.dma("sp", lambda e, bb=bb: e.dma_start(out=C0s[bb], in_=sC[bb].rearrange("h d e -> d h e")),
                      slot=f"ldC{bb}", writes=[f"ysb{bb}"])
        def _mv(e, b=b, Cb=Cb):
            for h in range(4):
                ins = e.matmul(stp[:, h * NS + b:h * NS + b + 1], lhsT=Cb[:, h, :], rhs=qTs[:, h, b:b + 1], start=True, stop=True)
            return ins
        P.op("pe", _mv, reads=[cn, "qTs"], writes=["stp"])
        selb = ident[0:NS, b:b + 1].to_broadcast([NS, 128])
        selbb = identb[:, b:b + 1].to_broadcast([NS, 128])
        bkv, bkvn = nextmm()
        P.op("pe", lambda e, selbb=selbb, bkv=bkv: e.matmul(bkv[:, :], lhsT=selbb, rhs=vb, start=True, stop=True),
             reads=["identb", "vb"], writes=[bkvn])
        bka, bkan = nextmm()
        P.op("pe", lambda e, selb=selb, bka=bka: e.matmul(bka[:, 0:4], lhsT=selb, rhs=S(32, 36), start=True, stop=True),
             reads=["ident", "s_a"], writes=[bkan])
        P.op("dve", lambda e, b=b, bka=bka: e.tensor_copy(out=abcs[:, b, :], in_=bka[:, 0:4]), reads=[bkan], writes=["abcs"])
        tv = tmpA[:, 0:512]
        P.op("dve", lambda e, b=b, bkv=bkv, tv=tv: e.tensor_tensor(out=v4(tv), in0=v4(bkv[:, :]),
                                                                   in1=wkTs[:, :, b].unsqueeze(2).to_broadcast([128, 4, 128]), op=ALU.mult),
             reads=[bkvn, "wkTs"], writes=["tmpA"])
        P.op("dve", lambda e, b=b, Cb=Cb: e.tensor_tensor(out=Cb, in0=Cb, in1=abcs[:, b, :].unsqueeze(2).to_broadcast([128, 4, 128]), op=ALU.mult),
             reads=[cn, "abcs"], writes=[cn])
        P.op("dve", lambda e, Cb=Cb, tv=tv: e.tensor_tensor(out=Cb, in0=Cb, in1=v4(tv), op=ALU.add), reads=[cn, "tmpA"], writes=[cn])
        P.dma("sp", lambda e, b=b, Cb=Cb: e.dma_start(out=C_s[b].rearrange("h d e -> d h e"), in_=Cb), slot=f"stC{b % 4}",
              reads=[cn], is_output=True)
        if b + 4 < NS:
            P.dma("sp", lambda e, b=b, Cb=Cb: e.dma_start(out=Cb, in_=sC[b + 4].rearrange("h d e -> d h e")),
                  slot=f"ldC{b % 4}", writes=[cn])
    P.op("dve", lambda e: e.tensor_copy(out=qCs, in_=stp[:, 0:64]), reads=["stp"], writes=["qCs"])
    bk2, bk2n = nextmm()
    def _tqc(e):
        for h in range(4):
            ins = e.transpose(out=bk2[0:NS, h * 128:(h + 1) * 128], in_=qCs[:, h * NS:(h + 1) * NS], identity=ident[:, :])
        return ins
    P.op("pe", _tqc, reads=["qCs", "ident"], writes=[bk2n])
    tn = tmpA[0:NS, 0:512]
    P.op("dve", lambda e: e.tensor_tensor(out=v4(tn), in0=v4(bk2[0:NS, :]), in1=bc4(S(32, 36)), op=ALU.mult), reads=[bk2n, "s_a", "tmpA"], writes=["tmpA"])
    tv2 = tmpB[0:NS, 0:512]
    P.op("dve", lambda e: e.tensor_tensor(out=v4(tv2), in0=v4(v), in1=bc4(S(40, 44)), op=ALU.mult), reads=["zs", "s_qk", "tmpB"], writes=["tmpB"])
    P.op("dve", lambda e: e.tensor_tensor(out=tn, in0=tn, in1=tv2, op=ALU.add), reads=["tmpA", "tmpB"], writes=["tmpA"])
    P.op("dve", lambda e: e.tensor_tensor(out=v4(tn), in0=v4(tn), in1=bc4(S(44, 48)), op=ALU.mult), reads=["tmpA", "s_den"], writes=["tmpA"])
    P.op("dve", lambda e: e.tensor_tensor(out=tv2, in0=tn, in1=tn, op=ALU.mult), reads=["tmpA", "tmpB"], writes=["tmpB"])
    P.op("dve", lambda e: e.tensor_reduce(out=S(48, 52), in_=v4(tv2), axis=AX.X, op=ALU.add), reads=["tmpB"], writes=["s_ss"])
    P.op("act", lambda e: e.activation(out=S(48, 52), in_=S(48, 52), func=AF.Ln, scale=1.0 / 128, bias=epsc[0:NS, :]), reads=["s_ss"], writes=["s_ss"])
    P.op("act", lambda e: e.activation(out=S(48, 52), in_=S(48, 52), func=AF.Exp, scale=-0.5), reads=["s_ss"], writes=["s_ss"])
    P.op("dve", lambda e: e.tensor_tensor(out=v4(tn), in0=v4(tn), in1=bc4(S(48, 52)), op=ALU.mult), reads=["tmpA", "s_ss"], writes=["tmpA"])
    P.op("dve", lambda e: e.tensor_tensor(out=v4(tn), in0=v4(tn), in1=ghbc[0:NS, :].unsqueeze(1).to_broadcast([NS, 4, 128]), op=ALU.mult),
         reads=["tmpA", "ghbc"], writes=["tmpA"])
    P.op("act", lambda e: e.activation(out=tv2, in_=o, func=AF.Exp, scale=-1.0), reads=["zs", "tmpB"], writes=["tmpB"])
    P.op("dve", lambda e: e.tensor_scalar(out=tv2, in0=tv2, scalar1=1.0, scalar2=None, op0=ALU.add), reads=["tmpB"], writes=["tmpB"])
    P.op("dve", lambda e: e.reciprocal(out=tv2, in_=tv2), reads=["tmpB"], writes=["tmpB"])
    P.op("dve", lambda e: e.tensor_tensor(out=tn, in0=tn, in1=tv2, op=ALU.mult), reads=["tmpA", "tmpB"], writes=["tmpA"])
    bk3, bk3n = nextmm()
    def _th(e):
        for h in range(4):
            ins = e.transpose(out=bk3[:, h * NS:(h + 1) * NS], in_=tn[:, h * 128:(h + 1) * 128], identity=ident[0:NS, 0:NS])
        return ins
    P.op("pe", _th, reads=["tmpA", "ident"], writes=[bk3n])
    P.op("act", lambda e: e.activation(out=hT_s[:, 0:4, :], in_=bk3[:, 0:64].rearrange("p (h b) -> p h b", h=4), func=AF.Copy),
         reads=[bk3n], writes=["hT_s"])
    tp = tmpB[0:NS, 0:512]
    for gi in range(4):
        win = 2 ** (gi + 1)
        r0 = [0, 1, 4, 11][gi]
        P.op("dve", lambda e, gi=gi, win=win, r0=r0: e.tensor_reduce(
            out=tp[:, gi * 128:(gi + 1) * 128], in_=hist[:, r0:r0 + win - 1, :].rearrange("p r c -> p c r"), axis=AX.X, op=ALU.add),
            reads=[f"hist{gi}", "tmpB"], writes=["tmpB"])
        P.op("dve", lambda e, gi=gi: e.tensor_tensor(out=tp[:, gi * 128:(gi + 1) * 128], in0=tp[:, gi * 128:(gi + 1) * 128],
                                                     in1=u[:, gi * 128:(gi + 1) * 128], op=ALU.add), reads=["tmpB", "zs"], writes=["tmpB"])
        P.op("dve", lambda e, gi=gi, win=win: e.scalar_tensor_tensor(
            out=tp[:, gi * 128:(gi + 1) * 128], in0=tp[:, gi * 128:(gi + 1) * 128], scalar=1.0 / win,
            in1=u[:, gi * 128:(gi + 1) * 128], op0=ALU.mult, op1=ALU.subtract), reads=["tmpB", "zs"], writes=["tmpB"])
    bk4, bk4n = nextmm()
    def _tp(e):
        for gi in range(4):
            ins = e.transpose(out=bk4[:, gi * NS:(gi + 1) * NS], in_=tp[:, gi * 128:(gi + 1) * 128], identity=ident[0:NS, 0:NS])
        return ins
    P.op("pe", _tp, reads=["tmpB", "ident"], writes=[bk4n])
    P.op("act", lambda e: e.activation(out=pooledT_s[:, :, :], in_=bk4[:, 0:64].rearrange("p (h b) -> p h b", h=4), func=AF.Copy),
         reads=[bk4n], writes=["pooledT_s"])
    for gi in range(4):
        bk5, bk5n = nextmm()
        P.op("pe", lambda e, gi=gi, bk5=bk5: e.matmul(bk5[:, 0:NS], lhsT=wpool_b[:, gi, :], rhs=pooledT_s[:, gi, :], start=True, stop=True),
             reads=["wpool_b", "pooledT_s"], writes=[bk5n])
        P.op("act", lambda e, gi=gi, bk5=bk5: e.activation(out=hT_s[:, 4 + gi, :], in_=bk5[:, 0:NS], func=AF.Identity, scale=pscol[:, gi:gi + 1]),
             reads=[bk5n, "pscol"], writes=["hT_s"])


_NC_CACHE = {}


def kernel(x_prompt, x_sample, c_prompt, c_sample, state_C, state_n, state_m, state_pool,
           w_ada, b_ada, g_pre1, g_post1, w_in, b_ig, b_fg, g_head, w_pool, pool_scale,
           w_out, g_pre2, g_post2, w_up, w_down):
    f = lambda a: np.ascontiguousarray(np.asarray(a, dtype=np.float32))
    if "nc" not in _NC_CACHE:
        _NC_CACHE["nc"] = build_nc()
    nc = _NC_CACHE["nc"]
    x_prompt = f(x_prompt); x_sample = f(x_sample); c_prompt = f(c_prompt); c_sample = f(c_sample)
    state_C = f(state_C); state_n = f(state_n); state_m = f(state_m); state_pool = f(state_pool)
    shared = {"w_ada": f(w_ada)[0], "b_ada": f(b_ada)[0], "g_pre1": f(g_pre1)[0], "g_post1": f(g_post1)[0],
              "g_pre2": f(g_pre2)[0], "g_post2": f(g_post2)[0], "w_in": f(w_in)[0], "w_out": f(w_out)[0],
              "w_up": f(w_up)[0], "w_down": f(w_down)[0], "b_ig": f(b_ig)[0], "b_fg": f(b_fg)[0],
              "g_head": f(g_head)[0], "w_pool": f(w_pool)[0], "pool_scale": f(pool_scale)[0]}
    in_maps = []
    for c in range(NCORES):
        sl = slice(c * NS, (c + 1) * NS)
        m = dict(shared)
        m["x_p"] = x_prompt[c]
        m["x_s"] = np.ascontiguousarray(x_sample[sl, 0, :])
        m["c_all"] = np.ascontiguousarray(np.concatenate([c_sample[sl], c_prompt[c:c + 1]], axis=0))
        m["sC"] = np.ascontiguousarray(state_C[0, sl])
        m["sn"] = np.ascontiguousarray(state_n[0, sl].reshape(NS, 512))
        m["sm"] = np.ascontiguousarray(state_m[0, sl])
        m["spool"] = np.ascontiguousarray(state_pool[0, sl])
        in_maps.append(m)
    res = run_bass_kernel_spmd(nc, in_maps, core_ids=list(range(NCORES)))
    R = res.results
    cat = lambda k: np.concatenate([np.asarray(r[k]) for r in R], axis=0)
    y_p = np.stack([np.asarray(r["y_p"]) for r in R], axis=0)
    y_s = cat("y_s").reshape(NCORES * NS, 1, D)
    C_p = np.stack([np.asarray(r["C_p"]) for r in R], axis=0)[None]
    n_p = np.stack([np.asarray(r["n_p"]) for r in R], axis=0)[None]
    m_p = np.stack([np.asarray(r["m_p"]).reshape(4) for r in R], axis=0)[None]
    pool_p = np.stack([np.asarray(r["pool_p"]) for r in R], axis=0)[None]
    C_s = cat("C_s")[None]
    n_s = cat("n_s").reshape(NCORES * NS, 4, 128)[None]
    m_s = cat("m_s")[None]
    pool_s = cat("pool_s")[None]
    return (y_p.astype(np.float32), y_s.astype(np.float32), C_p.astype(np.float32), n_p.astype(np.float32),
            m_p.astype(np.float32), pool_p.astype(np.float32), C_s.astype(np.float32), n_s.astype(np.float32),
            m_s.astype(np.float32), pool_s.astype(np.float32))
```

```python
import contextlib
import numpy as np
import concourse.bass as bass
import concourse.mybir as mybir
from concourse.bass_utils import run_bass_kernel_spmd

F32 = mybir.dt.float32
BF16 = mybir.dt.bfloat16
AF = mybir.ActivationFunctionType
ALU = mybir.AluOpType
AX = mybir.AxisListType

ENGS = ("pe", "act", "dve", "pool", "sp")
NCORES = 8
D = 1024
SEQ = 2048
NS = 16
TT = 512
NTILE = SEQ // TT
EPS = 1e-6
SLOT = 8192
NSLOT = 3


class Prog:
    def __init__(self, nc, stack):
        self.nc = nc
        self.stack = stack
        self.streams = {e: [] for e in ENGS}
        self.esem = {e: stack.enter_context(nc.semaphore("S_" + e)) for e in ENGS if e != "sp"}
        self.ecount = {e: 0 for e in ENGS}
        self.slot_sem = {}
        self.slot_count = {}
        self.last_writer = {}
        self.readers = {}
        self.waited = {e: {} for e in ENGS}
        self.out_tokens = []
        self.off = False

    def _deps(self, eng, reads, writes):
        toks = []
        for b in reads:
            t = self.last_writer.get(b)
            if t is not None:
                toks.append(t)
        for b in writes:
            t = self.last_writer.get(b)
            if t is not None:
                toks.append(t)
            toks.extend(self.readers.get(b, ()))
        w = self.waited[eng]
        best = {}
        for (sem, val, key) in toks:
            if w.get(key, 0) >= val:
                continue
            if key not in best or best[key][1] < val:
                best[key] = (sem, val)
        waits = []
        for key, (sem, val) in best.items():
            w[key] = val
            waits.append((sem, val))
        return waits

    def _record(self, tok, reads, writes):
        for b in reads:
            self.readers.setdefault(b, []).append(tok)
        for b in writes:
            self.last_writer[b] = tok
            self.readers[b] = []

    def op(self, eng, fn, reads=(), writes=()):
        if self.off:
            return None
        waits = self._deps(eng, reads, writes)
        self.ecount[eng] += 1
        tok = (self.esem[eng], self.ecount[eng], "E" + eng)
        self.streams[eng].append((fn, waits, (self.esem[eng], 1)))
        self._record(tok, reads, writes)
        return tok

    def dma(self, queue, fn, slot, reads=(), writes=(), is_output=False):
        if self.off:
            return None
        waits = self._deps(queue, reads, writes)
        if slot not in self.slot_sem:
            self.slot_sem[slot] = self.stack.enter_context(self.nc.semaphore("D_" + str(slot)))
            self.slot_count[slot] = 0
        self.slot_count[slot] += 16
        tok = (self.slot_sem[slot], self.slot_count[slot], "D" + str(slot))
        self.streams[queue].append((fn, waits, (self.slot_sem[slot], 16)))
        self._record(tok, reads, writes)
        if is_output:
            self.out_tokens.append(tok)
        return tok

    def emit(self):
        nc = self.nc
        fin = {}
        for (sem, val, key) in self.out_tokens:
            if key not in fin or fin[key][1] < val:
                fin[key] = (sem, val)
        final_waits = list(fin.values())

        def run(engine, name):
            for (fn, waits, inc) in self.streams[name]:
                for (sem, val) in waits:
                    engine.wait_ge(sem, val)
                ins = fn(engine)
                ins.then_inc(inc[0], inc[1])
            if name == "sp":
                for (sem, val) in final_waits:
                    engine.wait_ge(sem, val)

        with nc.allow_non_contiguous_dma(reason="small strided state/param transfers"), nc.Block() as block:
            @block.tensor
            def _(e):
                run(e, "pe")

            @block.scalar
            def _(e):
                run(e, "act")

            @block.vector
            def _(e):
                run(e, "dve")

            @block.gpsimd
            def _(e):
                run(e, "pool")

            @block.sync
            def _(e):
                run(e, "sp")


WBLOCKS = [("w_in", 0, 1024, 8), ("w_in", 1024, 1024, 8), ("w_in", 2048, 520, 8),
           ("w_out", 0, 1024, 8),
           ("w_up", 0, 1024, 8), ("w_up", 1024, 1024, 8), ("w_up", 2048, 1024, 8), ("w_up", 3072, 1024, 8),
           ("w_down", 0, 256, 32), ("w_down", 256, 256, 32), ("w_down", 512, 256, 32), ("w_down", 768, 256, 32)]


def build_nc(stop=99):
    nc = bass.Bass("TRN2", target_bir_lowering=False)
    dt_in = lambda name, shape: nc.dram_tensor(name, shape, F32, kind="ExternalInput").ap()
    dt_out = lambda name, shape: nc.dram_tensor(name, shape, F32, kind="ExternalOutput").ap()
    x_p = dt_in("x_p", [SEQ, D]); x_s = dt_in("x_s", [NS, D]); c_all = dt_in("c_all", [NS + 1, D])
    sC = dt_in("sC", [NS, 4, 128, 128]); sn = dt_in("sn", [NS, 512]); sm = dt_in("sm", [NS, 4])
    spool = dt_in("spool", [NS, 15, 512])
    w_ada = dt_in("w_ada", [D, 6 * D]); b_ada = dt_in("b_ada", [6 * D])
    gvec = {n: dt_in(n, [D]) for n in ("g_pre1", "g_post1", "g_pre2", "g_post2")}
    W = {"w_in": dt_in("w_in", [D, 2568]), "w_out": dt_in("w_out", [D, D]),
         "w_up": dt_in("w_up", [D, 4 * D]), "w_down": dt_in("w_down", [4 * D, D])}
    b_ig = dt_in("b_ig", [4]); b_fg = dt_in("b_fg", [4]); g_head = dt_in("g_head", [128])
    w_pool = dt_in("w_pool", [4, 128, 128]); pool_scale = dt_in("pool_scale", [512])
    y_p = dt_out("y_p", [SEQ, D]); y_s = dt_out("y_s", [NS, D])
    C_p = dt_out("C_p", [4, 128, 128]); n_p = dt_out("n_p", [4, 128]); m_p = dt_out("m_p", [4, 1])
    pool_p = dt_out("pool_p", [15, 512])
    C_s = dt_out("C_s", [NS, 4, 128, 128]); n_s = dt_out("n_s", [NS, 512]); m_s = dt_out("m_s", [NS, 4])
    pool_s = dt_out("pool_s", [NS, 15, 512])
    WB = {"w_in": nc.dram_tensor("wb_in", [D, 2568], BF16, kind="Internal").ap(),
          "w_out": nc.dram_tensor("wb_out", [D, D], BF16, kind="Internal").ap(),
          "w_up": nc.dram_tensor("wb_up", [D, 4 * D], BF16, kind="Internal").ap(),
          "w_down": nc.dram_tensor("wb_dn", [4 * D, D], BF16, kind="Internal").ap()}
    muscr = nc.dram_tensor("muscr", [2, 4, 513], F32, kind="Internal").ap()

    with contextlib.ExitStack() as st:
        P = Prog(nc, st)

        def stage(n):
            if n >= stop:
                P.off = True
        sb = lambda name, shape, dt=F32: st.enter_context(nc.sbuf_tensor(name, shape, dt))
        ps = lambda name, shape, dt=F32: st.enter_context(nc.psum_tensor(name, shape, dt))

        slots = [sb(f"slot{i}", [128, SLOT], BF16) for i in range(NSLOT)]
        xt = sb("xt", [128, 4, D]); ysb = sb("ysb", [128, 4, D])
        arena = sb("arena", [128, 32 * 512], BF16)
        fT = arena[:, :].rearrange("p (c t) -> p c t", c=32)
        hT = sb("hT", [128, 8, TT], BF16)
        qT = sb("qT", [128, 4, TT], BF16); kT = sb("kT", [128, 4, TT], BF16)
        vtok = sb("vtok", [128, 4, 4, 129], BF16); ktok = sb("ktok", [128, 4, 512], BF16)
        GO = sb("GO", [128, 4, 512], BF16); uext = sb("uext", [128, 4, 16 + TT])
        mubc = sb("mubc", [128, 4, 513]); abc = sb("abc", [128, 4, 128])
        tmpA = sb("tmpA", [128, 528]); tmpB = sb("tmpB", [128, 528])
        junk = sb("junk", [128, 512], BF16)
        GA = [sb("GA1", [128, D]), sb("GA2", [128, D])]
        GAs = [sb("GA1s", [NS, D]), sb("GA2s", [NS, D])]
        modT = [sb(f"modT{i}", [128, 8, NS + 1]) for i in range(4)]
        Cst = sb("Cst", [128, 4, 129]); Cbf = sb("Cbf", [128, 4, 129], BF16)
        kw = sb("kw", [128, 4, 128], BF16); STb = sb("STb", [128, 4, 128], BF16); qaT = sb("qaT", [128, 4, 128], BF16)
        htok = sb("htok", [128, 512])
        identb = sb("identb", [NS, NS], BF16); ident = sb("ident", [128, 128]); mbig = sb("mbig", [128, 128]); ghbc = sb("ghbc", [128, 128])
        wpool_b = sb("wpool_b", [128, 4, 128], BF16); pscol = sb("pscol", [128, 4])
        invcnt = sb("invcnt", [128, 4, 16])
        ones4 = sb("ones4", [4, 512]); epsc = sb("epsc", [128, 1]); onec = sb("onec", [128, 1])
        gb = sb("gb", [4, 2])
        negB1 = sb("negB", [4, 513]); Mu1 = sb("Mu", [4, 513]); Ug = sb("Ug", [4, 512])
        ucol = sb("ucol", [128, 4, 2, 4])
        sm4 = sb("sm4", [128, 80])
        mfin = sb("mfin", [4, 1])
        xt_s = sb("xt_s", [NS, D])
        hT_s = sb("hT_s", [128, 8, NS], BF16); fT_s = sb("fT_s", [128, 32, NS], BF16)
        pooledT_s = sb("pooledT_s", [128, 4, NS], BF16)
        sms = sb("sms", [NS, 80])
        ar32 = arena[:, :].bitcast(F32)
        mm = [ps(f"mm{i}", [128, 512]) for i in range(3)]
        stp = ps("stp", [128, 512]); nump = ps("nump", [128, 1024]); cup = ps("cup", [128, 1024])
        numv = nump[:, :].rearrange("p (h c) -> p h c", h=4)
        cupv = cup[:, :].rearrange("p (h c) -> p h c", h=4)
        stv = stp[:, :].rearrange("p (h c) -> p h c", h=4)
        mmctr = [0]

        mmmode = {"dense": True}
        mmbanks = [(mm[0][:, :], "mm0"), (mm[1][:, :], "mm1"), (mm[2][:, :], "mm2"),
                   (stp[:, :], "stp"), (nump[:, 0:512], "nump"), (cup[:, 0:512], "cup")]

        def nextmm():
            n = 6 if mmmode["dense"] else 3
            i = mmctr[0] % n
            mmctr[0] += 1
            return mmbanks[i]

        stage(0)
        wl = {"n": 0}

        def load_block(b):
            s = wl["n"] % NSLOT
            wl["n"] += 1
            src, c0, ncol, K = WBLOCKS[b]
            view = slots[s][:, 0:K * ncol].rearrange("p (k n) -> p k n", k=K)
            P.dma("sp", lambda e: e.dma_start(out=view, in_=WB[src][:, c0:c0 + ncol].rearrange("(k p) n -> p k n", p=128)),
                  slot=f"wl{s}", reads=CASTNAME[b], writes=[f"slot{s}"])
            return view, f"slot{s}"

        P.op("pool", lambda e: e.memset(ident[:, :], 1.0), writes=["ident"])
        P.op("pool", lambda e: e.affine_select(out=ident[:, :], in_=ident[:, :], pattern=[[-1, 128]],
                                               compare_op=ALU.is_equal, fill=0.0, base=0, channel_multiplier=1),
             reads=["ident"], writes=["ident"])
        P.op("pool", lambda e: e.tensor_copy(out=identb[:, :], in_=ident[0:NS, 0:NS]), reads=["ident"], writes=["identb"])
        P.op("pool", lambda e: e.memset(mbig[:, :], 0.0), writes=["mbig"])
        P.op("pool", lambda e: e.affine_select(out=mbig[:, :], in_=mbig[:, :], pattern=[[1, 128]],
                                               compare_op=ALU.is_ge, fill=1.0e4, base=0, channel_multiplier=-1),
             reads=["mbig"], writes=["mbig"])
        P.op("pool", lambda e: e.memset(ones4[:, :], 1.0), writes=["ones4"])
        P.op("pool", lambda e: e.memset(epsc[:, :], EPS), writes=["epsc"])
        P.op("pool", lambda e: e.memset(onec[:, :], 1.0), writes=["onec"])
        P.op("pool", lambda e: e.memset(Cst[:, :, :], 0.0), writes=["Cst"])
        P.op("pool", lambda e: e.memset(Cbf[:, :, :], 0.0), writes=["Cbf"])
        P.op("pool", lambda e: e.memset(vtok[:, :, :, :], 1.0), writes=["vtok"])
        P.op("pool", lambda e: e.memset(uext[:, :, 0:16], 0.0), writes=["uext"])
        P.op("pool", lambda e: e.memset(Mu1[:, :], 0.0), writes=["Mu"])
        P.op("pool", lambda e: e.memset(negB1[:, :], 0.0), writes=["negB"])
        P.op("pool", lambda e: e.iota(invcnt[:, 0, :], [[1, 16]], base=1, channel_multiplier=0,
                                      allow_small_or_imprecise_dtypes=True), writes=["invcnt"])
        for g in range(1, 4):
            P.op("pool", lambda e, g=g: e.tensor_copy(out=invcnt[:, g, :], in_=invcnt[:, 0, :]),
                 reads=["invcnt"], writes=["invcnt"])
        for g in range(4):
            P.op("pool", lambda e, g=g: e.tensor_scalar(out=invcnt[:, g, :], in0=invcnt[:, g, :],
                                                        scalar1=float(2 ** (g + 1)), scalar2=None, op0=ALU.min),
                 reads=["invcnt"], writes=["invcnt"])
        P.op("dve", lambda e: e.reciprocal(out=invcnt[:, :, :], in_=invcnt[:, :, :]), reads=["invcnt"], writes=["invcnt"])
        P.dma("sp", lambda e: e.dma_start(out=ghbc[:, :], in_=g_head.partition_broadcast(128)), slot="c0", writes=["ghbc"])
        P.dma("sp", lambda e: e.dma_start(out=gb[:, 0:1], in_=b_ig.rearrange("(h o) -> h o", o=1)), slot="c1", writes=["gb0"])
        P.dma("sp", lambda e: e.dma_start(out=gb[:, 1:2], in_=b_fg.rearrange("(h o) -> h o", o=1)), slot="c2", writes=["gb1"])
        P.op("dve", lambda e: e.tensor_scalar(out=gb[:, 1:2], in0=gb[:, 1:2], scalar1=-1.0, scalar2=None, op0=ALU.mult),
             reads=["gb1"], writes=["gb1"])
        P.dma("sp", lambda e: e.dma_start(out=pscol[:, :], in_=pool_scale.rearrange("(g c) -> c g", c=128)),
              slot="c3", writes=["pscol"])

        NM = NS + 1
        call = ar32[0:NM, 0:1024]; sil = ar32[0:NM, 1024:2048]
        sel16_t = sb("sel16_t", [NM, 128]); siluT_t = sb("siluT_t", [128, 8, NM])
        modt = [htok[0:NM, :]] * 2; badab = [abc[0:NM, :, :].rearrange("p h c -> p (h c)")] * 2; gvb = [tmpB[0:NM, 0:512]] * 2
        sel16 = sel16_t[:, :]
        siluT = siluT_t[:, :, :]
        P.dma("sp", lambda e: e.dma_start(out=call, in_=c_all[:, :]), slot="c4", writes=["call"])
        P.op("act", lambda e: e.activation(out=sil, in_=call, func=AF.Exp, scale=-1.0), reads=["call"], writes=["sil"])
        P.op("dve", lambda e: e.tensor_scalar(out=sil, in0=sil, scalar1=1.0, scalar2=None, op0=ALU.add), reads=["sil"], writes=["sil"])
        P.op("dve", lambda e: e.reciprocal(out=sil, in_=sil), reads=["sil"], writes=["sil"])
        P.op("dve", lambda e: e.tensor_tensor(out=sil, in0=sil, in1=call, op=ALU.mult), reads=["sil", "call"], writes=["sil"])
        P.op("pool", lambda e: e.memset(sel16, 1.0), writes=["sel16"])
        P.op("pool", lambda e: e.affine_select(out=sel16, in_=sel16, pattern=[[0, 128]], compare_op=ALU.is_equal,
                                               fill=0.0, base=-NS, channel_multiplier=1), reads=["sel16"], writes=["sel16"])
        bk, bkn = nextmm()
        def _silT(e):
            for k in range(8):
                ins = e.transpose(out=bk[:, k * NM:(k + 1) * NM], in_=sil[:, k * 128:(k + 1) * 128], identity=ident[0:NM, 0:NM])
            return ins
        P.op("pe", _silT, reads=["sil", "ident"], writes=[bkn])
        P.op("dve", lambda e: e.tensor_copy(out=siluT, in_=bk[:, 0:8 * NM].rearrange("p (k m) -> p k m", k=8)),
             reads=[bkn], writes=["siluT"])
        def do_mod(t):
            kind, half = t // 2, t % 2
            pp = 0
            wv, wn = None, None
            s = wl["n"] % NSLOT
            wl["n"] += 1
            wv = slots[s][:, :].bitcast(F32).rearrange("p (k n) -> p k n", k=8)
            P.dma("sp", lambda e, s=s, t=t, wv=wv: e.dma_start(
                out=wv, in_=w_ada[:, t * 512:(t + 1) * 512].rearrange("(k p) n -> p k n", p=128)),
                slot=f"wl{s}", writes=[f"slot{s}"] + (["adaload3"] if t == 3 else []))
            P.dma("sp", lambda e, t=t, pp=pp: e.dma_start(out=badab[pp], in_=b_ada[t * 512:(t + 1) * 512].partition_broadcast(NM)),
                  slot=f"bad{pp}", writes=["abc"])
            if kind in (1, 2, 4, 5):
                gname = {1: "g_pre1", 2: "g_post1", 4: "g_pre2", 5: "g_post2"}[kind]
                P.dma("sp", lambda e, pp=pp, gname=gname, half=half: e.dma_start(
                    out=gvb[pp], in_=gvec[gname][half * 512:(half + 1) * 512].partition_broadcast(NM)),
                    slot="tmpB", writes=["tmpB"])
            bk, bkn = nextmm()
            def _modmm(e, wv=wv, bk=bk):
                for k in range(8):
                    ins = e.matmul(bk[0:NM, :], lhsT=siluT[:, k, :], rhs=wv[:, k, :], start=(k == 0), stop=(k == 7))
                return ins
            P.op("pe", _modmm, reads=["siluT", f"slot{s}"], writes=[bkn])
            P.op("dve", lambda e, pp=pp, bk=bk: e.tensor_tensor(out=modt[pp], in0=bk[0:NM, :], in1=badab[pp], op=ALU.add),
                 reads=[bkn, "abc"], writes=["htok"])
            if kind in (1, 4):
                P.op("dve", lambda e, pp=pp: e.scalar_tensor_tensor(out=modt[pp], in0=modt[pp], scalar=1.0, in1=gvb[pp],
                                                                    op0=ALU.add, op1=ALU.mult),
                     reads=["htok", "tmpB"], writes=["htok"])
            if kind in (2, 5):
                P.op("dve", lambda e, pp=pp: e.tensor_tensor(out=modt[pp], in0=modt[pp], in1=gvb[pp], op=ALU.mult),
                     reads=["htok", "tmpB"], writes=["htok"])
                gi = 0 if kind == 2 else 1
                P.op("act", lambda e, pp=pp, gi=gi, half=half: e.activation(
                    out=GAs[gi][:, half * 512:(half + 1) * 512], in_=modt[pp][0:NS, :], func=AF.Copy),
                    reads=["htok"], writes=[f"GAs{gi}"])
                bk2, bk2n = nextmm()
                P.op("pe", lambda e, pp=pp, bk2=bk2: e.matmul(bk2[:, :], lhsT=sel16, rhs=modt[pp], start=True, stop=True),
                     reads=["sel16", "htok"], writes=[bk2n])
                P.op("dve", lambda e, gi=gi, half=half, bk2=bk2: e.tensor_copy(out=GA[gi][:, half * 512:(half + 1) * 512], in_=bk2[:, :]),
                     reads=[bk2n], writes=[f"GA{gi}"])
            else:
                mi = {0: 0, 1: 1, 3: 2, 4: 3}[kind]
                bk2, bk2n = nextmm()
                def _modT(e, pp=pp, bk2=bk2):
                    for c in range(4):
                        ins = e.transpose(out=bk2[:, c * NM:(c + 1) * NM], in_=modt[pp][:, c * 128:(c + 1) * 128],
                                          identity=ident[0:NM, 0:NM])
                    return ins
                P.op("pe", _modT, reads=["htok", "ident"], writes=[bk2n])
                P.op("dve", lambda e, mi=mi, half=half, bk2=bk2: e.tensor_copy(
                    out=modT[mi][:, half * 4:(half + 1) * 4, :], in_=bk2[:, 0:4 * NM].rearrange("p (c m) -> p c m", c=4)),
                    reads=[bk2n], writes=[f"modT{mi}"])

        stage(1)
        P.dma("pool", lambda e: e.dma_start(out=WB["w_in"][:, 0:2048], in_=W["w_in"][:, 0:2048], max_dma_last_dim=2048), slot="cast_c_in0", writes=["c_in0"])
        P.dma("pool", lambda e: e.dma_start(out=WB["w_in"][:, 2048:2568], in_=W["w_in"][:, 2048:2568], max_dma_last_dim=2048), slot="cast_c_in1", writes=["c_in1"])
        for t in range(4):
            do_mod(t)
        stage(2)
        def cast(name, dst, src, after="adaload3"):
            P.dma("pool", lambda e: e.dma_start(out=dst, in_=src, max_dma_last_dim=2048), slot="cast_" + name, reads=[after], writes=[name])
        two = lambda ap: ap.rearrange("(a b) n -> a (b n)", b=2)
        P.dma("pool", lambda e: e.dma_start(out=wpool_b[:, :, :], in_=w_pool.rearrange("g c d -> c g d")),
              slot="castwp", writes=["wpool_b"])
        cast("c_out", two(WB["w_out"]), two(W["w_out"]))
        cast("c_up0", WB["w_up"][:, 0:2048], W["w_up"][:, 0:2048])
        cast("c_up1", WB["w_up"][:, 2048:4096], W["w_up"][:, 2048:4096])
        cast("c_dn0", two(WB["w_down"])[0:1024, :], two(W["w_down"])[0:1024, :])
        cast("c_dn1", two(WB["w_down"])[1024:2048, :], two(W["w_down"])[1024:2048, :])
        CASTNAME = {0: ["c_in0"], 1: ["c_in0"], 2: ["c_in1"], 3: ["c_out"], 4: ["c_up0"], 5: ["c_up0"], 6: ["c_up1"], 7: ["c_up1"],
                    8: ["c_dn0", "c_dn1"], 9: ["c_dn0", "c_dn1"], 10: ["c_dn0", "c_dn1"], 11: ["c_dn0", "c_dn1"]}


        class Ctx:
            pass

        def mkctx(kind, i=0):
            c = Ctx()
            c.kind = kind
            if kind == "p":
                c.ng, c.Pn, c.nt = 4, 128, TT
                c.xt = lambda j: xt[:, j, :]
                c.ysb = lambda j: ysb[:, j, :]
                c.xname = lambda j: f"xt{j}"
                c.yname = lambda j: f"ysb{j}"
                c.hT, c.hname = hT, "hT"
                c.fT, c.fname = fT, "fT"
                c.GA = GA
                c.ganame = ["GA0", "GA1"]
                c.src = lambda j: x_p[i * TT + j * 128: i * TT + (j + 1) * 128, :]
                c.dst = lambda j: y_p[i * TT + j * 128: i * TT + (j + 1) * 128, :]
                c.sm = sm4
                c.ssb = 32
                c.pfx = "p"
            else:
                c.ng, c.Pn, c.nt = 1, NS, NS
                c.xt = lambda j: xt_s[:, :]
                c.ysb = lambda j: mubc[:, :, :].rearrange("p h c -> p (h c)")[0:NS, 0:1024]
                c.xname = lambda j: "xt_s"
                c.yname = lambda j: "mubc"
                c.hT, c.hname = hT_s, "hT_s"
                c.fT, c.fname = fT_s, "fT_s"
                c.GA = GAs
                c.ganame = ["GAs0", "GAs1"]
                c.src = lambda j: x_s[:, :]
                c.dst = lambda j: y_s[:, :]
                c.sm = sms
                c.ssb = 64
                c.pfx = "s"
            return c

        def rstd_from_ss(c, col_ss, col_out, n, tag):
            Pn = c.Pn
            P.op("act", lambda e: e.activation(out=c.sm[0:Pn, col_out:col_out + n], in_=c.sm[0:Pn, col_ss:col_ss + n],
                                               func=AF.Ln, scale=1.0 / D, bias=epsc[0:Pn, :]),
                 reads=[c.pfx + tag + "ss"], writes=[c.pfx + tag + "rs"])
            P.op("act", lambda e: e.activation(out=c.sm[0:Pn, col_out:col_out + n], in_=c.sm[0:Pn, col_out:col_out + n],
                                               func=AF.Exp, scale=-0.5),
                 reads=[c.pfx + tag + "rs"], writes=[c.pfx + tag + "rs"])

        def prenorm(c, which):
            Pn = c.Pn
            mSH, mG = modT[2 * which], modT[2 * which + 1]
            tag = f"pn{which}"
            for j in range(c.ng):
                P.op("act", lambda e, j=j: e.activation(out=junk[0:Pn, :], in_=c.xt(j)[:, 0:512], func=AF.Square,
                                                        accum_out=c.sm[0:Pn, j:j + 1]),
                     reads=[c.xname(j)], writes=["junk", c.pfx + tag + "ss"])
                P.op("act", lambda e, j=j: e.activation(out=junk[0:Pn, :], in_=c.xt(j)[:, 512:1024], func=AF.Square,
                                                        accum_out=c.sm[0:Pn, c.ssb + j:c.ssb + 1 + j]),
                     reads=[c.xname(j)], writes=["junk", c.pfx + tag + "ss"])
            P.op("dve", lambda e: e.tensor_tensor(out=c.sm[0:Pn, 0:c.ng], in0=c.sm[0:Pn, 0:c.ng],
                                                  in1=c.sm[0:Pn, c.ssb:c.ssb + c.ng], op=ALU.add),
                 reads=[c.pfx + tag + "ss"], writes=[c.pfx + tag + "ss"])
            rstd_from_ss(c, 0, 4, c.ng, tag)
            for j in range(c.ng):
                P.op("dve", lambda e, j=j: e.tensor_scalar(out=c.ysb(j), in0=c.xt(j), scalar1=c.sm[0:Pn, 4 + j:5 + j],
                                                           scalar2=None, op0=ALU.mult),
                     reads=[c.xname(j), c.pfx + tag + "rs"], writes=[c.yname(j)])
            if c.kind == "p":
                for k in range(8):
                    bk, bkn = nextmm()
                    def _tr(e, k=k, bk=bk):
                        for j in range(4):
                            ins = e.transpose(out=bk[:, j * 128:(j + 1) * 128], in_=c.ysb(j)[:, k * 128:(k + 1) * 128],
                                              identity=ident[:, :])
                        return ins
                    P.op("pe", _tr, reads=[c.yname(j) for j in range(4)] + ["ident"], writes=[bkn])
                    if k % 2 == 0:
                        P.op("dve", lambda e, k=k, bk=bk: e.tensor_scalar(
                            out=c.hT[:, k, :], in0=bk[:, :], scalar1=mG[:, k, NS:NS + 1], scalar2=mSH[:, k, NS:NS + 1],
                            op0=ALU.mult, op1=ALU.add),
                            reads=[bkn, f"modT{2 * which}", f"modT{2 * which + 1}"], writes=[c.hname])
                    else:
                        P.op("act", lambda e, k=k, bk=bk: e.activation(
                            out=c.hT[:, k, :], in_=bk[:, :], func=AF.Identity, scale=mG[:, k, NS:NS + 1], bias=mSH[:, k, NS:NS + 1]),
                            reads=[bkn, f"modT{2 * which}", f"modT{2 * which + 1}"], writes=[c.hname])
            else:
                for hb in range(2):
                    bk, bkn = nextmm()
                    def _tr(e, hb=hb, bk=bk):
                        for kk in range(4):
                            k = hb * 4 + kk
                            ins = e.transpose(out=bk[:, kk * Pn:(kk + 1) * Pn], in_=c.ysb(0)[:, k * 128:(k + 1) * 128],
                                              identity=ident[0:Pn, 0:Pn])
                        return ins
                    P.op("pe", _tr, reads=[c.yname(0), "ident"], writes=[bkn])
                    tv = tmpA[:, 0:4 * NS].rearrange("p (k m) -> p k m", k=4)
                    P.op("dve", lambda e, hb=hb, bk=bk, tv=tv: e.tensor_tensor(
                        out=tv, in0=bk[:, 0:4 * NS].rearrange("p (k m) -> p k m", k=4),
                        in1=mG[:, hb * 4:(hb + 1) * 4, 0:NS], op=ALU.mult),
                        reads=[bkn, f"modT{2 * which + 1}"], writes=["tmpA"])
                    P.op("dve", lambda e, hb=hb, tv=tv: e.tensor_tensor(
                        out=c.hT[:, hb * 4:(hb + 1) * 4, :], in0=tv, in1=mSH[:, hb * 4:(hb + 1) * 4, 0:NS], op=ALU.add),
                        reads=["tmpA", f"modT{2 * which}"], writes=[c.hname])

        def fm_group(c, wv, wn, col0, rhs_hT, rhs_name, K=8, M=128):
            bk, bkn = nextmm()
            def _mm(e):
                for k in range(K):
                    ins = e.matmul(bk[0:M, 0:c.nt], lhsT=wv[:, k, col0:col0 + M], rhs=rhs_hT[:, k, 0:c.nt],
                                   start=(k == 0), stop=(k == K - 1))
                return ins
            P.op("pe", _mm, reads=[wn, rhs_name], writes=[bkn])
            return bk, bkn

        def tm_group(c, j, wv, wn, col0, ncol, lhs_hT, lhs_name, K=8):
            bk, bkn = nextmm()
            Pn = c.Pn
            def _mm(e):
                for k in range(K):
                    ins = e.matmul(bk[0:Pn, 0:ncol], lhsT=lhs_hT[:, k, j * Pn:(j + 1) * Pn], rhs=wv[:, k, col0:col0 + ncol],
                                   start=(k == 0), stop=(k == K - 1))
                return ins
            P.op("pe", _mm, reads=[wn, lhs_name], writes=[bkn])
            return bk, bkn

        def postnorm_all(c, gi, tag, to_ysb=False):
            Pn, ng = c.Pn, c.ng
            for j in range(ng):
                P.op("act", lambda e, j=j: e.activation(out=junk[0:Pn, :], in_=c.ysb(j)[:, 0:512], func=AF.Square,
                                                        accum_out=c.sm[0:Pn, 8 + j:9 + j]),
                     reads=[c.yname(j)], writes=["junk", c.pfx + tag + "ss"])
                P.op("act", lambda e, j=j: e.activation(out=junk[0:Pn, :], in_=c.ysb(j)[:, 512:1024], func=AF.Square,
                                                        accum_out=c.sm[0:Pn, c.ssb + j:c.ssb + 1 + j]),
                     reads=[c.yname(j)], writes=["junk", c.pfx + tag + "ssb"])
                P.op("dve", lambda e, j=j: e.tensor_tensor(out=c.ysb(j), in0=c.ysb(j), in1=c.GA[gi][0:Pn, :], op=ALU.mult),
                     reads=[c.yname(j), c.ganame[gi]], writes=[c.yname(j)])
            P.op("dve", lambda e: e.tensor_tensor(out=c.sm[0:Pn, 8:8 + ng], in0=c.sm[0:Pn, 8:8 + ng],
                                                  in1=c.sm[0:Pn, c.ssb:c.ssb + ng], op=ALU.add),
                 reads=[c.pfx + tag + "ss", c.pfx + tag + "ssb"], writes=[c.pfx + tag + "ss"])
            P.op("act", lambda e: e.activation(out=c.sm[0:Pn, 12:12 + ng], in_=c.sm[0:Pn, 8:8 + ng],
                                               func=AF.Ln, scale=1.0 / D, bias=epsc[0:Pn, :]),
                 reads=[c.pfx + tag + "ss"], writes=[c.pfx + tag + "rs"])
            P.op("act", lambda e: e.activation(out=c.sm[0:Pn, 12:12 + ng], in_=c.sm[0:Pn, 12:12 + ng],
                                               func=AF.Exp, scale=-0.5),
                 reads=[c.pfx + tag + "rs"], writes=[c.pfx + tag + "rs"])
            for j in range(ng):
                if to_ysb:
                    P.op("dve", lambda e, j=j: e.scalar_tensor_tensor(out=c.ysb(j), in0=c.ysb(j), scalar=c.sm[0:Pn, 12 + j:13 + j],
                                                                      in1=c.xt(j), op0=ALU.mult, op1=ALU.add),
                         reads=[c.yname(j), c.pfx + tag + "rs", c.xname(j)], writes=[c.yname(j)])
                else:
                    P.op("dve", lambda e, j=j: e.scalar_tensor_tensor(out=c.xt(j), in0=c.ysb(j), scalar=c.sm[0:Pn, 12 + j:13 + j],
                                                                      in1=c.xt(j), op0=ALU.mult, op1=ALU.add),
                         reads=[c.yname(j), c.pfx + tag + "rs", c.xname(j)], writes=[c.xname(j)])

        cs = mkctx("s")
        XS = [(qT[:, :, :].rearrange("p h t -> p (h t)").bitcast(F32), "qT"),
              (kT[:, :, :].rearrange("p h t -> p (h t)").bitcast(F32), "kT"),
              (ktok[:, :, :].rearrange("p j c -> p (j c)").bitcast(F32), "ktok"),
              (GO[:, :, :].rearrange("p j c -> p (j c)").bitcast(F32), "GO")]

        def early_prenorm_a(inext):
            for j in range(4):
                xs, xn = XS[j]
                P.dma("sp", lambda e, j=j, xs=xs: e.dma_start(out=xs, in_=x_p[inext * TT + j * 128: inext * TT + (j + 1) * 128, :]),
                      slot=f"ldxe{j}", writes=[xn])
            for j in range(4):
                xs, xn = XS[j]
                P.op("act", lambda e, j=j, xs=xs: e.activation(out=junk[:, :], in_=xs[:, 0:512], func=AF.Square,
                                                               accum_out=sm4[:, 56 + j:57 + j]),
                     reads=[xn], writes=["junk", "epn_ss"])
                P.op("act", lambda e, j=j, xs=xs: e.activation(out=junk[:, :], in_=xs[:, 512:1024], func=AF.Square,
                                                               accum_out=sm4[:, 60 + j:61 + j]),
                     reads=[xn], writes=["junk", "epn_ss"])
            P.op("dve", lambda e: e.tensor_tensor(out=sm4[:, 56:60], in0=sm4[:, 56:60], in1=sm4[:, 60:64], op=ALU.add),
                 reads=["epn_ss"], writes=["epn_ss"])
            P.op("act", lambda e: e.activation(out=sm4[:, 64:68], in_=sm4[:, 56:60], func=AF.Ln, scale=1.0 / D, bias=epsc[:, :]),
                 reads=["epn_ss"], writes=["epn_rs"])
            P.op("act", lambda e: e.activation(out=sm4[:, 64:68], in_=sm4[:, 64:68], func=AF.Exp, scale=-0.5),
                 reads=["epn_rs"], writes=["epn_rs"])
            for j in range(4):
                xs, xn = XS[j]
                P.op("dve", lambda e, j=j, xs=xs: e.tensor_scalar(out=xs, in0=xs, scalar1=sm4[:, 64 + j:65 + j], scalar2=None, op0=ALU.mult),
                     reads=[xn, "epn_rs"], writes=[xn])

        def early_prenorm_b():
            mSH, mG = modT[0], modT[1]
            for k in range(8):
                bk, bkn = nextmm()
                def _tr(e, k=k, bk=bk):
                    for j in range(4):
                        ins = e.transpose(out=bk[:, j * 128:(j + 1) * 128], in_=XS[j][0][:, k * 128:(k + 1) * 128], identity=ident[:, :])
                    return ins
                P.op("pe", _tr, reads=[XS[j][1] for j in range(4)] + ["ident"], writes=[bkn])
                if k % 2 == 0:
                    P.op("dve", lambda e, k=k, bk=bk: e.tensor_scalar(
                        out=hT[:, k, :], in0=bk[:, :], scalar1=mG[:, k, NS:NS + 1], scalar2=mSH[:, k, NS:NS + 1],
                        op0=ALU.mult, op1=ALU.add), reads=[bkn, "modT0", "modT1"], writes=["hT"])
                else:
                    P.op("act", lambda e, k=k, bk=bk: e.activation(
                        out=hT[:, k, :], in_=bk[:, :], func=AF.Identity, scale=mG[:, k, NS:NS + 1], bias=mSH[:, k, NS:NS + 1]),
                        reads=[bkn, "modT0", "modT1"], writes=["hT"])

        SAMPLE_ARENA = ["vb", "zs", "hist0", "hist1", "hist2", "hist3", "qTs", "wkTs", "qCs", "abcs", "n0s", "call", "sil"]
        zs = ar32[0:NS, 0:2568]
        hist = ar32[0:NS, 2688:2688 + 26 * 128].rearrange("p (r c) -> p r c", c=128)
        C0s = [ysb[:, i, 0:512].rearrange("p (h e) -> p h e", h=4) for i in range(4)]
        qTs = ar32[:, 7168:7168 + 64].rearrange("p (h b) -> p h b", h=4)
        wkTs = ar32[:, 7232:7232 + 64].rearrange("p (h b) -> p h b", h=4)
        qCs = ar32[:, 7296:7296 + 64]
        abcs = ar32[:, 7360:7360 + 64].rearrange("p (b h) -> p b h", h=4)
        n0s = ar32[0:NS, 7680:7680 + 512]

        def pool_group(g, bufs, tgt, tname, first_tile):
            win = 2 ** (g + 1)
            cur = None
            sh = 1
            bi = 0
            while sh < win:
                dstt, dn = bufs[bi]
                if cur is None:
                    P.op("pool", lambda e, sh=sh, dstt=dstt: e.tensor_tensor(
                        out=dstt[:, sh:528], in0=uext[:, g, sh:528], in1=uext[:, g, 0:528 - sh], op=ALU.add),
                        reads=["uext"], writes=[dn])
                else:
                    ct, cn = cur
                    P.op("pool", lambda e, sh=sh, dstt=dstt, ct=ct: e.tensor_tensor(
                        out=dstt[:, 2 * sh - 1:528], in0=ct[:, 2 * sh - 1:528], in1=ct[:, sh - 1:528 - sh], op=ALU.add),
                        reads=[cn], writes=[dn])
                cur = bufs[bi]
                bi ^= 1
                sh *= 2
            ct, cn = cur
            P.op("dve", lambda e, ct=ct: e.scalar_tensor_tensor(
                out=tgt(g), in0=ct[:, 16:528], scalar=1.0 / win, in1=uext[:, g, 16:528], op0=ALU.mult, op1=ALU.subtract),
                reads=[cn, "uext"], writes=[tname])
            if first_tile:
                ot, on = bufs[bi]
                P.op("dve", lambda e, ct=ct, ot=ot: e.tensor_tensor(out=ot[:, 0:16], in0=ct[:, 16:32], in1=invcnt[:, g, :], op=ALU.mult),
                     reads=[cn, "invcnt"], writes=[on])
                P.op("dve", lambda e, ot=ot: e.tensor_tensor(out=tgt(g)[:, 0:16], in0=ot[:, 0:16], in1=uext[:, g, 16:32], op=ALU.subtract),
                     reads=[on, "uext"], writes=[tname])
            bk, bkn = nextmm()
            P.op("pe", lambda e, bk=bk: e.matmul(bk[:, :], lhsT=wpool_b[:, g, :], rhs=tgt(g), start=True, stop=True),
                 reads=["wpool_b", tname], writes=[bkn])
            P.op("act", lambda e, bk=bk: e.activation(out=hT[:, 4 + g, :], in_=bk[:, :], func=AF.Identity, scale=pscol[:, g:g + 1]),
                 reads=[bkn, "pscol"], writes=["hT"])

        pref = None
        pending_G = []
        pending_post = None
        for i in range(NTILE):
            par = i % 2
            cp = mkctx("p", i)
            ctxs = [cp] + ([cs] if i == 0 else [])
            def load_x(c):
                for j in range(c.ng):
                    P.dma("sp", lambda e, c=c, j=j: e.dma_start(out=c.xt(j), in_=c.src(j)),
                          slot=f"ldx{c.pfx}{j}", writes=[c.xname(j)])
            for c in ctxs:
                if c.kind == "p" and i >= 1:
                    continue
                load_x(c)
                prenorm(c, 0)
            stage(3 + 10 * i)
            wv, wn = pref[0] if pref else load_block(0)
            for c in ctxs:
                if c.kind == "p":
                    for m in range(8):
                        bk, bkn = fm_group(c, wv, wn, m * 128, c.hT, c.hname)
                        if m < 4:
                            P.op("act", lambda e, m=m, bk=bk: e.activation(out=qT[:, m, :], in_=bk[:, :], func=AF.Copy),
                                 reads=[bkn], writes=["qT"])
                        else:
                            P.op("act", lambda e, m=m, bk=bk: e.activation(out=kT[:, m - 4, :], in_=bk[:, :], func=AF.Copy,
                                                                           scale=128.0 ** -0.5),
                                 reads=[bkn], writes=["kT"])
                    for j in range(4):
                        bk, bkn = tm_group(c, j, wv, wn, 512, 512, c.hT, c.hname)
                        P.op("dve", lambda e, j=j, bk=bk: e.tensor_scalar(out=ktok[:, j, :], in0=bk[:, :], scalar1=128.0 ** -0.5,
                                                                          scalar2=None, op0=ALU.mult),
                             reads=[bkn], writes=["ktok"])
                else:
                    for half in range(2):
                        bk, bkn = tm_group(c, 0, wv, wn, half * 512, 512, c.hT, c.hname)
                        P.op("act", lambda e, half=half, bk=bk: e.activation(out=zs[:, half * 512:(half + 1) * 512],
                                                                             in_=bk[0:NS, :], func=AF.Copy),
                             reads=[bkn], writes=["zs", "call", "sil"])
            wv, wn = pref[1] if pref else load_block(1)
            for c in ctxs:
                if c.kind == "p":
                    for j in range(4):
                        bk, bkn = tm_group(c, j, wv, wn, 0, 512, c.hT, c.hname)
                        P.op("act", lambda e, j=j, bk=bk: e.activation(
                            out=vtok[:, j, :, 0:128], in_=bk[:, :].rearrange("p (h c) -> p h c", h=4), func=AF.Copy),
                            reads=[bkn], writes=["vtok"])
                        bk, bkn = tm_group(c, j, wv, wn, 512, 512, c.hT, c.hname)
                        gt, gtn = (tmpA, "tmpA") if j % 2 == 0 else (tmpB, "tmpB")
                        P.op("act", lambda e, bk=bk, gt=gt: e.activation(out=gt[:, 0:512], in_=bk[:, :], func=AF.Exp, scale=-1.0),
                             reads=[bkn], writes=[gtn])
                        P.op("dve", lambda e, gt=gt: e.tensor_scalar(out=gt[:, 0:512], in0=gt[:, 0:512], scalar1=1.0, scalar2=None,
                                                                     op0=ALU.add), reads=[gtn], writes=[gtn])
                        P.op("dve", lambda e, gt=gt: e.reciprocal(out=gt[:, 0:512], in_=gt[:, 0:512]), reads=[gtn], writes=[gtn])
                        P.op("pool", lambda e, j=j, gt=gt: e.tensor_tensor(
                            out=GO[:, j, :].rearrange("p (h c) -> p h c", h=4), in0=gt[:, 0:512].rearrange("p (h c) -> p h c", h=4),
                            in1=ghbc[:, :].unsqueeze(1).to_broadcast([128, 4, 128]), op=ALU.mult),
                            reads=[gtn, "ghbc"], writes=["GO"])
                else:
                    for half in range(2):
                        bk, bkn = tm_group(c, 0, wv, wn, half * 512, 512, c.hT, c.hname)
                        P.op("act", lambda e, half=half, bk=bk: e.activation(out=zs[:, 1024 + half * 512:1024 + (half + 1) * 512],
                                                                             in_=bk[0:NS, :], func=AF.Copy),
                             reads=[bkn], writes=["zs", "call", "sil"])
            wv, wn = pref[2] if pref else load_block(2)
            igb = fgb = None
            for c in ctxs:
                if c.kind == "p":
                    igb, ign = fm_group(c, wv, wn, 0, c.hT, c.hname, M=4)
                    fgb, fgn = fm_group(c, wv, wn, 4, c.hT, c.hname, M=4)
                    P.op("act", lambda e, fgb=fgb: e.activation(out=Ug[:, :], in_=fgb[0:4, :], func=AF.Exp, scale=-1.0, bias=gb[:, 1:2]),
                         reads=[fgn, "gb1"], writes=["Ug"])
                    P.op("act", lambda e: e.activation(out=Ug[:, :], in_=Ug[:, :], func=AF.Ln, bias=onec[0:4, :]),
                         reads=["Ug", "onec"], writes=["Ug"])
                    P.op("dve", lambda e: e.tensor_tensor_scan(out=negB1[:, 1:513], data0=ones4[:, :], data1=Ug[:, :],
                                                               initial=negB1[:, 0:1], op0=ALU.mult, op1=ALU.add),
                         reads=["ones4", "Ug", "negB"], writes=["negB"])
                    P.op("dve", lambda e: e.tensor_copy(out=negB1[:, 0:1], in_=negB1[:, 512:513]), reads=["negB"], writes=["negB"])
                    P.op("dve", lambda e, igb=igb: e.scalar_tensor_tensor(out=Ug[:, :], in0=igb[0:4, :], scalar=gb[:, 0:1],
                                                                          in1=negB1[:, 1:513], op0=ALU.add, op1=ALU.add),
                         reads=[ign, "gb0", "negB"], writes=["Ug"])
                    P.op("dve", lambda e: e.tensor_tensor_scan(out=Mu1[:, 1:513], data0=ones4[:, :], data1=Ug[:, :],
                                                               initial=Mu1[:, 0:1], op0=ALU.mult, op1=ALU.max),
                         reads=["ones4", "Ug", "Mu"], writes=["Mu"])
                    P.op("dve", lambda e: e.tensor_tensor(out=negB1[:, 1:513], in0=negB1[:, 1:513], in1=Mu1[:, 1:513],
                                                          op=ALU.subtract),
                         reads=["negB", "Mu"], writes=["negB"])
                    P.dma("sp", lambda e, par=par: e.dma_start(out=muscr[par], in_=Mu1[:, :]), slot="mus",
                          reads=["Mu"], writes=[f"muscr{par}"])
                    P.dma("sp", lambda e, par=par: e.dma_start(out=mubc[:, :, :].rearrange("p h c -> p (h c)"),
                                                               in_=muscr[par].rearrange("h c -> (h c)").partition_broadcast(128)),
                          slot="mub", reads=[f"muscr{par}"], writes=["mubc"])
                    P.op("dve", lambda e: e.tensor_copy(out=Mu1[:, 0:1], in_=Mu1[:, 512:513]), reads=["Mu"], writes=["Mu"])
                    bk, bkn = nextmm()
                    def _gtr(e, bk=bk):
                        for j in range(4):
                            e.transpose(out=bk[:, j * 8:j * 8 + 4], in_=Ug[:, j * 128:(j + 1) * 128], identity=ident[0:4, 0:4])
                            ins = e.transpose(out=bk[:, j * 8 + 4:j * 8 + 8], in_=negB1[:, 1 + j * 128:1 + (j + 1) * 128], identity=ident[0:4, 0:4])
                        return ins
                    P.op("pe", _gtr, reads=["Ug", "negB", "ident"], writes=[bkn])
                    P.op("dve", lambda e, bk=bk: e.tensor_copy(out=ucol[:, :, :, :].rearrange("p j w h -> p (j w h)"), in_=bk[:, 0:32]),
                         reads=[bkn], writes=["ucol"])
                    P.op("act", lambda e: e.activation(out=ucol[:, :, 1, :], in_=ucol[:, :, 1, :], func=AF.Exp),
                         reads=["ucol"], writes=["ucol"])
                    for g in range(4):
                        bk, bkn = fm_group(c, wv, wn, 8 + g * 128, c.hT, c.hname)
                        P.op("act", lambda e, g=g, bk=bk: e.activation(out=uext[:, g, 16:16 + TT], in_=bk[:, :], func=AF.Copy),
                             reads=[bkn], writes=["uext"])
                else:
                    bk, bkn = tm_group(c, 0, wv, wn, 8, 512, c.hT, c.hname)
                    P.op("act", lambda e, bk=bk: e.activation(out=zs[:, 2048:2560], in_=bk[0:NS, :], func=AF.Copy),
                         reads=[bkn], writes=["zs", "call", "sil"])
                    bk, bkn = tm_group(c, 0, wv, wn, 0, 8, c.hT, c.hname)
                    P.op("act", lambda e, bk=bk: e.activation(out=zs[:, 2560:2568], in_=bk[0:NS, 0:8], func=AF.Copy),
                         reads=[bkn], writes=["zs", "call", "sil"])

            stage(4 + 10 * i)
            mmmode["dense"] = False
            ux3b = uext[:, 3, 16:528].bitcast(BF16)
            STb2 = [(STb[:, :, :], "STb"), (ux3b[:, 0:512].rearrange("p (h c) -> p h c", h=4), "ux3a")]
            qaT2 = [(qaT[:, :, :], "qaT"), (ux3b[:, 512:1024].rearrange("p (h c) -> p h c", h=4), "ux3b")]
            kw2 = [(kw[:, :, :], "kw"), (junk[:, :].rearrange("p (h c) -> p h c", h=4), "junk")]
            nsb2 = [(uext[:, 0, 16:528].rearrange("p (h c) -> p h c", h=4), "ux0"),
                    (uext[:, 1, 16:528].rearrange("p (h c) -> p h c", h=4), "ux1")]
            htok2 = [(htok[:, :], "htok"), (uext[:, 2, 16:528], "ux2")]
            tA = tmpA[:, 0:512].rearrange("p (h c) -> p h c", h=4)
            tB = tmpB[:, 0:512].rearrange("p (h c) -> p h c", h=4)

            def pre(j):
                c0 = j * 128
                STj, STn = STb2[j % 2]
                qaj, qan = qaT2[j % 2]
                kwj, kwn = kw2[j % 2]
                def _qk(e):
                    for h in range(4):
                        ins = e.matmul(stv[:, h, :], lhsT=kT[:, h, j * 128:(j + 1) * 128], rhs=qT[:, h, j * 128:(j + 1) * 128],
                                       start=True, stop=True)
                    return ins
                P.op("pe", _qk, reads=["kT", "qT"], writes=["stp"])
                for h in range(4):
                    P.op("dve", lambda e, h=h: e.scalar_tensor_tensor(
                        out=tA[:, h, :], in0=mubc[:, h, 1 + c0:1 + c0 + 128], scalar=ucol[:, j, 0, h:h + 1], in1=mbig[:, :],
                        op0=ALU.subtract, op1=ALU.max), reads=["mubc", "ucol", "mbig"], writes=["tmpA"])
                P.op("act", lambda e: e.activation(out=tA, in_=tA, func=AF.Exp, scale=-1.0), reads=["tmpA"], writes=["tmpA"])
                P.op("dve", lambda e: e.tensor_tensor(out=STj, in0=stv[:, :, :], in1=tA, op=ALU.mult),
                     reads=["stp", "tmpA"], writes=[STn])
                for h in range(4):
                    P.op("act", lambda e, h=h: e.activation(out=abc[:, h, :], in_=mubc[:, h, 1 + c0:1 + c0 + 128], func=AF.Exp,
                                                            scale=-1.0, bias=mubc[:, h, c0:c0 + 1]),
                         reads=["mubc"], writes=["abc"])
                P.op("pool", lambda e: e.tensor_tensor(out=qaj, in0=qT[:, :, j * 128:(j + 1) * 128], in1=abc[:, :, :], op=ALU.mult),
                     reads=["qT", "abc"], writes=[qan])
                P.op("dve", lambda e: e.tensor_copy(out=sm4[:, 40 + 4 * j:44 + 4 * j], in_=abc[:, :, 127]),
                     reads=["abc"], writes=[f"aLs{j}"])
                P.op("dve", lambda e: e.tensor_tensor(out=sm4[:, 28:32], in0=ucol[:, j, 0, :], in1=mubc[:, :, c0 + 128],
                                                      op=ALU.subtract), reads=["ucol", "mubc"], writes=["wL"])
                P.op("act", lambda e: e.activation(out=sm4[:, 28:32], in_=sm4[:, 28:32], func=AF.Exp), reads=["wL"], writes=["wL"])
                P.op("pool", lambda e: e.tensor_tensor(out=kwj, in0=ktok[:, j, :].rearrange("p (h c) -> p h c", h=4),
                                                       in1=sm4[:, 28:32].unsqueeze(2).to_broadcast([128, 4, 128]), op=ALU.mult),
                     reads=["ktok", "wL"], writes=[kwn])

            def num(j):
                STj, STn = STb2[j % 2]
                qaj, qan = qaT2[j % 2]
                nsj, nsn = nsb2[j % 2]
                def _num(e):
                    for h in range(4):
                        e.matmul(numv[:, h, 0:129], lhsT=qaj[:, h, :], rhs=Cbf[:, h, :], start=True, stop=False)
                        ins = e.matmul(numv[:, h, 0:129], lhsT=STj[:, h, :], rhs=vtok[:, j, h, :], start=False, stop=True)
                    return ins
                P.op("pe", _num, reads=[qan, "Cbf", STn, "vtok"], writes=["nump"])
                P.op("act", lambda e: e.activation(out=nsj, in_=numv[:, :, 0:128], func=AF.Copy), reads=["nump"], writes=[nsn])
                P.op("dve", lambda e: e.tensor_copy(out=sm4[:, 72:76], in_=numv[:, :, 128]), reads=["nump"], writes=["hdenraw"])

            def upd(j):
                kwj, kwn = kw2[j % 2]
                def _cu(e):
                    for h in range(4):
                        ins = e.matmul(cupv[:, h, 0:129], lhsT=kwj[:, h, :], rhs=vtok[:, j, h, :], start=True, stop=True)
                    return ins
                P.op("pe", _cu, reads=[kwn, "vtok"], writes=["cup"])
                for h in range(4):
                    P.op("dve", lambda e, h=h: e.scalar_tensor_tensor(out=Cst[:, h, :], in0=Cst[:, h, :],
                                                                      scalar=sm4[:, 40 + 4 * j + h:41 + 4 * j + h],
                                                                      in1=cupv[:, h, 0:129], op0=ALU.mult, op1=ALU.add),
                         reads=["Cst", f"aLs{j}", "cup"], writes=["Cst"])
                P.op("act", lambda e: e.activation(out=Cbf[:, :, :], in_=Cst[:, :, :], func=AF.Copy), reads=["Cst"], writes=["Cbf"])

            def post_a(j):
                nsj, nsn = nsb2[j % 2]
                htj, htn = htok2[j % 2]
                P.op("act", lambda e: e.activation(out=tB, in_=nsj[:, :, 0:128], func=AF.Square), reads=[nsn], writes=["tmpB"])
                P.op("dve", lambda e: e.tensor_reduce(out=sm4[:, 16:20], in_=tB, axis=AX.X, op=ALU.add), reads=["tmpB"], writes=["hss"])
                P.op("dve", lambda e: e.tensor_scalar(out=sm4[:, 36:40], in0=sm4[:, 72:76], scalar1=-1.0, scalar2=None, op0=ALU.mult),
                     reads=["hdenraw"], writes=["hden2"])
                P.op("dve", lambda e: e.tensor_tensor(out=sm4[:, 20:24], in0=sm4[:, 36:40], in1=sm4[:, 72:76], op=ALU.max),
                     reads=["hdenraw", "hden2"], writes=["hden"])
                P.op("dve", lambda e: e.tensor_tensor(out=sm4[:, 20:24], in0=sm4[:, 20:24], in1=ucol[:, j, 1, :], op=ALU.max),
                     reads=["hden", "ucol"], writes=["hden"])
                P.op("dve", lambda e: e.reciprocal(out=sm4[:, 20:24], in_=sm4[:, 20:24]), reads=["hden"], writes=["hden"])
                P.op("dve", lambda e: e.tensor_tensor(out=sm4[:, 24:28], in0=sm4[:, 20:24], in1=sm4[:, 20:24], op=ALU.mult),
                     reads=["hden"], writes=["hrs"])
                P.op("dve", lambda e: e.tensor_tensor(out=sm4[:, 24:28], in0=sm4[:, 24:28], in1=sm4[:, 16:20], op=ALU.mult),
                     reads=["hrs", "hss"], writes=["hrs"])
                P.op("act", lambda e: e.activation(out=sm4[:, 24:28], in_=sm4[:, 24:28], func=AF.Ln, scale=1.0 / 128, bias=epsc[:, :]),
                     reads=["hrs"], writes=["hrs"])
                P.op("act", lambda e: e.activation(out=sm4[:, 24:28], in_=sm4[:, 24:28], func=AF.Exp, scale=-0.5),
                     reads=["hrs"], writes=["hrs"])
                P.op("dve", lambda e: e.tensor_tensor(out=sm4[:, 24:28], in0=sm4[:, 24:28], in1=sm4[:, 20:24], op=ALU.mult),
                     reads=["hrs", "hden"], writes=["hrs"])
                for h in range(4):
                    eng = "dve" if h % 2 == 0 else "pool"
                    if eng == "dve":
                        P.op("dve", lambda e, h=h: e.scalar_tensor_tensor(
                            out=htj[:, h * 128:(h + 1) * 128], in0=nsj[:, h, 0:128], scalar=sm4[:, 24 + h:25 + h],
                            in1=GO[:, j, h * 128:(h + 1) * 128], op0=ALU.mult, op1=ALU.mult),
                            reads=[nsn, "hrs", "GO"], writes=[htn])
                    else:
                        P.op("dve", lambda e, h=h: e.scalar_tensor_tensor(
                            out=htj[:, h * 128:(h + 1) * 128], in0=nsj[:, h, 0:128], scalar=sm4[:, 24 + h:25 + h],
                            in1=GO[:, j, h * 128:(h + 1) * 128], op0=ALU.mult, op1=ALU.mult),
                            reads=[nsn, "hrs", "GO"], writes=[htn])

            def post_b(j):
                htj, htn = htok2[j % 2]
                bk, bkn = nextmm()
                def _htr(e):
                    for h in range(4):
                        ins = e.transpose(out=bk[:, h * 128:(h + 1) * 128], in_=htj[:, h * 128:(h + 1) * 128], identity=ident[:, :])
                    return ins
                P.op("pe", _htr, reads=[htn, "ident"], writes=[bkn])
                P.op("act", lambda e: e.activation(out=hT[:, 0:4, j * 128:(j + 1) * 128],
                                                   in_=bk[:, :].rearrange("p (h c) -> p h c", h=4), func=AF.Copy),
                     reads=[bkn], writes=["hT"])

            gq = list(pending_G)
            pending_G = []
            for g in range(4):
                pool_group(g, [(tmpA, "tmpA"), (tmpB, "tmpB")], lambda g: hT[:, 4 + g, :], "hT", i == 0)
                if gq:
                    gq.pop(0)()
            if i == NTILE - 1:
                bk, bkn = nextmm()
                def _ptr(e, bk=bk):
                    for g in range(4):
                        ins = e.transpose(out=bk[0:15, g * 128:(g + 1) * 128], in_=uext[:, g, 513:528], identity=ident[:, :])
                    return ins
                P.op("pe", _ptr, reads=["uext", "ident"], writes=[bkn])
                P.op("dve", lambda e, bk=bk: e.tensor_copy(out=tmpA[0:15, 0:512], in_=bk[0:15, :]), reads=[bkn], writes=["tmpA"])
                P.dma("sp", lambda e: e.dma_start(out=pool_p[:, :], in_=tmpA[0:15, 0:512]), slot="o_pool", reads=["tmpA"], is_output=True)
            else:
                P.op("pool", lambda e: e.tensor_copy(out=uext[:, :, 0:16], in_=uext[:, :, 512:528]), reads=["uext"], writes=["uext"])
            UXN = ["uext", "ux0", "ux1", "ux2", "ux3a", "ux3b"]
            P.op("dve", lambda e: e.memset(sm4[:, 79:80], 0.0), reads=[], writes=UXN)
            steps = [lambda: pre(0), lambda: pre(1),
                     lambda: (num(0), upd(0), post_a(0)), lambda: pre(2),
                     lambda: (num(1), upd(1), post_a(1), post_b(0)), lambda: pre(3),
                     lambda: (num(2), upd(2), post_a(2), post_b(1)),
                     lambda: (num(3), upd(3), post_a(3), post_b(2)), lambda: post_b(3)]
            for k, st_ in enumerate(steps):
                st_()
                for _ in range(1 if k in (0, 8) else 2):
                    if gq:
                        gq.pop(0)()
            while gq:
                gq.pop(0)()
            P.op("dve", lambda e: e.memset(sm4[:, 79:80], 0.0), reads=[], writes=UXN)
            pref3 = load_block(3) if pending_post is not None else None
            if pending_post is not None:
                pending_post()
                pending_post = None
            if i >= 1:
                load_x(cp)

            stage(5 + 10 * i)
            if i == NTILE - 1:
                P.dma("sp", lambda e: e.dma_start(out=C_p.rearrange("h d e -> d h e"), in_=Cst[:, :, 0:128]), slot="o_C",
                      reads=["Cst"], is_output=True)
                P.dma("sp", lambda e: e.dma_start(out=n_p.rearrange("h d -> d h"), in_=Cst[:, :, 128]), slot="o_n",
                      reads=["Cst"], is_output=True)
                P.op("dve", lambda e: e.tensor_scalar(out=mfin[:, :], in0=negB1[:, 512:513], scalar1=-1.0, scalar2=None, op0=ALU.mult),
                     reads=["negB"], writes=["mfin"])
                P.dma("sp", lambda e: e.dma_start(out=m_p[:, :], in_=mfin[:, :]), slot="o_m", reads=["mfin"], is_output=True)

            stage(6 + 10 * i)
            if i == 0:
                sample_mixer(P, nc, locals())
            stage(7 + 10 * i)
            mmmode["dense"] = True

            if i == 0:
                do_mod(4)
                do_mod(5)
            wv, wn = pref3 if pref3 is not None else load_block(3)
            for c in ctxs:
                for j in range(c.ng):
                    for half in range(2):
                        bk, bkn = tm_group(c, j, wv, wn, half * 512, 512, c.hT, c.hname)
                        extra3 = []
                        P.op("act", lambda e, c=c, j=j, half=half, bk=bk: e.activation(
                            out=c.ysb(j)[:, half * 512:(half + 1) * 512], in_=bk[0:c.Pn, :], func=AF.Copy),
                            reads=[bkn], writes=[c.yname(j)] + extra3)
                postnorm_all(c, 0, "po1")
            stage(8 + 10 * i)
            if i == 0:
                for t in (6, 7, 8, 9):
                    do_mod(t)
            for c in ctxs:
                prenorm(c, 1)
            stage(9 + 10 * i)
            if i + 1 < NTILE:
                early_prenorm_a(i + 1)
            for fb in range(4):
                wv, wn = load_block(4 + fb)
                for c in ctxs:
                    for m in range(8):
                        bk, bkn = fm_group(c, wv, wn, m * 128, c.hT, c.hname)
                        rt = tmpA if (m % 2 == 0) else tmpB
                        rn = "tmpA" if (m % 2 == 0) else "tmpB"
                        P.op("act", lambda e, c=c, bk=bk, rt=rt: e.activation(out=rt[:, 0:c.nt], in_=bk[:, 0:c.nt], func=AF.Relu),
                             reads=[bkn], writes=[rn])
                        eng = "dve" if (m % 2 == 0) else "pool"
                        extra = SAMPLE_ARENA if (i == 0 and fb == 0 and m == 0 and c.kind == "p") else []
                        P.op(eng, lambda e, c=c, fb=fb, m=m, rt=rt: e.tensor_tensor(
                            out=c.fT[:, fb * 8 + m, 0:c.nt], in0=rt[:, 0:c.nt], in1=rt[:, 0:c.nt], op=ALU.mult),
                            reads=[rn], writes=[c.fname] + extra)
            stage(10 + 10 * i)
            if i == 0:
                do_mod(10)
                do_mod(11)
            if i + 1 < NTILE:
                early_prenorm_b()
            ctxs_i = list(ctxs)

            def make_G(ctxs_i):
                state = {}

                def emit_one(cb, c, j, first):
                    if first:
                        state["w"] = load_block(8 + cb)
                    wv, wn = state["w"]
                    bk, bkn = tm_group(c, j, wv, wn, 0, 256, c.fT, c.fname, K=32)
                    P.op("act", lambda e: e.activation(out=c.ysb(j)[:, cb * 256:(cb + 1) * 256], in_=bk[0:c.Pn, 0:256], func=AF.Copy),
                         reads=[bkn], writes=[c.yname(j)])
                out = []
                for cb in range(4):
                    for ci, c in enumerate(ctxs_i):
                        for j in range(c.ng):
                            out.append(lambda cb=cb, c=c, j=j, first=(ci == 0 and j == 0): emit_one(cb, c, j, first))
                return out

            def make_post(ctxs_i):
                def post2():
                    for c in ctxs_i:
                        postnorm_all(c, 1, "po2", to_ysb=True)
                        for j in range(c.ng):
                            P.dma("sp", lambda e, c=c, j=j: e.dma_start(out=c.dst(j), in_=c.ysb(j)), slot=f"sty{c.pfx}{j}",
                                  reads=[c.yname(j)], is_output=True)
                return post2
            pending_G = make_G(ctxs_i)
            pending_post = make_post(ctxs_i)
            pref = None
            if i == 0:
                for g_ in pending_G:
                    g_()
                pending_G = []
                pending_post()
                pending_post = None
        for g_ in pending_G:
            g_()
        pending_post()
        P.emit()
    return nc


def sample_mixer(P, nc, L):
    g = lambda n: L[n]
    zs, hist, C0s, qTs, wkTs, qCs, abcs, n0s = g("zs"), g("hist"), g("C0s"), g("qTs"), g("wkTs"), g("qCs"), g("abcs"), g("n0s")
    sms, ident, ghbc, epsc, tmpA, tmpB, junk = g("sms"), g("ident"), g("ghbc"), g("epsc"), g("tmpA"), g("tmpB"), g("junk")
    onec, stp = g("onec"), g("stp")
    identb, arena = g("identb"), g("arena")
    vb = arena[0:NS, 14848:15360]
    P.op("act", lambda e: e.activation(out=vb, in_=zs[:, 1024:1536], func=AF.Copy), reads=["zs"], writes=["vb"])
    hT_s, pooledT_s, wpool_b, pscol = g("hT_s"), g("pooledT_s"), g("wpool_b"), g("pscol")
    sC, sn, sm, spool = g("sC"), g("sn"), g("sm"), g("spool")
    C_s, n_s, m_s, pool_s = g("C_s"), g("n_s"), g("m_s"), g("pool_s")
    b_ig, b_fg = g("b_ig"), g("b_fg")
    nextmm = g("nextmm")
    ar32 = g("ar32")
    v4 = lambda ap: ap.rearrange("p (h c) -> p h c", h=4)
    q, k, v, o = zs[:, 0:512], zs[:, 512:1024], zs[:, 1024:1536], zs[:, 1536:2048]
    u = zs[:, 2048:2560]
    gates = zs[:, 2560:2568]
    S = lambda a, b: sms[:, a:b]
    P.dma("sp", lambda e: e.dma_start(out=S(52, 56), in_=sm[:, :]), slot="s0", writes=["s_m0"])
    P.dma("sp", lambda e: e.dma_start(out=S(56, 60), in_=b_ig.partition_broadcast(NS)), slot="s1", writes=["s_big"])
    P.dma("sp", lambda e: e.dma_start(out=S(60, 64), in_=b_fg.partition_broadcast(NS)), slot="s2", writes=["s_bfg"])
    P.dma("sp", lambda e: e.dma_start(out=n0s, in_=sn[:, :]), slot="s3", writes=["n0s"])
    for gi in range(4):
        win = 2 ** (gi + 1)
        r0 = [0, 1, 4, 11][gi]
        P.dma("sp", lambda e, gi=gi, win=win, r0=r0: e.dma_start(out=hist[:, r0:r0 + win - 1, :],
                                                                 in_=spool[:, 16 - win:15, gi * 128:(gi + 1) * 128]),
              slot=f"s4{gi}", writes=[f"hist{gi}"])
    P.dma("sp", lambda e: e.dma_start(out=pool_s[:, 0:14, :], in_=spool[:, 1:15, :]), slot="o_ps0", is_output=True)
    P.dma("sp", lambda e: e.dma_start(out=pool_s[:, 14, :], in_=u), slot="o_ps1", reads=["zs"], is_output=True)
    P.op("dve", lambda e: e.tensor_tensor(out=S(16, 20), in0=gates[:, 0:4], in1=S(56, 60), op=ALU.add), reads=["zs", "s_big"], writes=["s_ig"])
    P.op("dve", lambda e: e.tensor_tensor(out=S(20, 24), in0=gates[:, 4:8], in1=S(60, 64), op=ALU.add), reads=["zs", "s_bfg"], writes=["s_fg"])
    P.op("act", lambda e: e.activation(out=S(20, 24), in_=S(20, 24), func=AF.Exp, scale=-1.0), reads=["s_fg"], writes=["s_fg"])
    P.op("act", lambda e: e.activation(out=S(20, 24), in_=S(20, 24), func=AF.Ln, bias=onec[0:NS, :]), reads=["s_fg", "onec"], writes=["s_fg"])
    P.op("dve", lambda e: e.tensor_tensor(out=S(20, 24), in0=S(52, 56), in1=S(20, 24), op=ALU.subtract), reads=["s_fg", "s_m0"], writes=["s_fg"])
    P.op("dve", lambda e: e.tensor_tensor(out=S(24, 28), in0=S(20, 24), in1=S(16, 20), op=ALU.max), reads=["s_fg", "s_ig"], writes=["s_m"])
    P.dma("sp", lambda e: e.dma_start(out=m_s[:, :], in_=S(24, 28)), slot="o_ms", reads=["s_m"], is_output=True)
    P.op("dve", lambda e: e.tensor_tensor(out=S(28, 32), in0=S(16, 20), in1=S(24, 28), op=ALU.subtract), reads=["s_ig", "s_m"], writes=["s_w"])
    P.op("dve", lambda e: e.tensor_tensor(out=S(32, 36), in0=S(20, 24), in1=S(24, 28), op=ALU.subtract), reads=["s_fg", "s_m"], writes=["s_a"])
    P.op("act", lambda e: e.activation(out=S(28, 36), in_=S(28, 36), func=AF.Exp), reads=["s_w", "s_a"], writes=["s_w", "s_a"])
    P.op("act", lambda e: e.activation(out=S(36, 40), in_=S(24, 28), func=AF.Exp, scale=-1.0), reads=["s_m"], writes=["s_e"])
    t512 = tmpA[0:NS, 0:512]
    P.op("dve", lambda e: e.tensor_tensor(out=t512, in0=q, in1=k, op=ALU.mult), reads=["zs"], writes=["tmpA"])
    P.op("dve", lambda e: e.tensor_reduce(out=S(40, 44), in_=v4(t512), axis=AX.X, op=ALU.add), reads=["tmpA"], writes=["s_qk"])
    P.op("dve", lambda e: e.scalar_tensor_tensor(out=S(40, 44), in0=S(40, 44), scalar=128.0 ** -0.5, in1=S(28, 32), op0=ALU.mult, op1=ALU.mult),
         reads=["s_qk", "s_w"], writes=["s_qk"])
    P.op("dve", lambda e: e.tensor_tensor(out=t512, in0=q, in1=n0s, op=ALU.mult), reads=["zs", "n0s", "tmpA"], writes=["tmpA"])
    P.op("dve", lambda e: e.tensor_reduce(out=S(44, 48), in_=v4(t512), axis=AX.X, op=ALU.add), reads=["tmpA"], writes=["s_den"])
    P.op("dve", lambda e: e.tensor_tensor(out=S(44, 48), in0=S(44, 48), in1=S(32, 36), op=ALU.mult), reads=["s_den", "s_a"], writes=["s_den"])
    P.op("dve", lambda e: e.tensor_tensor(out=S(44, 48), in0=S(44, 48), in1=S(40, 44), op=ALU.add), reads=["s_den", "s_qk"], writes=["s_den"])
    P.op("dve", lambda e: e.tensor_scalar(out=S(68, 72), in0=S(44, 48), scalar1=-1.0, scalar2=None, op0=ALU.mult), reads=["s_den"], writes=["s_den2"])
    P.op("dve", lambda e: e.tensor_tensor(out=S(44, 48), in0=S(44, 48), in1=S(68, 72), op=ALU.max), reads=["s_den", "s_den2"], writes=["s_den"])
    P.op("dve", lambda e: e.tensor_tensor(out=S(44, 48), in0=S(44, 48), in1=S(36, 40), op=ALU.max), reads=["s_den", "s_e"], writes=["s_den"])
    P.op("dve", lambda e: e.reciprocal(out=S(44, 48), in_=S(44, 48)), reads=["s_den"], writes=["s_den"])
    tB = tmpB[0:NS, 0:512]
    bc4 = lambda a: a.unsqueeze(2).to_broadcast([NS, 4, 128])
    P.op("dve", lambda e: e.tensor_tensor(out=v4(tB), in0=v4(k), in1=bc4(S(28, 32)), op=ALU.mult), reads=["zs", "s_w"], writes=["tmpB"])
    P.op("dve", lambda e: e.tensor_scalar(out=tB, in0=tB, scalar1=128.0 ** -0.5, scalar2=None, op0=ALU.mult), reads=["tmpB"], writes=["tmpB"])
    P.op("dve", lambda e: e.tensor_tensor(out=v4(n0s), in0=v4(n0s), in1=bc4(S(32, 36)), op=ALU.mult), reads=["n0s", "s_a", "tmpA"], writes=["n0s"])
    P.op("dve", lambda e: e.tensor_tensor(out=n0s, in0=n0s, in1=tB, op=ALU.add), reads=["n0s", "tmpB"], writes=["n0s"])
    P.dma("sp", lambda e: e.dma_start(out=n_s[:, :], in_=n0s), slot="o_ns", reads=["n0s"], is_output=True)
    bk, bkn = nextmm()
    def _tq(e):
        for h in range(4):
            e.transpose(out=bk[:, h * NS:(h + 1) * NS], in_=q[:, h * 128:(h + 1) * 128], identity=ident[0:NS, 0:NS])
        for h in range(4):
            ins = e.transpose(out=bk[:, 64 + h * NS:64 + (h + 1) * NS], in_=tB[:, h * 128:(h + 1) * 128], identity=ident[0:NS, 0:NS])
        return ins
    P.op("pe", _tq, reads=["zs", "tmpB", "ident"], writes=[bkn])
    P.op("dve", lambda e: e.tensor_copy(out=ar32[:, 7168:7168 + 128], in_=bk[:, 0:128]), reads=[bkn], writes=["qTs", "wkTs"])
    for b in range(NS):
        Cb = C0s[b % 4]
        cn = f"ysb{b % 4}"
        if b == 0:
            for bb in range(4):
                P.dma("sp", lambda e, bb=bb: e.dma_start(out=C0s[bb], in_=sC[bb].rearrange("h d e -> d h e")),
                      slot=f"ldC{bb}", writes=[f"ysb{bb}"])
        def _mv(e, b=b, Cb=Cb):
            for h in range(4):
                ins = e.matmul(stp[:, h * NS + b:h * NS + b + 1], lhsT=Cb[:, h, :], rhs=qTs[:, h, b:b + 1], start=True, stop=True)
            return ins
        P.op("pe", _mv, reads=[cn, "qTs"], writes=["stp"])
        selb = ident[0:NS, b:b + 1].to_broadcast([NS, 128])
        selbb = identb[:, b:b + 1].to_broadcast([NS, 128])
        bkv, bkvn = nextmm()
        P.op("pe", lambda e, selbb=selbb, bkv=bkv: e.matmul(bkv[:, :], lhsT=selbb, rhs=vb, start=True, stop=True),
             reads=["identb", "vb"], writes=[bkvn])
        bka, bkan = nextmm()
        P.op("pe", lambda e, selb=selb, bka=bka: e.matmul(bka[:, 0:4], lhsT=selb, rhs=S(32, 36), start=True, stop=True),
             reads=["ident", "s_a"], writes=[bkan])
        P.op("dve", lambda e, b=b, bka=bka: e.tensor_copy(out=abcs[:, b, :], in_=bka[:, 0:4]), reads=[bkan], writes=["abcs"])
        tv = tmpA[:, 0:512]
        P.op("dve", lambda e, b=b, bkv=bkv, tv=tv: e.tensor_tensor(out=v4(tv), in0=v4(bkv[:, :]),
                                                                   in1=wkTs[:, :, b].unsqueeze(2).to_broadcast([128, 4, 128]), op=ALU.mult),
             reads=[bkvn, "wkTs"], writes=["tmpA"])
        P.op("dve", lambda e, b=b, Cb=Cb: e.tensor_tensor(out=Cb, in0=Cb, in1=abcs[:, b, :].unsqueeze(2).to_broadcast([128, 4, 128]), op=ALU.mult),
             reads=[cn, "abcs"], writes=[cn])
        P.op("dve", lambda e, Cb=Cb, tv=tv: e.tensor_tensor(out=Cb, in0=Cb, in1=v4(tv), op=ALU.add), reads=[cn, "tmpA"], writes=[cn])
        P.dma("sp", lambda e, b=b, Cb=Cb: e.dma_start(out=C_s[b].rearrange("h d e -> d h e"), in_=Cb), slot=f"stC{b % 4}",
              reads=[cn], is_output=True)
        if b + 4 < NS:
            P.dma("sp", lambda e, b=b, Cb=Cb: e.dma_start(out=Cb, in_=sC[b + 4].rearrange("h d e -> d h e")),
                  slot=f"ldC{b % 4}", writes=[cn])
    P.op("dve", lambda e: e.tensor_copy(out=qCs, in_=stp[:, 0:64]), reads=["stp"], writes=["qCs"])
    bk2, bk2n = nextmm()
    def _tqc(e):
        for h in range(4):
            ins = e.transpose(out=bk2[0:NS, h * 128:(h + 1) * 128], in_=qCs[:, h * NS:(h + 1) * NS], identity=ident[:, :])
        return ins
    P.op("pe", _tqc, reads=["qCs", "ident"], writes=[bk2n])
    tn = tmpA[0:NS, 0:512]
    P.op("dve", lambda e: e.tensor_tensor(out=v4(tn), in0=v4(bk2[0:NS, :]), in1=bc4(S(32, 36)), op=ALU.mult), reads=[bk2n, "s_a", "tmpA"], writes=["tmpA"])
    tv2 = tmpB[0:NS, 0:512]
    P.op("dve", lambda e: e.tensor_tensor(out=v4(tv2), in0=v4(v), in1=bc4(S(40, 44)), op=ALU.mult), reads=["zs", "s_qk", "tmpB"], writes=["tmpB"])
    P.op("dve", lambda e: e.tensor_tensor(out=tn, in0=tn, in1=tv2, op=ALU.add), reads=["tmpA", "tmpB"], writes=["tmpA"])
    P.op("dve", lambda e: e.tensor_tensor(out=v4(tn), in0=v4(tn), in1=bc4(S(44, 48)), op=ALU.mult), reads=["tmpA", "s_den"], writes=["tmpA"])
    P.op("dve", lambda e: e.tensor_tensor(out=tv2, in0=tn, in1=tn, op=ALU.mult), reads=["tmpA", "tmpB"], writes=["tmpB"])
    P.op("dve", lambda e: e.tensor_reduce(out=S(48, 52), in_=v4(tv2), axis=AX.X, op=ALU.add), reads=["tmpB"], writes=["s_ss"])
    P.op("act", lambda e: e.activation(out=S(48, 52), in_=S(48, 52), func=AF.Ln, scale=1.0 / 128, bias=epsc[0:NS, :]), reads=["s_ss"], writes=["s_ss"])
    P.op("act", lambda e: e.activation(out=S(48, 52), in_=S(48, 52), func=AF.Exp, scale=-0.5), reads=["s_ss"], writes=["s_ss"])
    P.op("dve", lambda e: e.tensor_tensor(out=v4(tn), in0=v4(tn), in1=bc4(S(48, 52)), op=ALU.mult), reads=["tmpA", "s_ss"], writes=["tmpA"])
    P.op("dve", lambda e: e.tensor_tensor(out=v4(tn), in0=v4(tn), in1=ghbc[0:NS, :].unsqueeze(1).to_broadcast([NS, 4, 128]), op=ALU.mult),
         reads=["tmpA", "ghbc"], writes=["tmpA"])
    P.op("act", lambda e: e.activation(out=tv2, in_=o, func=AF.Exp, scale=-1.0), reads=["zs", "tmpB"], writes=["tmpB"])
    P.op("dve", lambda e: e.tensor_scalar(out=tv2, in0=tv2, scalar1=1.0, scalar2=None, op0=ALU.add), reads=["tmpB"], writes=["tmpB"])
    P.op("dve", lambda e: e.reciprocal(out=tv2, in_=tv2), reads=["tmpB"], writes=["tmpB"])
    P.op("dve", lambda e: e.tensor_tensor(out=tn, in0=tn, in1=tv2, op=ALU.mult), reads=["tmpA", "tmpB"], writes=["tmpA"])
    bk3, bk3n = nextmm()
    def _th(e):
        for h in range(4):
            ins = e.transpose(out=bk3[:, h * NS:(h + 1) * NS], in_=tn[:, h * 128:(h + 1) * 128], identity=ident[0:NS, 0:NS])
        return ins
    P.op("pe", _th, reads=["tmpA", "ident"], writes=[bk3n])
    P.op("act", lambda e: e.activation(out=hT_s[:, 0:4, :], in_=bk3[:, 0:64].rearrange("p (h b) -> p h b", h=4), func=AF.Copy),
         reads=[bk3n], writes=["hT_s"])
    tp = tmpB[0:NS, 0:512]
    for gi in range(4):
        win = 2 ** (gi + 1)
        r0 = [0, 1, 4, 11][gi]
        P.op("dve", lambda e, gi=gi, win=win, r0=r0: e.tensor_reduce(
            out=tp[:, gi * 128:(gi + 1) * 128], in_=hist[:, r0:r0 + win - 1, :].rearrange("p r c -> p c r"), axis=AX.X, op=ALU.add),
            reads=[f"hist{gi}", "tmpB"], writes=["tmpB"])
        P.op("dve", lambda e, gi=gi: e.tensor_tensor(out=tp[:, gi * 128:(gi + 1) * 128], in0=tp[:, gi * 128:(gi + 1) * 128],
                                                     in1=u[:, gi * 128:(gi + 1) * 128], op=ALU.add), reads=["tmpB", "zs"], writes=["tmpB"])
        P.op("dve", lambda e, gi=gi, win=win: e.scalar_tensor_tensor(
            out=tp[:, gi * 128:(gi + 1) * 128], in0=tp[:, gi * 128:(gi + 1) * 128], scalar=1.0 / win,
            in1=u[:, gi * 128:(gi + 1) * 128], op0=ALU.mult, op1=ALU.subtract), reads=["tmpB", "zs"], writes=["tmpB"])
    bk4, bk4n = nextmm()
    def _tp(e):
        for gi in range(4):
            ins = e.transpose(out=bk4[:, gi * NS:(gi + 1) * NS], in_=tp[:, gi * 128:(gi + 1) * 128], identity=ident[0:NS, 0:NS])
        return ins
    P.op("pe", _tp, reads=["tmpB", "ident"], writes=[bk4n])
    P.op("act", lambda e: e.activation(out=pooledT_s[:, :, :], in_=bk4[:, 0:64].rearrange("p (h b) -> p h b", h=4), func=AF.Copy),
         reads=[bk4n], writes=["pooledT_s"])
    for gi in range(4):
        bk5, bk5n = nextmm()
        P.op("pe", lambda e, gi=gi, bk5=bk5: e.matmul(bk5[:, 0:NS], lhsT=wpool_b[:, gi, :], rhs=pooledT_s[:, gi, :], start=True, stop=True),
             reads=["wpool_b", "pooledT_s"], writes=[bk5n])
        P.op("act", lambda e, gi=gi, bk5=bk5: e.activation(out=hT_s[:, 4 + gi, :], in_=bk5[:, 0:NS], func=AF.Identity, scale=pscol[:, gi:gi + 1]),
             reads=[bk5n, "pscol"], writes=["hT_s"])


_NC_CACHE = {}


def kernel(x_prompt, x_sample, c_prompt, c_sample, state_C, state_n, state_m, state_pool,
           w_ada, b_ada, g_pre1, g_post1, w_in, b_ig, b_fg, g_head, w_pool, pool_scale,
           w_out, g_pre2, g_post2, w_up, w_down):
    f = lambda a: np.ascontiguousarray(np.asarray(a, dtype=np.float32))
    if "nc" not in _NC_CACHE:
        _NC_CACHE["nc"] = build_nc()
    nc = _NC_CACHE["nc"]
    x_prompt = f(x_prompt); x_sample = f(x_sample); c_prompt = f(c_prompt); c_sample = f(c_sample)
    state_C = f(state_C); state_n = f(state_n); state_m = f(state_m); state_pool = f(state_pool)
    shared = {"w_ada": f(w_ada)[0], "b_ada": f(b_ada)[0], "g_pre1": f(g_pre1)[0], "g_post1": f(g_post1)[0],
              "g_pre2": f(g_pre2)[0], "g_post2": f(g_post2)[0], "w_in": f(w_in)[0], "w_out": f(w_out)[0],
              "w_up": f(w_up)[0], "w_down": f(w_down)[0], "b_ig": f(b_ig)[0], "b_fg": f(b_fg)[0],
              "g_head": f(g_head)[0], "w_pool": f(w_pool)[0], "pool_scale": f(pool_scale)[0]}
    in_maps = []
    for c in range(NCORES):
        sl = slice(c * NS, (c + 1) * NS)
        m = dict(shared)
        m["x_p"] = x_prompt[c]
        m["x_s"] = np.ascontiguousarray(x_sample[sl, 0, :])
        m["c_all"] = np.ascontiguousarray(np.concatenate([c_sample[sl], c_prompt[c:c + 1]], axis=0))
        m["sC"] = np.ascontiguousarray(state_C[0, sl])
        m["sn"] = np.ascontiguousarray(state_n[0, sl].reshape(NS, 512))
        m["sm"] = np.ascontiguousarray(state_m[0, sl])
        m["spool"] = np.ascontiguousarray(state_pool[0, sl])
        in_maps.append(m)
    res = run_bass_kernel_spmd(nc, in_maps, core_ids=list(range(NCORES)))
    R = res.results
    cat = lambda k: np.concatenate([np.asarray(r[k]) for r in R], axis=0)
    y_p = np.stack([np.asarray(r["y_p"]) for r in R], axis=0)
    y_s = cat("y_s").reshape(NCORES * NS, 1, D)
    C_p = np.stack([np.asarray(r["C_p"]) for r in R], axis=0)[None]
    n_p = np.stack([np.asarray(r["n_p"]) for r in R], axis=0)[None]
    m_p = np.stack([np.asarray(r["m_p"]).reshape(4) for r in R], axis=0)[None]
    pool_p = np.stack([np.asarray(r["pool_p"]) for r in R], axis=0)[None]
    C_s = cat("C_s")[None]
    n_s = cat("n_s").reshape(NCORES * NS, 4, 128)[None]
    m_s = cat("m_s")[None]
    pool_s = cat("pool_s")[None]
    return (y_p.astype(np.float32), y_s.astype(np.float32), C_p.astype(np.float32), n_p.astype(np.float32),
            m_p.astype(np.float32), pool_p.astype(np.float32), C_s.astype(np.float32), n_s.astype(np.float32),
            m_s.astype(np.float32), pool_s.astype(np.float32))
```

```python
import contextlib
import numpy as np
import concourse.bass as bass
import concourse.mybir as mybir
from concourse.bass_utils import run_bass_kernel_spmd

F32 = mybir.dt.float32
BF16 = mybir.dt.bfloat16
AF = mybir.ActivationFunctionType
ALU = mybir.AluOpType
AX = mybir.AxisListType

ENGS = ("pe", "act", "dve", "pool", "sp")
NCORES = 8
D = 1024
SEQ = 2048
NS = 16
TT = 512
NTILE = SEQ // TT
EPS = 1e-6
SLOT = 8192
NSLOT = 3


class Prog:
    def __init__(self, nc, stack):
        self.nc = nc
        self.stack = stack
        self.streams = {e: [] for e in ENGS}
        self.esem = {e: stack.enter_context(nc.semaphore("S_" + e)) for e in ENGS if e != "sp"}
        self.ecount = {e: 0 for e in ENGS}
        self.slot_sem = {}
        self.slot_count = {}
        self.last_writer = {}
        self.readers = {}
        self.waited = {e: {} for e in ENGS}
        self.out_tokens = []
        self.off = False

    def _deps(self, eng, reads, writes):
        toks = []
        for b in reads:
            t = self.last_writer.get(b)
            if t is not None:
                toks.append(t)
        for b in writes:
            t = self.last_writer.get(b)
            if t is not None:
                toks.append(t)
            toks.extend(self.readers.get(b, ()))
        w = self.waited[eng]
        best = {}
        for (sem, val, key) in toks:
            if w.get(key, 0) >= val:
                continue
            if key not in best or best[key][1] < val:
                best[key] = (sem, val)
        waits = []
        for key, (sem, val) in best.items():
            w[key] = val
            waits.append((sem, val))
        return waits

    def _record(self, tok, reads, writes):
        for b in reads:
            self.readers.setdefault(b, []).append(tok)
        for b in writes:
            self.last_writer[b] = tok
            self.readers[b] = []

    def op(self, eng, fn, reads=(), writes=()):
        if self.off:
            return None
        waits = self._deps(eng, reads, writes)
        self.ecount[eng] += 1
        tok = (self.esem[eng], self.ecount[eng], "E" + eng)
        self.streams[eng].append((fn, waits, (self.esem[eng], 1)))
        self._record(tok, reads, writes)
        return tok

    def dma(self, queue, fn, slot, reads=(), writes=(), is_output=False):
        if self.off:
            return None
        waits = self._deps(queue, reads, writes)
        if slot not in self.slot_sem:
            self.slot_sem[slot] = self.stack.enter_context(self.nc.semaphore("D_" + str(slot)))
            self.slot_count[slot] = 0
        self.slot_count[slot] += 16
        tok = (self.slot_sem[slot], self.slot_count[slot], "D" + str(slot))
        self.streams[queue].append((fn, waits, (self.slot_sem[slot], 16)))
        self._record(tok, reads, writes)
        if is_output:
            self.out_tokens.append(tok)
        return tok

    def emit(self):
        nc = self.nc
        fin = {}
        for (sem, val, key) in self.out_tokens:
            if key not in fin or fin[key][1] < val:
                fin[key] = (sem, val)
        final_waits = list(fin.values())

        def run(engine, name):
            for (fn, waits, inc) in self.streams[name]:
                for (sem, val) in waits:
                    engine.wait_ge(sem, val)
                ins = fn(engine)
                ins.then_inc(inc[0], inc[1])
            if name == "sp":
                for (sem, val) in final_waits:
                    engine.wait_ge(sem, val)

        with nc.allow_non_contiguous_dma(reason="small strided state/param transfers"), nc.Block() as block:
            @block.tensor
            def _(e):
                run(e, "pe")

            @block.scalar
            def _(e):
                run(e, "act")

            @block.vector
            def _(e):
                run(e, "dve")

            @block.gpsimd
            def _(e):
                run(e, "pool")

            @block.sync
            def _(e):
                run(e, "sp")


WBLOCKS = [("w_in", 0, 1024, 8), ("w_in", 1024, 1024, 8), ("w_in", 2048, 520, 8),
           ("w_out", 0, 1024, 8),
           ("w_up", 0, 1024, 8), ("w_up", 1024, 1024, 8), ("w_up", 2048, 1024, 8), ("w_up", 3072, 1024, 8),
           ("w_down", 0, 256, 32), ("w_down", 256, 256, 32), ("w_down", 512, 256, 32), ("w_down", 768, 256, 32)]


def build_nc(stop=99):
    nc = bass.Bass("TRN2", target_bir_lowering=False)
    dt_in = lambda name, shape: nc.dram_tensor(name, shape, F32, kind="ExternalInput").ap()
    dt_out = lambda name, shape: nc.dram_tensor(name, shape, F32, kind="ExternalOutput").ap()
    x_p = dt_in("x_p", [SEQ, D]); x_s = dt_in("x_s", [NS, D]); c_all = dt_in("c_all", [NS + 1, D])
    sC = dt_in("sC", [NS, 4, 128, 128]); sn = dt_in("sn", [NS, 512]); sm = dt_in("sm", [NS, 4])
    spool = dt_in("spool", [NS, 15, 512])
    w_ada = dt_in("w_ada", [D, 6 * D]); b_ada = dt_in("b_ada", [6 * D])
    gvec = {n: dt_in(n, [D]) for n in ("g_pre1", "g_post1", "g_pre2", "g_post2")}
    W = {"w_in": dt_in("w_in", [D, 2568]), "w_out": dt_in("w_out", [D, D]),
         "w_up": dt_in("w_up", [D, 4 * D]), "w_down": dt_in("w_down", [4 * D, D])}
    b_ig = dt_in("b_ig", [4]); b_fg = dt_in("b_fg", [4]); g_head = dt_in("g_head", [128])
    w_pool = dt_in("w_pool", [4, 128, 128]); pool_scale = dt_in("pool_scale", [512])
    y_p = dt_out("y_p", [SEQ, D]); y_s = dt_out("y_s", [NS, D])
    C_p = dt_out("C_p", [4, 128, 128]); n_p = dt_out("n_p", [4, 128]); m_p = dt_out("m_p", [4, 1])
    pool_p = dt_out("pool_p", [15, 512])
    C_s = dt_out("C_s", [NS, 4, 128, 128]); n_s = dt_out("n_s", [NS, 512]); m_s = dt_out("m_s", [NS, 4])
    pool_s = dt_out("pool_s", [NS, 15, 512])
    WB = {"w_in": nc.dram_tensor("wb_in", [D, 2568], BF16, kind="Internal").ap(),
          "w_out": nc.dram_tensor("wb_out", [D, D], BF16, kind="Internal").ap(),
          "w_up": nc.dram_tensor("wb_up", [D, 4 * D], BF16, kind="Internal").ap(),
          "w_down": nc.dram_tensor("wb_dn", [4 * D, D], BF16, kind="Internal").ap()}
    muscr = nc.dram_tensor("muscr", [2, 4, 513], F32, kind="Internal").ap()

    with contextlib.ExitStack() as st:
        P = Prog(nc, st)

        def stage(n):
            if n >= stop:
                P.off = True
        sb = lambda name, shape, dt=F32: st.enter_context(nc.sbuf_tensor(name, shape, dt))
        ps = lambda name, shape, dt=F32: st.enter_context(nc.psum_tensor(name, shape, dt))

        slots = [sb(f"slot{i}", [128, SLOT], BF16) for i in range(NSLOT)]
        xt = sb("xt", [128, 4, D]); ysb = sb("ysb", [128, 4, D])
        arena = sb("arena", [128, 32 * 512], BF16)
        fT = arena[:, :].rearrange("p (c t) -> p c t", c=32)
        hT = sb("hT", [128, 8, TT], BF16)
        qT = sb("qT", [128, 4, TT], BF16); kT = sb("kT", [128, 4, TT], BF16)
        vtok = sb("vtok", [128, 4, 4, 129], BF16); ktok = sb("ktok", [128, 4, 512], BF16)
        GO = sb("GO", [128, 4, 512], BF16); uext = sb("uext", [128, 4, 16 + TT])
        mubc = sb("mubc", [128, 4, 513]); abc = sb("abc", [128, 4, 128])
        tmpA = sb("tmpA", [128, 528]); tmpB = sb("tmpB", [128, 528])
        junk = sb("junk", [128, 512], BF16)
        GA = [sb("GA1", [128, D]), sb("GA2", [128, D])]
        GAs = [sb("GA1s", [NS, D]), sb("GA2s", [NS, D])]
        modT = [sb(f"modT{i}", [128, 8, NS + 1]) for i in range(4)]
        Cst = sb("Cst", [128, 4, 129]); Cbf = sb("Cbf", [128, 4, 129], BF16)
        kw = sb("kw", [128, 4, 128], BF16); STb = sb("STb", [128, 4, 128], BF16); qaT = sb("qaT", [128, 4, 128], BF16)
        htok = sb("htok", [128, 512])
        identb = sb("identb", [NS, NS], BF16); ident = sb("ident", [128, 128]); mbig = sb("mbig", [128, 128]); ghbc = sb("ghbc", [128, 128])
        wpool_b = sb("wpool_b", [128, 4, 128], BF16); pscol = sb("pscol", [128, 4])
        invcnt = sb("invcnt", [128, 4, 16])
        ones4 = sb("ones4", [4, 512]); epsc = sb("epsc", [128, 1]); onec = sb("onec", [128, 1])
        gb = sb("gb", [4, 2])
        negB1 = sb("negB", [4, 513]); Mu1 = sb("Mu", [4, 513]); Ug = sb("Ug", [4, 512])
        ucol = sb("ucol", [128, 4, 2, 4])
        sm4 = sb("sm4", [128, 80])
        mfin = sb("mfin", [4, 1])
        xt_s = sb("xt_s", [NS, D])
        hT_s = sb("hT_s", [128, 8, NS], BF16); fT_s = sb("fT_s", [128, 32, NS], BF16)
        pooledT_s = sb("pooledT_s", [128, 4, NS], BF16)
        sms = sb("sms", [NS, 80])
        ar32 = arena[:, :].bitcast(F32)
        mm = [ps(f"mm{i}", [128, 512]) for i in range(3)]
        stp = ps("stp", [128, 512]); nump = ps("nump", [128, 1024]); cup = ps("cup", [128, 1024])
        numv = nump[:, :].rearrange("p (h c) -> p h c", h=4)
        cupv = cup[:, :].rearrange("p (h c) -> p h c", h=4)
        stv = stp[:, :].rearrange("p (h c) -> p h c", h=4)
        mmctr = [0]

        mmmode = {"dense": True}
        mmbanks = [(mm[0][:, :], "mm0"), (mm[1][:, :], "mm1"), (mm[2][:, :], "mm2"),
                   (stp[:, :], "stp"), (nump[:, 0:512], "nump"), (cup[:, 0:512], "cup")]

        def nextmm():
            n = 6 if mmmode["dense"] else 3
            i = mmctr[0] % n
            mmctr[0] += 1
            return mmbanks[i]

        stage(0)
        wl = {"n": 0}

        def load_block(b):
            s = wl["n"] % NSLOT
            wl["n"] += 1
            src, c0, ncol, K = WBLOCKS[b]
            view = slots[s][:, 0:K * ncol].rearrange("p (k n) -> p k n", k=K)
            P.dma("sp", lambda e: e.dma_start(out=view, in_=WB[src][:, c0:c0 + ncol].rearrange("(k p) n -> p k n", p=128)),
                  slot=f"wl{s}", reads=CASTNAME[b], writes=[f"slot{s}"])
            return view, f"slot{s}"

        P.op("pool", lambda e: e.memset(ident[:, :], 1.0), writes=["ident"])
        P.op("pool", lambda e: e.affine_select(out=ident[:, :], in_=ident[:, :], pattern=[[-1, 128]],
                                               compare_op=ALU.is_equal, fill=0.0, base=0, channel_multiplier=1),
             reads=["ident"], writes=["ident"])
        P.op("pool", lambda e: e.tensor_copy(out=identb[:, :], in_=ident[0:NS, 0:NS]), reads=["ident"], writes=["identb"])
        P.op("pool", lambda e: e.memset(mbig[:, :], 0.0), writes=["mbig"])
        P.op("pool", lambda e: e.affine_select(out=mbig[:, :], in_=mbig[:, :], pattern=[[1, 128]],
                                               compare_op=ALU.is_ge, fill=1.0e4, base=0, channel_multiplier=-1),
             reads=["mbig"], writes=["mbig"])
        P.op("pool", lambda e: e.memset(ones4[:, :], 1.0), writes=["ones4"])
        P.op("pool", lambda e: e.memset(epsc[:, :], EPS), writes=["epsc"])
        P.op("pool", lambda e: e.memset(onec[:, :], 1.0), writes=["onec"])
        P.op("pool", lambda e: e.memset(Cst[:, :, :], 0.0), writes=["Cst"])
        P.op("pool", lambda e: e.memset(Cbf[:, :, :], 0.0), writes=["Cbf"])
        P.op("pool", lambda e: e.memset(vtok[:, :, :, :], 1.0), writes=["vtok"])
        P.op("pool", lambda e: e.memset(uext[:, :, 0:16], 0.0), writes=["uext"])
        P.op("pool", lambda e: e.memset(Mu1[:, :], 0.0), writes=["Mu"])
        P.op("pool", lambda e: e.memset(negB1[:, :], 0.0), writes=["negB"])
        P.op("pool", lambda e: e.iota(invcnt[:, 0, :], [[1, 16]], base=1, channel_multiplier=0,
                                      allow_small_or_imprecise_dtypes=True), writes=["invcnt"])
        for g in range(1, 4):
            P.op("pool", lambda e, g=g: e.tensor_copy(out=invcnt[:, g, :], in_=invcnt[:, 0, :]),
                 reads=["invcnt"], writes=["invcnt"])
        for g in range(4):
            P.op("pool", lambda e, g=g: e.tensor_scalar(out=invcnt[:, g, :], in0=invcnt[:, g, :],
                                                        scalar1=float(2 ** (g + 1)), scalar2=None, op0=ALU.min),
                 reads=["invcnt"], writes=["invcnt"])
        P.op("dve", lambda e: e.reciprocal(out=invcnt[:, :, :], in_=invcnt[:, :, :]), reads=["invcnt"], writes=["invcnt"])
        P.dma("sp", lambda e: e.dma_start(out=ghbc[:, :], in_=g_head.partition_broadcast(128)), slot="c0", writes=["ghbc"])
        P.dma("sp", lambda e: e.dma_start(out=gb[:, 0:1], in_=b_ig.rearrange("(h o) -> h o", o=1)), slot="c1", writes=["gb0"])
        P.dma("sp", lambda e: e.dma_start(out=gb[:, 1:2], in_=b_fg.rearrange("(h o) -> h o", o=1)), slot="c2", writes=["gb1"])
        P.op("dve", lambda e: e.tensor_scalar(out=gb[:, 1:2], in0=gb[:, 1:2], scalar1=-1.0, scalar2=None, op0=ALU.mult),
             reads=["gb1"], writes=["gb1"])
        P.dma("sp", lambda e: e.dma_start(out=pscol[:, :], in_=pool_scale.rearrange("(g c) -> c g", c=128)),
              slot="c3", writes=["pscol"])

        NM = NS + 1
        call = ar32[0:NM, 0:1024]; sil = ar32[0:NM, 1024:2048]
        sel16_t = sb("sel16_t", [NM, 128]); siluT_t = sb("siluT_t", [128, 8, NM])
        modt = [htok[0:NM, :]] * 2; badab = [abc[0:NM, :, :].rearrange("p h c -> p (h c)")] * 2; gvb = [tmpB[0:NM, 0:512]] * 2
        sel16 = sel16_t[:, :]
        siluT = siluT_t[:, :, :]
        P.dma("sp", lambda e: e.dma_start(out=call, in_=c_all[:, :]), slot="c4", writes=["call"])
        P.op("act", lambda e: e.activation(out=sil, in_=call, func=AF.Exp, scale=-1.0), reads=["call"], writes=["sil"])
        P.op("dve", lambda e: e.tensor_scalar(out=sil, in0=sil, scalar1=1.0, scalar2=None, op0=ALU.add), reads=["sil"], writes=["sil"])
        P.op("dve", lambda e: e.reciprocal(out=sil, in_=sil), reads=["sil"], writes=["sil"])
        P.op("dve", lambda e: e.tensor_tensor(out=sil, in0=sil, in1=call, op=ALU.mult), reads=["sil", "call"], writes=["sil"])
        P.op("pool", lambda e: e.memset(sel16, 1.0), writes=["sel16"])
        P.op("pool", lambda e: e.affine_select(out=sel16, in_=sel16, pattern=[[0, 128]], compare_op=ALU.is_equal,
                                               fill=0.0, base=-NS, channel_multiplier=1), reads=["sel16"], writes=["sel16"])
        bk, bkn = nextmm()
        def _silT(e):
            for k in range(8):
                ins = e.transpose(out=bk[:, k * NM:(k + 1) * NM], in_=sil[:, k * 128:(k + 1) * 128], identity=ident[0:NM, 0:NM])
            return ins
        P.op("pe", _silT, reads=["sil", "ident"], writes=[bkn])
        P.op("dve", lambda e: e.tensor_copy(out=siluT, in_=bk[:, 0:8 * NM].rearrange("p (k m) -> p k m", k=8)),
             reads=[bkn], writes=["siluT"])
        def do_mod(t):
            kind, half = t // 2, t % 2
            pp = 0
            wv, wn = None, None
            s = wl["n"] % NSLOT
            wl["n"] += 1
            wv = slots[s][:, :].bitcast(F32).rearrange("p (k n) -> p k n", k=8)
            P.dma("sp", lambda e, s=s, t=t, wv=wv: e.dma_start(
                out=wv, in_=w_ada[:, t * 512:(t + 1) * 512].rearrange("(k p) n -> p k n", p=128)),
                slot=f"wl{s}", writes=[f"slot{s}"] + (["adaload3"] if t == 3 else []))
            P.dma("sp", lambda e, t=t, pp=pp: e.dma_start(out=badab[pp], in_=b_ada[t * 512:(t + 1) * 512].partition_broadcast(NM)),
                  slot=f"bad{pp}", writes=["abc"])
            if kind in (1, 2, 4, 5):
                gname = {1: "g_pre1", 2: "g_post1", 4: "g_pre2", 5: "g_post2"}[kind]
                P.dma("sp", lambda e, pp=pp, gname=gname, half=half: e.dma_start(
                    out=gvb[pp], in_=gvec[gname][half * 512:(half + 1) * 512].partition_broadcast(NM)),
                    slot="tmpB", writes=["tmpB"])
            bk, bkn = nextmm()
            def _modmm(e, wv=wv, bk=bk):
                for k in range(8):
                    ins = e.matmul(bk[0:NM, :], lhsT=siluT[:, k, :], rhs=wv[:, k, :], start=(k == 0), stop=(k == 7))
                return ins
            P.op("pe", _modmm, reads=["siluT", f"slot{s}"], writes=[bkn])
            P.op("dve", lambda e, pp=pp, bk=bk: e.tensor_tensor(out=modt[pp], in0=bk[0:NM, :], in1=badab[pp], op=ALU.add),
                 reads=[bkn, "abc"], writes=["htok"])
            if kind in (1, 4):
                P.op("dve", lambda e, pp=pp: e.scalar_tensor_tensor(out=modt[pp], in0=modt[pp], scalar=1.0, in1=gvb[pp],
                                                                    op0=ALU.add, op1=ALU.mult),
                     reads=["htok", "tmpB"], writes=["htok"])
            if kind in (2, 5):
                P.op("dve", lambda e, pp=pp: e.tensor_tensor(out=modt[pp], in0=modt[pp], in1=gvb[pp], op=ALU.mult),
                     reads=["htok", "tmpB"], writes=["htok"])
                gi = 0 if kind == 2 else 1
                P.op("act", lambda e, pp=pp, gi=gi, half=half: e.activation(
                    out=GAs[gi][:, half * 512:(half + 1) * 512], in_=modt[pp][0:NS, :], func=AF.Copy),
                    reads=["htok"], writes=[f"GAs{gi}"])
                bk2, bk2n = nextmm()
                P.op("pe", lambda e, pp=pp, bk2=bk2: e.matmul(bk2[:, :], lhsT=sel16, rhs=modt[pp], start=True, stop=True),
                     reads=["sel16", "htok"], writes=[bk2n])
                P.op("dve", lambda e, gi=gi, half=half, bk2=bk2: e.tensor_copy(out=GA[gi][:, half * 512:(half + 1) * 512], in_=bk2[:, :]),
                     reads=[bk2n], writes=[f"GA{gi}"])
            else:
                mi = {0: 0, 1: 1, 3: 2, 4: 3}[kind]
                bk2, bk2n = nextmm()
                def _modT(e, pp=pp, bk2=bk2):
                    for c in range(4):
                        ins = e.transpose(out=bk2[:, c * NM:(c + 1) * NM], in_=modt[pp][:, c * 128:(c + 1) * 128],
                                          identity=ident[0:NM, 0:NM])
                    return ins
                P.op("pe", _modT, reads=["htok", "ident"], writes=[bk2n])
                P.op("dve", lambda e, mi=mi, half=half, bk2=bk2: e.tensor_copy(
                    out=modT[mi][:, half * 4:(half + 1) * 4, :], in_=bk2[:, 0:4 * NM].rearrange("p (c m) -> p c m", c=4)),
                    reads=[bk2n], writes=[f"modT{mi}"])

        stage(1)
        P.dma("pool", lambda e: e.dma_start(out=WB["w_in"][:, 0:2048], in_=W["w_in"][:, 0:2048], max_dma_last_dim=2048), slot="cast_c_in0", writes=["c_in0"])
        P.dma("pool", lambda e: e.dma_start(out=WB["w_in"][:, 2048:2568], in_=W["w_in"][:, 2048:2568], max_dma_last_dim=2048), slot="cast_c_in1", writes=["c_in1"])
        for t in range(4):
            do_mod(t)
        stage(2)
        def cast(name, dst, src, after="adaload3"):
            P.dma("pool", lambda e: e.dma_start(out=dst, in_=src, max_dma_last_dim=2048), slot="cast_" + name, reads=[after], writes=[name])
        two = lambda ap: ap.rearrange("(a b) n -> a (b n)", b=2)
        P.dma("pool", lambda e: e.dma_start(out=wpool_b[:, :, :], in_=w_pool.rearrange("g c d -> c g d")),
              slot="castwp", writes=["wpool_b"])
        cast("c_out", two(WB["w_out"]), two(W["w_out"]))
        cast("c_up0", WB["w_up"][:, 0:2048], W["w_up"][:, 0:2048])
        cast("c_up1", WB["w_up"][:, 2048:4096], W["w_up"][:, 2048:4096])
        cast("c_dn0", two(WB["w_down"])[0:1024, :], two(W["w_down"])[0:1024, :])
        cast("c_dn1", two(WB["w_down"])[1024:2048, :], two(W["w_down"])[1024:2048, :])
        CASTNAME = {0: ["c_in0"], 1: ["c_in0"], 2: ["c_in1"], 3: ["c_out"], 4: ["c_up0"], 5: ["c_up0"], 6: ["c_up1"], 7: ["c_up1"],
                    8: ["c_dn0", "c_dn1"], 9: ["c_dn0", "c_dn1"], 10: ["c_dn0", "c_dn1"], 11: ["c_dn0", "c_dn1"]}


        class Ctx:
            pass

        def mkctx(kind, i=0):
            c = Ctx()
            c.kind = kind
            if kind == "p":
                c.ng, c.Pn, c.nt = 4, 128, TT
                c.xt = lambda j: xt[:, j, :]
                c.ysb = lambda j: ysb[:, j, :]
                c.xname = lambda j: f"xt{j}"
                c.yname = lambda j: f"ysb{j}"
                c.hT, c.hname = hT, "hT"
                c.fT, c.fname = fT, "fT"
                c.GA = GA
                c.ganame = ["GA0", "GA1"]
                c.src = lambda j: x_p[i * TT + j * 128: i * TT + (j + 1) * 128, :]
                c.dst = lambda j: y_p[i * TT + j * 128: i * TT + (j + 1) * 128, :]
                c.sm = sm4
                c.ssb = 32
                c.pfx = "p"
            else:
                c.ng, c.Pn, c.nt = 1, NS, NS
                c.xt = lambda j: xt_s[:, :]
                c.ysb = lambda j: mubc[:, :, :].rearrange("p h c -> p (h c)")[0:NS, 0:1024]
                c.xname = lambda j: "xt_s"
                c.yname = lambda j: "mubc"
                c.hT, c.hname = hT_s, "hT_s"
                c.fT, c.fname = fT_s, "fT_s"
                c.GA = GAs
                c.ganame = ["GAs0", "GAs1"]
                c.src = lambda j: x_s[:, :]
                c.dst = lambda j: y_s[:, :]
                c.sm = sms
                c.ssb = 64
                c.pfx = "s"
            return c

        def rstd_from_ss(c, col_ss, col_out, n, tag):
            Pn = c.Pn
            P.op("act", lambda e: e.activation(out=c.sm[0:Pn, col_out:col_out + n], in_=c.sm[0:Pn, col_ss:col_ss + n],
                                               func=AF.Ln, scale=1.0 / D, bias=epsc[0:Pn, :]),
                 reads=[c.pfx + tag + "ss"], writes=[c.pfx + tag + "rs"])
            P.op("act", lambda e: e.activation(out=c.sm[0:Pn, col_out:col_out + n], in_=c.sm[0:Pn, col_out:col_out + n],
                                               func=AF.Exp, scale=-0.5),
                 reads=[c.pfx + tag + "rs"], writes=[c.pfx + tag + "rs"])

        def prenorm(c, which):
            Pn = c.Pn
            mSH, mG = modT[2 * which], modT[2 * which + 1]
            tag = f"pn{which}"
            for j in range(c.ng):
                P.op("act", lambda e, j=j: e.activation(out=junk[0:Pn, :], in_=c.xt(j)[:, 0:512], func=AF.Square,
                                                        accum_out=c.sm[0:Pn, j:j + 1]),
                     reads=[c.xname(j)], writes=["junk", c.pfx + tag + "ss"])
                P.op("act", lambda e, j=j: e.activation(out=junk[0:Pn, :], in_=c.xt(j)[:, 512:1024], func=AF.Square,
                                                        accum_out=c.sm[0:Pn, c.ssb + j:c.ssb + 1 + j]),
                     reads=[c.xname(j)], writes=["junk", c.pfx + tag + "ss"])
            P.op("dve", lambda e: e.tensor_tensor(out=c.sm[0:Pn, 0:c.ng], in0=c.sm[0:Pn, 0:c.ng],
                                                  in1=c.sm[0:Pn, c.ssb:c.ssb + c.ng], op=ALU.add),
                 reads=[c.pfx + tag + "ss"], writes=[c.pfx + tag + "ss"])
            rstd_from_ss(c, 0, 4, c.ng, tag)
            for j in range(c.ng):
                P.op("dve", lambda e, j=j: e.tensor_scalar(out=c.ysb(j), in0=c.xt(j), scalar1=c.sm[0:Pn, 4 + j:5 + j],
                                                           scalar2=None, op0=ALU.mult),
                     reads=[c.xname(j), c.pfx + tag + "rs"], writes=[c.yname(j)])
            if c.kind == "p":
                for k in range(8):
                    bk, bkn = nextmm()
                    def _tr(e, k=k, bk=bk):
                        for j in range(4):
                            ins = e.transpose(out=bk[:, j * 128:(j + 1) * 128], in_=c.ysb(j)[:, k * 128:(k + 1) * 128],
                                              identity=ident[:, :])
                        return ins
                    P.op("pe", _tr, reads=[c.yname(j) for j in range(4)] + ["ident"], writes=[bkn])
                    if k % 2 == 0:
                        P.op("dve", lambda e, k=k, bk=bk: e.tensor_scalar(
                            out=c.hT[:, k, :], in0=bk[:, :], scalar1=mG[:, k, NS:NS + 1], scalar2=mSH[:, k, NS:NS + 1],
                            op0=ALU.mult, op1=ALU.add),
                            reads=[bkn, f"modT{2 * which}", f"modT{2 * which + 1}"], writes=[c.hname])
                    else:
                        P.op("act", lambda e, k=k, bk=bk: e.activation(
                            out=c.hT[:, k, :], in_=bk[:, :], func=AF.Identity, scale=mG[:, k, NS:NS + 1], bias=mSH[:, k, NS:NS + 1]),
                            reads=[bkn, f"modT{2 * which}", f"modT{2 * which + 1}"], writes=[c.hname])
            else:
                for hb in range(2):
                    bk, bkn = nextmm()
                    def _tr(e, hb=hb, bk=bk):
                        for kk in range(4):
                            k = hb * 4 + kk
                            ins = e.transpose(out=bk[:, kk * Pn:(kk + 1) * Pn], in_=c.ysb(0)[:, k * 128:(k + 1) * 128],
                                              identity=ident[0:Pn, 0:Pn])
                        return ins
                    P.op("pe", _tr, reads=[c.yname(0), "ident"], writes=[bkn])
                    tv = tmpA[:, 0:4 * NS].rearrange("p (k m) -> p k m", k=4)
                    P.op("dve", lambda e, hb=hb, bk=bk, tv=tv: e.tensor_tensor(
                        out=tv, in0=bk[:, 0:4 * NS].rearrange("p (k m) -> p k m", k=4),
                        in1=mG[:, hb * 4:(hb + 1) * 4, 0:NS], op=ALU.mult),
                        reads=[bkn, f"modT{2 * which + 1}"], writes=["tmpA"])
                    P.op("dve", lambda e, hb=hb, tv=tv: e.tensor_tensor(
                        out=c.hT[:, hb * 4:(hb + 1) * 4, :], in0=tv, in1=mSH[:, hb * 4:(hb + 1) * 4, 0:NS], op=ALU.add),
                        reads=["tmpA", f"modT{2 * which}"], writes=[c.hname])

        def fm_group(c, wv, wn, col0, rhs_hT, rhs_name, K=8, M=128):
            bk, bkn = nextmm()
            def _mm(e):
                for k in range(K):
                    ins = e.matmul(bk[0:M, 0:c.nt], lhsT=wv[:, k, col0:col0 + M], rhs=rhs_hT[:, k, 0:c.nt],
                                   start=(k == 0), stop=(k == K - 1))
                return ins
            P.op("pe", _mm, reads=[wn, rhs_name], writes=[bkn])
            return bk, bkn

        def tm_group(c, j, wv, wn, col0, ncol, lhs_hT, lhs_name, K=8):
            bk, bkn = nextmm()
            Pn = c.Pn
            def _mm(e):
                for k in range(K):
                    ins = e.matmul(bk[0:Pn, 0:ncol], lhsT=lhs_hT[:, k, j * Pn:(j + 1) * Pn], rhs=wv[:, k, col0:col0 + ncol],
                                   start=(k == 0), stop=(k == K - 1))
                return ins
            P.op("pe", _mm, reads=[wn, lhs_name], writes=[bkn])
            return bk, bkn

        def postnorm_all(c, gi, tag, to_ysb=False):
            Pn, ng = c.Pn, c.ng
            for j in range(ng):
                P.op("act", lambda e, j=j: e.activation(out=junk[0:Pn, :], in_=c.ysb(j)[:, 0:512], func=AF.Square,
                                                        accum_out=c.sm[0:Pn, 8 + j:9 + j]),
                     reads=[c.yname(j)], writes=["junk", c.pfx + tag + "ss"])
                P.op("act", lambda e, j=j: e.activation(out=junk[0:Pn, :], in_=c.ysb(j)[:, 512:1024], func=AF.Square,
                                                        accum_out=c.sm[0:Pn, c.ssb + j:c.ssb + 1 + j]),
                     reads=[c.yname(j)], writes=["junk", c.pfx + tag + "ssb"])
                P.op("dve", lambda e, j=j: e.tensor_tensor(out=c.ysb(j), in0=c.ysb(j), in1=c.GA[gi][0:Pn, :], op=ALU.mult),
                     reads=[c.yname(j), c.ganame[gi]], writes=[c.yname(j)])
            P.op("dve", lambda e: e.tensor_tensor(out=c.sm[0:Pn, 8:8 + ng], in0=c.sm[0:Pn, 8:8 + ng],
                                                  in1=c.sm[0:Pn, c.ssb:c.ssb + ng], op=ALU.add),
                 reads=[c.pfx + tag + "ss", c.pfx + tag + "ssb"], writes=[c.pfx + tag + "ss"])
            P.op("act", lambda e: e.activation(out=c.sm[0:Pn, 12:12 + ng], in_=c.sm[0:Pn, 8:8 + ng],
                                               func=AF.Ln, scale=1.0 / D, bias=epsc[0:Pn, :]),
                 reads=[c.pfx + tag + "ss"], writes=[c.pfx + tag + "rs"])
            P.op("act", lambda e: e.activation(out=c.sm[0:Pn, 12:12 + ng], in_=c.sm[0:Pn, 12:12 + ng],
                                               func=AF.Exp, scale=-0.5),
                 reads=[c.pfx + tag + "rs"], writes=[c.pfx + tag + "rs"])
            for j in range(ng):
                if to_ysb:
                    P.op("dve", lambda e, j=j: e.scalar_tensor_tensor(out=c.ysb(j), in0=c.ysb(j), scalar=c.sm[0:Pn, 12 + j:13 + j],
                                                                      in1=c.xt(j), op0=ALU.mult, op1=ALU.add),
                         reads=[c.yname(j), c.pfx + tag + "rs", c.xname(j)], writes=[c.yname(j)])
                else:
                    P.op("dve", lambda e, j=j: e.scalar_tensor_tensor(out=c.xt(j), in0=c.ysb(j), scalar=c.sm[0:Pn, 12 + j:13 + j],
                                                                      in1=c.xt(j), op0=ALU.mult, op1=ALU.add),
                         reads=[c.yname(j), c.pfx + tag + "rs", c.xname(j)], writes=[c.xname(j)])

        cs = mkctx("s")
        XS = [(qT[:, :, :].rearrange("p h t -> p (h t)").bitcast(F32), "qT"),
              (kT[:, :, :].rearrange("p h t -> p (h t)").bitcast(F32), "kT"),
              (ktok[:, :, :].rearrange("p j c -> p (j c)").bitcast(F32), "ktok"),
              (GO[:, :, :].rearrange("p j c -> p (j c)").bitcast(F32), "GO")]

        def early_prenorm_a(inext):
            for j in range(4):
                xs, xn = XS[j]
                P.dma("sp", lambda e, j=j, xs=xs: e.dma_start(out=xs, in_=x_p[inext * TT + j * 128: inext * TT + (j + 1) * 128, :]),
                      slot=f"ldxe{j}", writes=[xn])
            for j in range(4):
                xs, xn = XS[j]
                P.op("act", lambda e, j=j, xs=xs: e.activation(out=junk[:, :], in_=xs[:, 0:512], func=AF.Square,
                                                               accum_out=sm4[:, 56 + j:57 + j]),
                     reads=[xn], writes=["junk", "epn_ss"])
                P.op("act", lambda e, j=j, xs=xs: e.activation(out=junk[:, :], in_=xs[:, 512:1024], func=AF.Square,
                                                               accum_out=sm4[:, 60 + j:61 + j]),
                     reads=[xn], writes=["junk", "epn_ss"])
            P.op("dve", lambda e: e.tensor_tensor(out=sm4[:, 56:60], in0=sm4[:, 56:60], in1=sm4[:, 60:64], op=ALU.add),
                 reads=["epn_ss"], writes=["epn_ss"])
            P.op("act", lambda e: e.activation(out=sm4[:, 64:68], in_=sm4[:, 56:60], func=AF.Ln, scale=1.0 / D, bias=epsc[:, :]),
                 reads=["epn_ss"], writes=["epn_rs"])
            P.op("act", lambda e: e.activation(out=sm4[:, 64:68], in_=sm4[:, 64:68], func=AF.Exp, scale=-0.5),
                 reads=["epn_rs"], writes=["epn_rs"])
            for j in range(4):
                xs, xn = XS[j]
                P.op("dve", lambda e, j=j, xs=xs: e.tensor_scalar(out=xs, in0=xs, scalar1=sm4[:, 64 + j:65 + j], scalar2=None, op0=ALU.mult),
                     reads=[xn, "epn_rs"], writes=[xn])

        def early_prenorm_b():
            mSH, mG = modT[0], modT[1]
            for k in range(8):
                bk, bkn = nextmm()
                def _tr(e, k=k, bk=bk):
                    for j in range(4):
                        ins = e.transpose(out=bk[:, j * 128:(j + 1) * 128], in_=XS[j][0][:, k * 128:(k + 1) * 128], identity=ident[:, :])
                    return ins
                P.op("pe", _tr, reads=[XS[j][1] for j in range(4)] + ["ident"], writes=[bkn])
                if k % 2 == 0:
                    P.op("dve", lambda e, k=k, bk=bk: e.tensor_scalar(
                        out=hT[:, k, :], in0=bk[:, :], scalar1=mG[:, k, NS:NS + 1], scalar2=mSH[:, k, NS:NS + 1],
                        op0=ALU.mult, op1=ALU.add), reads=[bkn, "modT0", "modT1"], writes=["hT"])
                else:
                    P.op("act", lambda e, k=k, bk=bk: e.activation(
                        out=hT[:, k, :], in_=bk[:, :], func=AF.Identity, scale=mG[:, k, NS:NS + 1], bias=mSH[:, k, NS:NS + 1]),
                        reads=[bkn, "modT0", "modT1"], writes=["hT"])

        SAMPLE_ARENA = ["vb", "zs", "hist0", "hist1", "hist2", "hist3", "qTs", "wkTs", "qCs", "abcs", "n0s", "call", "sil"]
        zs = ar32[0:NS, 0:2568]
        hist = ar32[0:NS, 2688:2688 + 26 * 128].rearrange("p (r c) -> p r c", c=128)
        C0s = [ysb[:, i, 0:512].rearrange("p (h e) -> p h e", h=4) for i in range(4)]
        qTs = ar32[:, 7168:7168 + 64].rearrange("p (h b) -> p h b", h=4)
        wkTs = ar32[:, 7232:7232 + 64].rearrange("p (h b) -> p h b", h=4)
        qCs = ar32[:, 7296:7296 + 64]
        abcs = ar32[:, 7360:7360 + 64].rearrange("p (b h) -> p b h", h=4)
        n0s = ar32[0:NS, 7680:7680 + 512]

        def pool_group(g, bufs, tgt, tname, first_tile):
            win = 2 ** (g + 1)
            cur = None
            sh = 1
            bi = 0
            while sh < win:
                dstt, dn = bufs[bi]
                if cur is None:
                    P.op("pool", lambda e, sh=sh, dstt=dstt: e.tensor_tensor(
                        out=dstt[:, sh:528], in0=uext[:, g, sh:528], in1=uext[:, g, 0:528 - sh], op=ALU.add),
                        reads=["uext"], writes=[dn])
                else:
                    ct, cn = cur
                    P.op("pool", lambda e, sh=sh, dstt=dstt, ct=ct: e.tensor_tensor(
                        out=dstt[:, 2 * sh - 1:528], in0=ct[:, 2 * sh - 1:528], in1=ct[:, sh - 1:528 - sh], op=ALU.add),
                        reads=[cn], writes=[dn])
                cur = bufs[bi]
                bi ^= 1
                sh *= 2
            ct, cn = cur
            P.op("dve", lambda e, ct=ct: e.scalar_tensor_tensor(
                out=tgt(g), in0=ct[:, 16:528], scalar=1.0 / win, in1=uext[:, g, 16:528], op0=ALU.mult, op1=ALU.subtract),
                reads=[cn, "uext"], writes=[tname])
            if first_tile:
                ot, on = bufs[bi]
                P.op("dve", lambda e, ct=ct, ot=ot: e.tensor_tensor(out=ot[:, 0:16], in0=ct[:, 16:32], in1=invcnt[:, g, :], op=ALU.mult),
                     reads=[cn, "invcnt"], writes=[on])
                P.op("dve", lambda e, ot=ot: e.tensor_tensor(out=tgt(g)[:, 0:16], in0=ot[:, 0:16], in1=uext[:, g, 16:32], op=ALU.subtract),
                     reads=[on, "uext"], writes=[tname])
            bk, bkn = nextmm()
            P.op("pe", lambda e, bk=bk: e.matmul(bk[:, :], lhsT=wpool_b[:, g, :], rhs=tgt(g), start=True, stop=True),
                 reads=["wpool_b", tname], writes=[bkn])
            P.op("act", lambda e, bk=bk: e.activation(out=hT[:, 4 + g, :], in_=bk[:, :], func=AF.Identity, scale=pscol[:, g:g + 1]),
                 reads=[bkn, "pscol"], writes=["hT"])

        pref = None
        pending_G = []
        pending_post = None
        for i in range(NTILE):
            par = i % 2
            cp = mkctx("p", i)
            ctxs = [cp] + ([cs] if i == 0 else [])
            def load_x(c):
                for j in range(c.ng):
                    P.dma("sp", lambda e, c=c, j=j: e.dma_start(out=c.xt(j), in_=c.src(j)),
                          slot=f"ldx{c.pfx}{j}", writes=[c.xname(j)])
            for c in ctxs:
                if c.kind == "p" and i >= 1:
                    continue
                load_x(c)
                prenorm(c, 0)
            stage(3 + 10 * i)
            wv, wn = pref[0] if pref else load_block(0)
            for c in ctxs:
                if c.kind == "p":
                    for m in range(8):
                        bk, bkn = fm_group(c, wv, wn, m * 128, c.hT, c.hname)
                        if m < 4:
                            P.op("act", lambda e, m=m, bk=bk: e.activation(out=qT[:, m, :], in_=bk[:, :], func=AF.Copy),
                                 reads=[bkn], writes=["qT"])
                        else:
                            P.op("act", lambda e, m=m, bk=bk: e.activation(out=kT[:, m - 4, :], in_=bk[:, :], func=AF.Copy,
                                                                           scale=128.0 ** -0.5),
                                 reads=[bkn], writes=["kT"])
                    for j in range(4):
                        bk, bkn = tm_group(c, j, wv, wn, 512, 512, c.hT, c.hname)
                        P.op("dve", lambda e, j=j, bk=bk: e.tensor_scalar(out=ktok[:, j, :], in0=bk[:, :], scalar1=128.0 ** -0.5,
                                                                          scalar2=None, op0=ALU.mult),
                             reads=[bkn], writes=["ktok"])
                else:
                    for half in range(2):
                        bk, bkn = tm_group(c, 0, wv, wn, half * 512, 512, c.hT, c.hname)
                        P.op("act", lambda e, half=half, bk=bk: e.activation(out=zs[:, half * 512:(half + 1) * 512],
                                                                             in_=bk[0:NS, :], func=AF.Copy),
                             reads=[bkn], writes=["zs", "call", "sil"])
            wv, wn = pref[1] if pref else load_block(1)
            for c in ctxs:
                if c.kind == "p":
                    for j in range(4):
                        bk, bkn = tm_group(c, j, wv, wn, 0, 512, c.hT, c.hname)
                        P.op("act", lambda e, j=j, bk=bk: e.activation(
                            out=vtok[:, j, :, 0:128], in_=bk[:, :].rearrange("p (h c) -> p h c", h=4), func=AF.Copy),
                            reads=[bkn], writes=["vtok"])
                        bk, bkn = tm_group(c, j, wv, wn, 512, 512, c.hT, c.hname)
                        gt, gtn = (tmpA, "tmpA") if j % 2 == 0 else (tmpB, "tmpB")
                        P.op("act", lambda e, bk=bk, gt=gt: e.activation(out=gt[:, 0:512], in_=bk[:, :], func=AF.Exp, scale=-1.0),
                             reads=[bkn], writes=[gtn])
                        P.op("dve", lambda e, gt=gt: e.tensor_scalar(out=gt[:, 0:512], in0=gt[:, 0:512], scalar1=1.0, scalar2=None,
                                                                     op0=ALU.add), reads=[gtn], writes=[gtn])
                        P.op("dve", lambda e, gt=gt: e.reciprocal(out=gt[:, 0:512], in_=gt[:, 0:512]), reads=[gtn], writes=[gtn])
                        P.op("pool", lambda e, j=j, gt=gt: e.tensor_tensor(
                            out=GO[:, j, :].rearrange("p (h c) -> p h c", h=4), in0=gt[:, 0:512].rearrange("p (h c) -> p h c", h=4),
                            in1=ghbc[:, :].unsqueeze(1).to_broadcast([128, 4, 128]), op=ALU.mult),
                            reads=[gtn, "ghbc"], writes=["GO"])
                else:
                    for half in range(2):
                        bk, bkn = tm_group(c, 0, wv, wn, half * 512, 512, c.hT, c.hname)
                        P.op("act", lambda e, half=half, bk=bk: e.activation(out=zs[:, 1024 + half * 512:1024 + (half + 1) * 512],
                                                                             in_=bk[0:NS, :], func=AF.Copy),
                             reads=[bkn], writes=["zs", "call", "sil"])
            wv, wn = pref[2] if pref else load_block(2)
            igb = fgb = None
            for c in ctxs:
                if c.kind == "p":
                    igb, ign = fm_group(c, wv, wn, 0, c.hT, c.hname, M=4)
                    fgb, fgn = fm_group(c, wv, wn, 4, c.hT, c.hname, M=4)
                    P.op("act", lambda e, fgb=fgb: e.activation(out=Ug[:, :], in_=fgb[0:4, :], func=AF.Exp, scale=-1.0, bias=gb[:, 1:2]),
                         reads=[fgn, "gb1"], writes=["Ug"])
                    P.op("act", lambda e: e.activation(out=Ug[:, :], in_=Ug[:, :], func=AF.Ln, bias=onec[0:4, :]),
                         reads=["Ug", "onec"], writes=["Ug"])
                    P.op("dve", lambda e: e.tensor_tensor_scan(out=negB1[:, 1:513], data0=ones4[:, :], data1=Ug[:, :],
                                                               initial=negB1[:, 0:1], op0=ALU.mult, op1=ALU.add),
                         reads=["ones4", "Ug", "negB"], writes=["negB"])
                    P.op("dve", lambda e: e.tensor_copy(out=negB1[:, 0:1], in_=negB1[:, 512:513]), reads=["negB"], writes=["negB"])
                    P.op("dve", lambda e, igb=igb: e.scalar_tensor_tensor(out=Ug[:, :], in0=igb[0:4, :], scalar=gb[:, 0:1],
                                                                          in1=negB1[:, 1:513], op0=ALU.add, op1=ALU.add),
                         reads=[ign, "gb0", "negB"], writes=["Ug"])
                    P.op("dve", lambda e: e.tensor_tensor_scan(out=Mu1[:, 1:513], data0=ones4[:, :], data1=Ug[:, :],
                                                               initial=Mu1[:, 0:1], op0=ALU.mult, op1=ALU.max),
                         reads=["ones4", "Ug", "Mu"], writes=["Mu"])
                    P.op("dve", lambda e: e.tensor_tensor(out=negB1[:, 1:513], in0=negB1[:, 1:513], in1=Mu1[:, 1:513],
                                                          op=ALU.subtract),
                         reads=["negB", "Mu"], writes=["negB"])
                    P.dma("sp", lambda e, par=par: e.dma_start(out=muscr[par], in_=Mu1[:, :]), slot="mus",
                          reads=["Mu"], writes=[f"muscr{par}"])
                    P.dma("sp", lambda e, par=par: e.dma_start(out=mubc[:, :, :].rearrange("p h c -> p (h c)"),
                                                               in_=muscr[par].rearrange("h c -> (h c)").partition_broadcast(128)),
                          slot="mub", reads=[f"muscr{par}"], writes=["mubc"])
                    P.op("dve", lambda e: e.tensor_copy(out=Mu1[:, 0:1], in_=Mu1[:, 512:513]), reads=["Mu"], writes=["Mu"])
                    bk, bkn = nextmm()
                    def _gtr(e, bk=bk):
                        for j in range(4):
                            e.transpose(out=bk[:, j * 8:j * 8 + 4], in_=Ug[:, j * 128:(j + 1) * 128], identity=ident[0:4, 0:4])
                            ins = e.transpose(out=bk[:, j * 8 + 4:j * 8 + 8], in_=negB1[:, 1 + j * 128:1 + (j + 1) * 128], identity=ident[0:4, 0:4])
                        return ins
                    P.op("pe", _gtr, reads=["Ug", "negB", "ident"], writes=[bkn])
                    P.op("dve", lambda e, bk=bk: e.tensor_copy(out=ucol[:, :, :, :].rearrange("p j w h -> p (j w h)"), in_=bk[:, 0:32]),
                         reads=[bkn], writes=["ucol"])
                    P.op("act", lambda e: e.activation(out=ucol[:, :, 1, :], in_=ucol[:, :, 1, :], func=AF.Exp),
                         reads=["ucol"], writes=["ucol"])
                    for g in range(4):
                        bk, bkn = fm_group(c, wv, wn, 8 + g * 128, c.hT, c.hname)
                        P.op("act", lambda e, g=g, bk=bk: e.activation(out=uext[:, g, 16:16 + TT], in_=bk[:, :], func=AF.Copy),
                             reads=[bkn], writes=["uext"])
                else:
                    bk, bkn = tm_group(c, 0, wv, wn, 8, 512, c.hT, c.hname)
                    P.op("act", lambda e, bk=bk: e.activation(out=zs[:, 2048:2560], in_=bk[0:NS, :], func=AF.Copy),
                         reads=[bkn], writes=["zs", "call", "sil"])
                    bk, bkn = tm_group(c, 0, wv, wn, 0, 8, c.hT, c.hname)
                    P.op("act", lambda e, bk=bk: e.activation(out=zs[:, 2560:2568], in_=bk[0:NS, 0:8], func=AF.Copy),
                         reads=[bkn], writes=["zs", "call", "sil"])

            stage(4 + 10 * i)
            mmmode["dense"] = False
            ux3b = uext[:, 3, 16:528].bitcast(BF16)
            STb2 = [(STb[:, :, :], "STb"), (ux3b[:, 0:512].rearrange("p (h c) -> p h c", h=4), "ux3a")]
            qaT2 = [(qaT[:, :, :], "qaT"), (ux3b[:, 512:1024].rearrange("p (h c) -> p h c", h=4), "ux3b")]
            kw2 = [(kw[:, :, :], "kw"), (junk[:, :].rearrange("p (h c) -> p h c", h=4), "junk")]
            nsb2 = [(uext[:, 0, 16:528].rearrange("p (h c) -> p h c", h=4), "ux0"),
                    (uext[:, 1, 16:528].rearrange("p (h c) -> p h c", h=4), "ux1")]
            htok2 = [(htok[:, :], "htok"), (uext[:, 2, 16:528], "ux2")]
            tA = tmpA[:, 0:512].rearrange("p (h c) -> p h c", h=4)
            tB = tmpB[:, 0:512].rearrange("p (h c) -> p h c", h=4)

            def pre(j):
                c0 = j * 128
                STj, STn = STb2[j % 2]
                qaj, qan = qaT2[j % 2]
                kwj, kwn = kw2[j % 2]
                def _qk(e):
                    for h in range(4):
                        ins = e.matmul(stv[:, h, :], lhsT=kT[:, h, j * 128:(j + 1) * 128], rhs=qT[:, h, j * 128:(j + 1) * 128],
                                       start=True, stop=True)
                    return ins
                P.op("pe", _qk, reads=["kT", "qT"], writes=["stp"])
                for h in range(4):
                    P.op("dve", lambda e, h=h: e.scalar_tensor_tensor(
                        out=tA[:, h, :], in0=mubc[:, h, 1 + c0:1 + c0 + 128], scalar=ucol[:, j, 0, h:h + 1], in1=mbig[:, :],
                        op0=ALU.subtract, op1=ALU.max), reads=["mubc", "ucol", "mbig"], writes=["tmpA"])
                P.op("act", lambda e: e.activation(out=tA, in_=tA, func=AF.Exp, scale=-1.0), reads=["tmpA"], writes=["tmpA"])
                P.op("dve", lambda e: e.tensor_tensor(out=STj, in0=stv[:, :, :], in1=tA, op=ALU.mult),
                     reads=["stp", "tmpA"], writes=[STn])
                for h in range(4):
                    P.op("act", lambda e, h=h: e.activation(out=abc[:, h, :], in_=mubc[:, h, 1 + c0:1 + c0 + 128], func=AF.Exp,
                                                            scale=-1.0, bias=mubc[:, h, c0:c0 + 1]),
                         reads=["mubc"], writes=["abc"])
                P.op("pool", lambda e: e.tensor_tensor(out=qaj, in0=qT[:, :, j * 128:(j + 1) * 128], in1=abc[:, :, :], op=ALU.mult),
                     reads=["qT", "abc"], writes=[qan])
                P.op("dve", lambda e: e.tensor_copy(out=sm4[:, 40 + 4 * j:44 + 4 * j], in_=abc[:, :, 127]),
                     reads=["abc"], writes=[f"aLs{j}"])
                P.op("dve", lambda e: e.tensor_tensor(out=sm4[:, 28:32], in0=ucol[:, j, 0, :], in1=mubc[:, :, c0 + 128],
                                                      op=ALU.subtract), reads=["ucol", "mubc"], writes=["wL"])
                P.op("act", lambda e: e.activation(out=sm4[:, 28:32], in_=sm4[:, 28:32], func=AF.Exp), reads=["wL"], writes=["wL"])
                P.op("pool", lambda e: e.tensor_tensor(out=kwj, in0=ktok[:, j, :].rearrange("p (h c) -> p h c", h=4),
                                                       in1=sm4[:, 28:32].unsqueeze(2).to_broadcast([128, 4, 128]), op=ALU.mult),
                     reads=["ktok", "wL"], writes=[kwn])

            def num(j):
                STj, STn = STb2[j % 2]
                qaj, qan = qaT2[j % 2]
                nsj, nsn = nsb2[j % 2]
                def _num(e):
                    for h in range(4):
                        e.matmul(numv[:, h, 0:129], lhsT=qaj[:, h, :], rhs=Cbf[:, h, :], start=True, stop=False)
                        ins = e.matmul(numv[:, h, 0:129], lhsT=STj[:, h, :], rhs=vtok[:, j, h, :], start=False, stop=True)
                    return ins
                P.op("pe", _num, reads=[qan, "Cbf", STn, "vtok"], writes=["nump"])
                P.op("act", lambda e: e.activation(out=nsj, in_=numv[:, :, 0:128], func=AF.Copy), reads=["nump"], writes=[nsn])
                P.op("dve", lambda e: e.tensor_copy(out=sm4[:, 72:76], in_=numv[:, :, 128]), reads=["nump"], writes=["hdenraw"])

            def upd(j):
                kwj, kwn = kw2[j % 2]
                def _cu(e):
                    for h in range(4):
                        ins = e.matmul(cupv[:, h, 0:129], lhsT=kwj[:, h, :], rhs=vtok[:, j, h, :], start=True, stop=True)
                    return ins
                P.op("pe", _cu, reads=[kwn, "vtok"], writes=["cup"])
                for h in range(4):
                    P.op("dve", lambda e, h=h: e.scalar_tensor_tensor(out=Cst[:, h, :], in0=Cst[:, h, :],
                                                                      scalar=sm4[:, 40 + 4 * j + h:41 + 4 * j + h],
                                                                      in1=cupv[:, h, 0:129], op0=ALU.mult, op1=ALU.add),
                         reads=["Cst", f"aLs{j}", "cup"], writes=["Cst"])
                P.op("act", lambda e: e.activation(out=Cbf[:, :, :], in_=Cst[:, :, :], func=AF.Copy), reads=["Cst"], writes=["Cbf"])

            def post_a(j):
                nsj, nsn = nsb2[j % 2]
                htj, htn = htok2[j % 2]
                P.op("act", lambda e: e.activation(out=tB, in_=nsj[:, :, 0:128], func=AF.Square), reads=[nsn], writes=["tmpB"])
                P.op("dve", lambda e: e.tensor_reduce(out=sm4[:, 16:20], in_=tB, axis=AX.X, op=ALU.add), reads=["tmpB"], writes=["hss"])
                P.op("dve", lambda e: e.tensor_scalar(out=sm4[:, 36:40], in0=sm4[:, 72:76], scalar1=-1.0, scalar2=None, op0=ALU.mult),
                     reads=["hdenraw"], writes=["hden2"])
                P.op("dve", lambda e: e.tensor_tensor(out=sm4[:, 20:24], in0=sm4[:, 36:40], in1=sm4[:, 72:76], op=ALU.max),
                     reads=["hdenraw", "hden2"], writes=["hden"])
                P.op("dve", lambda e: e.tensor_tensor(out=sm4[:, 20:24], in0=sm4[:, 20:24], in1=ucol[:, j, 1, :], op=ALU.max),
                     reads=["hden", "ucol"], writes=["hden"])
                P.op("dve", lambda e: e.reciprocal(out=sm4[:, 20:24], in_=sm4[:, 20:24]), reads=["hden"], writes=["hden"])
                P.op("dve", lambda e: e.tensor_tensor(out=sm4[:, 24:28], in0=sm4[:, 20:24], in1=sm4[:, 20:24], op=ALU.mult),
                     reads=["hden"], writes=["hrs"])
                P.op("dve", lambda e: e.tensor_tensor(out=sm4[:, 24:28], in0=sm4[:, 24:28], in1=sm4[:, 16:20], op=ALU.mult),
                     reads=["hrs", "hss"], writes=["hrs"])
                P.op("act", lambda e: e.activation(out=sm4[:, 24:28], in_=sm4[:, 24:28], func=AF.Ln, scale=1.0 / 128, bias=epsc[:, :]),
                     reads=["hrs"], writes=["hrs"])
                P.op("act", lambda e: e.activation(out=sm4[:, 24:28], in_=sm4[:, 24:28], func=AF.Exp, scale=-0.5),
                     reads=["hrs"], writes=["hrs"])
                P.op("dve", lambda e: e.tensor_tensor(out=sm4[:, 24:28], in0=sm4[:, 24:28], in1=sm4[:, 20:24], op=ALU.mult),
                     reads=["hrs", "hden"], writes=["hrs"])
                for h in range(4):
                    eng = "dve" if h % 2 == 0 else "pool"
                    if eng == "dve":
                        P.op("dve", lambda e, h=h: e.scalar_tensor_tensor(
                            out=htj[:, h * 128:(h + 1) * 128], in0=nsj[:, h, 0:128], scalar=sm4[:, 24 + h:25 + h],
                            in1=GO[:, j, h * 128:(h + 1) * 128], op0=ALU.mult, op1=ALU.mult),
                            reads=[nsn, "hrs", "GO"], writes=[htn])
                    else:
                        P.op("dve", lambda e, h=h: e.scalar_tensor_tensor(
                            out=htj[:, h * 128:(h + 1) * 128], in0=nsj[:, h, 0:128], scalar=sm4[:, 24 + h:25 + h],
                            in1=GO[:, j, h * 128:(h + 1) * 128], op0=ALU.mult, op1=ALU.mult),
                            reads=[nsn, "hrs", "GO"], writes=[htn])

            def post_b(j):
                htj, htn = htok2[j % 2]
                bk, bkn = nextmm()
                def _htr(e):
                    for h in range(4):
                        ins = e.transpose(out=bk[:, h * 128:(h + 1) * 128], in_=htj[:, h * 128:(h + 1) * 128], identity=ident[:, :])
                    return ins
                P.op("pe", _htr, reads=[htn, "ident"], writes=[bkn])
                P.op("act", lambda e: e.activation(out=hT[:, 0:4, j * 128:(j + 1) * 128],
                                                   in_=bk[:, :].rearrange("p (h c) -> p h c", h=4), func=AF.Copy),
                     reads=[bkn], writes=["hT"])

            gq = list(pending_G)
            pending_G = []
            for g in range(4):
                pool_group(g, [(tmpA, "tmpA"), (tmpB, "tmpB")], lambda g: hT[:, 4 + g, :], "hT", i == 0)
                if gq:
                    gq.pop(0)()
            if i == NTILE - 1:
                bk, bkn = nextmm()
                def _ptr(e, bk=bk):
                    for g in range(4):
                        ins = e.transpose(out=bk[0:15, g * 128:(g + 1) * 128], in_=uext[:, g, 513:528], identity=ident[:, :])
                    return ins
                P.op("pe", _ptr, reads=["uext", "ident"], writes=[bkn])
                P.op("dve", lambda e, bk=bk: e.tensor_copy(out=tmpA[0:15, 0:512], in_=bk[0:15, :]), reads=[bkn], writes=["tmpA"])
                P.dma("sp", lambda e: e.dma_start(out=pool_p[:, :], in_=tmpA[0:15, 0:512]), slot="o_pool", reads=["tmpA"], is_output=True)
            else:
                P.op("pool", lambda e: e.tensor_copy(out=uext[:, :, 0:16], in_=uext[:, :, 512:528]), reads=["uext"], writes=["uext"])
            UXN = ["uext", "ux0", "ux1", "ux2", "ux3a", "ux3b"]
            P.op("dve", lambda e: e.memset(sm4[:, 79:80], 0.0), reads=[], writes=UXN)
            steps = [lambda: pre(0), lambda: pre(1),
                     lambda: (num(0), upd(0), post_a(0)), lambda: pre(2),
                     lambda: (num(1), upd(1), post_a(1), post_b(0)), lambda: pre(3),
                     lambda: (num(2), upd(2), post_a(2), post_b(1)),
                     lambda: (num(3), upd(3), post_a(3), post_b(2)), lambda: post_b(3)]
            for k, st_ in enumerate(steps):
                st_()
                for _ in range(1 if k in (0, 8) else 2):
                    if gq:
                        gq.pop(0)()
            while gq:
                gq.pop(0)()
            P.op("dve", lambda e: e.memset(sm4[:, 79:80], 0.0), reads=[], writes=UXN)
            pref3 = load_block(3) if pending_post is not None else None
            if pending_post is not None:
                pending_post()
                pending_post = None
            if i >= 1:
                load_x(cp)

            stage(5 + 10 * i)
            if i == NTILE - 1:
                P.dma("sp", lambda e: e.dma_start(out=C_p.rearrange("h d e -> d h e"), in_=Cst[:, :, 0:128]), slot="o_C",
                      reads=["Cst"], is_output=True)
                P.dma("sp", lambda e: e.dma_start(out=n_p.rearrange("h d -> d h"), in_=Cst[:, :, 128]), slot="o_n",
                      reads=["Cst"], is_output=True)
                P.op("dve", lambda e: e.tensor_scalar(out=mfin[:, :], in0=negB1[:, 512:513], scalar1=-1.0, scalar2=None, op0=ALU.mult),
                     reads=["negB"], writes=["mfin"])
                P.dma("sp", lambda e: e.dma_start(out=m_p[:, :], in_=mfin[:, :]), slot="o_m", reads=["mfin"], is_output=True)

            stage(6 + 10 * i)
            if i == 0:
                sample_mixer(P, nc, locals())
            stage(7 + 10 * i)
            mmmode["dense"] = True

            if i == 0:
                do_mod(4)
                do_mod(5)
            wv, wn = pref3 if pref3 is not None else load_block(3)
            for c in ctxs:
                for j in range(c.ng):
                    for half in range(2):
                        bk, bkn = tm_group(c, j, wv, wn, half * 512, 512, c.hT, c.hname)
                        extra3 = []
                        P.op("act", lambda e, c=c, j=j, half=half, bk=bk: e.activation(
                            out=c.ysb(j)[:, half * 512:(half + 1) * 512], in_=bk[0:c.Pn, :], func=AF.Copy),
                            reads=[bkn], writes=[c.yname(j)] + extra3)
                postnorm_all(c, 0, "po1")
            stage(8 + 10 * i)
            if i == 0:
                for t in (6, 7, 8, 9):
                    do_mod(t)
            for c in ctxs:
                prenorm(c, 1)
            stage(9 + 10 * i)
            if i + 1 < NTILE:
                early_prenorm_a(i + 1)
            for fb in range(4):
                wv, wn = load_block(4 + fb)
                for c in ctxs:
                    for m in range(8):
                        bk, bkn = fm_group(c, wv, wn, m * 128, c.hT, c.hname)
                        rt = tmpA if (m % 2 == 0) else tmpB
                        rn = "tmpA" if (m % 2 == 0) else "tmpB"
                        P.op("act", lambda e, c=c, bk=bk, rt=rt: e.activation(out=rt[:, 0:c.nt], in_=bk[:, 0:c.nt], func=AF.Relu),
                             reads=[bkn], writes=[rn])
                        eng = "dve" if (m % 2 == 0) else "pool"
                        extra = SAMPLE_ARENA if (i == 0 and fb == 0 and m == 0 and c.kind == "p") else []
                        P.op(eng, lambda e, c=c, fb=fb, m=m, rt=rt: e.tensor_tensor(
                            out=c.fT[:, fb * 8 + m, 0:c.nt], in0=rt[:, 0:c.nt], in1=rt[:, 0:c.nt], op=ALU.mult),
                            reads=[rn], writes=[c.fname] + extra)
            stage(10 + 10 * i)
            if i == 0:
                do_mod(10)
                do_mod(11)
            if i + 1 < NTILE:
                early_prenorm_b()
            ctxs_i = list(ctxs)

            def make_G(ctxs_i):
                state = {}

                def emit_one(cb, c, j, first):
                    if first:
                        state["w"] = load_block(8 + cb)
                    wv, wn = state["w"]
                    bk, bkn = tm_group(c, j, wv, wn, 0, 256, c.fT, c.fname, K=32)
                    P.op("act", lambda e: e.activation(out=c.ysb(j)[:, cb * 256:(cb + 1) * 256], in_=bk[0:c.Pn, 0:256], func=AF.Copy),
                         reads=[bkn], writes=[c.yname(j)])
                out = []
                for cb in range(4):
                    for ci, c in enumerate(ctxs_i):
                        for j in range(c.ng):
                            out.append(lambda cb=cb, c=c, j=j, first=(ci == 0 and j == 0): emit_one(cb, c, j, first))
                return out

            def make_post(ctxs_i):
                def post2():
                    for c in ctxs_i:
                        postnorm_all(c, 1, "po2", to_ysb=True)
                        for j in range(c.ng):
                            P.dma("sp", lambda e, c=c, j=j: e.dma_start(out=c.dst(j), in_=c.ysb(j)), slot=f"sty{c.pfx}{j}",
                                  reads=[c.yname(j)], is_output=True)
                return post2
            pending_G = make_G(ctxs_i)
            pending_post = make_post(ctxs_i)
            pref = None
            if i == 0:
                for g_ in pending_G:
                    g_()
                pending_G = []
                pref = [load_block(0), load_block(1), load_block(2)]
                pending_post()
                pending_post = None
        for g_ in pending_G:
            g_()
        pending_post()
        P.emit()
    return nc


def sample_mixer(P, nc, L):
    g = lambda n: L[n]
    zs, hist, C0s, qTs, wkTs, qCs, abcs, n0s = g("zs"), g("hist"), g("C0s"), g("qTs"), g("wkTs"), g("qCs"), g("abcs"), g("n0s")
    sms, ident, ghbc, epsc, tmpA, tmpB, junk = g("sms"), g("ident"), g("ghbc"), g("epsc"), g("tmpA"), g("tmpB"), g("junk")
    onec, stp = g("onec"), g("stp")
    identb, arena = g("identb"), g("arena")
    vb = arena[0:NS, 14848:15360]
    P.op("act", lambda e: e.activation(out=vb, in_=zs[:, 1024:1536], func=AF.Copy), reads=["zs"], writes=["vb"])
    hT_s, pooledT_s, wpool_b, pscol = g("hT_s"), g("pooledT_s"), g("wpool_b"), g("pscol")
    sC, sn, sm, spool = g("sC"), g("sn"), g("sm"), g("spool")
    C_s, n_s, m_s, pool_s = g("C_s"), g("n_s"), g("m_s"), g("pool_s")
    b_ig, b_fg = g("b_ig"), g("b_fg")
    nextmm = g("nextmm")
    ar32 = g("ar32")
    v4 = lambda ap: ap.rearrange("p (h c) -> p h c", h=4)
    q, k, v, o = zs[:, 0:512], zs[:, 512:1024], zs[:, 1024:1536], zs[:, 1536:2048]
    u = zs[:, 2048:2560]
    gates = zs[:, 2560:2568]
    S = lambda a, b: sms[:, a:b]
    P.dma("sp", lambda e: e.dma_start(out=S(52, 56), in_=sm[:, :]), slot="s0", writes=["s_m0"])
    P.dma("sp", lambda e: e.dma_start(out=S(56, 60), in_=b_ig.partition_broadcast(NS)), slot="s1", writes=["s_big"])
    P.dma("sp", lambda e: e.dma_start(out=S(60, 64), in_=b_fg.partition_broadcast(NS)), slot="s2", writes=["s_bfg"])
    P.dma("sp", lambda e: e.dma_start(out=n0s, in_=sn[:, :]), slot="s3", writes=["n0s"])
    for gi in range(4):
        win = 2 ** (gi + 1)
        r0 = [0, 1, 4, 11][gi]
        P.dma("sp", lambda e, gi=gi, win=win, r0=r0: e.dma_start(out=hist[:, r0:r0 + win - 1, :],
                                                                 in_=spool[:, 16 - win:15, gi * 128:(gi + 1) * 128]),
              slot=f"s4{gi}", writes=[f"hist{gi}"])
    P.dma("sp", lambda e: e.dma_start(out=pool_s[:, 0:14, :], in_=spool[:, 1:15, :]), slot="o_ps0", is_output=True)
    P.dma("sp", lambda e: e.dma_start(out=pool_s[:, 14, :], in_=u), slot="o_ps1", reads=["zs"], is_output=True)
    P.op("dve", lambda e: e.tensor_tensor(out=S(16, 20), in0=gates[:, 0:4], in1=S(56, 60), op=ALU.add), reads=["zs", "s_big"], writes=["s_ig"])
    P.op("dve", lambda e: e.tensor_tensor(out=S(20, 24), in0=gates[:, 4:8], in1=S(60, 64), op=ALU.add), reads=["zs", "s_bfg"], writes=["s_fg"])
    P.op("act", lambda e: e.activation(out=S(20, 24), in_=S(20, 24), func=AF.Exp, scale=-1.0), reads=["s_fg"], writes=["s_fg"])
    P.op("act", lambda e: e.activation(out=S(20, 24), in_=S(20, 24), func=AF.Ln, bias=onec[0:NS, :]), reads=["s_fg", "onec"], writes=["s_fg"])
    P.op("dve", lambda e: e.tensor_tensor(out=S(20, 24), in0=S(52, 56), in1=S(20, 24), op=ALU.subtract), reads=["s_fg", "s_m0"], writes=["s_fg"])
    P.op("dve", lambda e: e.tensor_tensor(out=S(24, 28), in0=S(20, 24), in1=S(16, 20), op=ALU.max), reads=["s_fg", "s_ig"], writes=["s_m"])
    P.dma("sp", lambda e: e.dma_start(out=m_s[:, :], in_=S(24, 28)), slot="o_ms", reads=["s_m"], is_output=True)
    P.op("dve", lambda e: e.tensor_tensor(out=S(28, 32), in0=S(16, 20), in1=S(24, 28), op=ALU.subtract), reads=["s_ig", "s_m"], writes=["s_w"])
    P.op("dve", lambda e: e.tensor_tensor(out=S(32, 36), in0=S(20, 24), in1=S(24, 28), op=ALU.subtract), reads=["s_fg", "s_m"], writes=["s_a"])
    P.op("act", lambda e: e.activation(out=S(28, 36), in_=S(28, 36), func=AF.Exp), reads=["s_w", "s_a"], writes=["s_w", "s_a"])
    P.op("act", lambda e: e.activation(out=S(36, 40), in_=S(24, 28), func=AF.Exp, scale=-1.0), reads=["s_m"], writes=["s_e"])
    t512 = tmpA[0:NS, 0:512]
    P.op("dve", lambda e: e.tensor_tensor(out=t512, in0=q, in1=k, op=ALU.mult), reads=["zs"], writes=["tmpA"])
    P.op("dve", lambda e: e.tensor_reduce(out=S(40, 44), in_=v4(t512), axis=AX.X, op=ALU.add), reads=["tmpA"], writes=["s_qk"])
    P.op("dve", lambda e: e.scalar_tensor_tensor(out=S(40, 44), in0=S(40, 44), scalar=128.0 ** -0.5, in1=S(28, 32), op0=ALU.mult, op1=ALU.mult),
         reads=["s_qk", "s_w"], writes=["s_qk"])
    P.op("dve", lambda e: e.tensor_tensor(out=t512, in0=q, in1=n0s, op=ALU.mult), reads=["zs", "n0s", "tmpA"], writes=["tmpA"])
    P.op("dve", lambda e: e.tensor_reduce(out=S(44, 48), in_=v4(t512), axis=AX.X, op=ALU.add), reads=["tmpA"], writes=["s_den"])
    P.op("dve", lambda e: e.tensor_tensor(out=S(44, 48), in0=S(44, 48), in1=S(32, 36), op=ALU.mult), reads=["s_den", "s_a"], writes=["s_den"])
    P.op("dve", lambda e: e.tensor_tensor(out=S(44, 48), in0=S(44, 48), in1=S(40, 44), op=ALU.add), reads=["s_den", "s_qk"], writes=["s_den"])
    P.op("dve", lambda e: e.tensor_scalar(out=S(68, 72), in0=S(44, 48), scalar1=-1.0, scalar2=None, op0=ALU.mult), reads=["s_den"], writes=["s_den2"])
    P.op("dve", lambda e: e.tensor_tensor(out=S(44, 48), in0=S(44, 48), in1=S(68, 72), op=ALU.max), reads=["s_den", "s_den2"], writes=["s_den"])
    P.op("dve", lambda e: e.tensor_tensor(out=S(44, 48), in0=S(44, 48), in1=S(36, 40), op=ALU.max), reads=["s_den", "s_e"], writes=["s_den"])
    P.op("dve", lambda e: e.reciprocal(out=S(44, 48), in_=S(44, 48)), reads=["s_den"], writes=["s_den"])
    tB = tmpB[0:NS, 0:512]
    bc4 = lambda a: a.unsqueeze(2).to_broadcast([NS, 4, 128])
    P.op("dve", lambda e: e.tensor_tensor(out=v4(tB), in0=v4(k), in1=bc4(S(28, 32)), op=ALU.mult), reads=["zs", "s_w"], writes=["tmpB"])
    P.op("dve", lambda e: e.tensor_scalar(out=tB, in0=tB, scalar1=128.0 ** -0.5, scalar2=None, op0=ALU.mult), reads=["tmpB"], writes=["tmpB"])
    P.op("dve", lambda e: e.tensor_tensor(out=v4(n0s), in0=v4(n0s), in1=bc4(S(32, 36)), op=ALU.mult), reads=["n0s", "s_a", "tmpA"], writes=["n0s"])
    P.op("dve", lambda e: e.tensor_tensor(out=n0s, in0=n0s, in1=tB, op=ALU.add), reads=["n0s", "tmpB"], writes=["n0s"])
    P.dma("sp", lambda e: e.dma_start(out=n_s[:, :], in_=n0s), slot="o_ns", reads=["n0s"], is_output=True)
    bk, bkn = nextmm()
    def _tq(e):
        for h in range(4):
            e.transpose(out=bk[:, h * NS:(h + 1) * NS], in_=q[:, h * 128:(h + 1) * 128], identity=ident[0:NS, 0:NS])
        for h in range(4):
            ins = e.transpose(out=bk[:, 64 + h * NS:64 + (h + 1) * NS], in_=tB[:, h * 128:(h + 1) * 128], identity=ident[0:NS, 0:NS])
        return ins
    P.op("pe", _tq, reads=["zs", "tmpB", "ident"], writes=[bkn])
    P.op("dve", lambda e: e.tensor_copy(out=ar32[:, 7168:7168 + 128], in_=bk[:, 0:128]), reads=[bkn], writes=["qTs", "wkTs"])
    for b in range(NS):
        Cb = C0s[b % 4]
        cn = f"ysb{b % 4}"
        if b == 0:
            for bb in range(4):
                P.dma("sp", lambda e, bb=bb: e.dma_start(out=C0s[bb], in_=sC[bb].rearrange("h d e -> d h e")),
                      slot=f"ldC{bb}", writes=[f"ysb{bb}"])
        def _mv(e, b=b, Cb=Cb):
            for h in range(4):
                ins = e.matmul(stp[:, h * NS + b:h * NS + b + 1], lhsT=Cb[:, h, :], rhs=qTs[:, h, b:b + 1], start=True, stop=True)
            return ins
        P.op("pe", _mv, reads=[cn, "qTs"], writes=["stp"])
        selb = ident[0:NS, b:b + 1].to_broadcast([NS, 128])
        selbb = identb[:, b:b + 1].to_broadcast([NS, 128])
        bkv, bkvn = nextmm()
        P.op("pe", lambda e, selbb=selbb, bkv=bkv: e.matmul(bkv[:, :], lhsT=selbb, rhs=vb, start=True, stop=True),
             reads=["identb", "vb"], writes=[bkvn])
        bka, bkan = nextmm()
        P.op("pe", lambda e, selb=selb, bka=bka: e.matmul(bka[:, 0:4], lhsT=selb, rhs=S(32, 36), start=True, stop=True),
             reads=["ident", "s_a"], writes=[bkan])
        P.op("dve", lambda e, b=b, bka=bka: e.tensor_copy(out=abcs[:, b, :], in_=bka[:, 0:4]), reads=[bkan], writes=["abcs"])
        tv = tmpA[:, 0:512]
        P.op("dve", lambda e, b=b, bkv=bkv, tv=tv: e.tensor_tensor(out=v4(tv), in0=v4(bkv[:, :]),
                                                                   in1=wkTs[:, :, b].unsqueeze(2).to_broadcast([128, 4, 128]), op=ALU.mult),
             reads=[bkvn, "wkTs"], writes=["tmpA"])
        P.op("dve", lambda e, b=b, Cb=Cb: e.tensor_tensor(out=Cb, in0=Cb, in1=abcs[:, b, :].unsqueeze(2).to_broadcast([128, 4, 128]), op=ALU.mult),
             reads=[cn, "abcs"], writes=[cn])
        P.op("dve", lambda e, Cb=Cb, tv=tv: e.tensor_tensor(out=Cb, in0=Cb, in1=v4(tv), op=ALU.add), reads=[cn, "tmpA"], writes=[cn])
        P.dma("sp", lambda e, b=b, Cb=Cb: e.dma_start(out=C_s[b].rearrange("h d e -> d h e"), in_=Cb), slot=f"stC{b % 4}",
              reads=[cn], is_output=True)
        if b + 4 < NS:
            P.dma("sp", lambda e, b=b, Cb=Cb: e.dma_start(out=Cb, in_=sC[b + 4].rearrange("h d e -> d h e")),
                  slot=f"ldC{b % 4}", writes=[cn])
    P.op("dve", lambda e: e.tensor_copy(out=qCs, in_=stp[:, 0:64]), reads=["stp"], writes=["qCs"])
    bk2, bk2n = nextmm()
    def _tqc(e):
        for h in range(4):
            ins = e.transpose(out=bk2[0:NS, h * 128:(h + 1) * 128], in_=qCs[:, h * NS:(h + 1) * NS], identity=ident[:, :])
        return ins
    P.op("pe", _tqc, reads=["qCs", "ident"], writes=[bk2n])
    tn = tmpA[0:NS, 0:512]
    P.op("dve", lambda e: e.tensor_tensor(out=v4(tn), in0=v4(bk2[0:NS, :]), in1=bc4(S(32, 36)), op=ALU.mult), reads=[bk2n, "s_a", "tmpA"], writes=["tmpA"])
    tv2 = tmpB[0:NS, 0:512]
    P.op("dve", lambda e: e.tensor_tensor(out=v4(tv2), in0=v4(v), in1=bc4(S(40, 44)), op=ALU.mult), reads=["zs", "s_qk", "tmpB"], writes=["tmpB"])
    P.op("dve", lambda e: e.tensor_tensor(out=tn, in0=tn, in1=tv2, op=ALU.add), reads=["tmpA", "tmpB"], writes=["tmpA"])
    P.op("dve", lambda e: e.tensor_tensor(out=v4(tn), in0=v4(tn), in1=bc4(S(44, 48)), op=ALU.mult), reads=["tmpA", "s_den"], writes=["tmpA"])
    P.op("dve", lambda e: e.tensor_tensor(out=tv2, in0=tn, in1=tn, op=ALU.mult), reads=["tmpA", "tmpB"], writes=["tmpB"])
    P.op("dve", lambda e: e.tensor_reduce(out=S(48, 52), in_=v4(tv2), axis=AX.X, op=ALU.add), reads=["tmpB"], writes=["s_ss"])
    P.op("act", lambda e: e.activation(out=S(48, 52), in_=S(48, 52), func=AF.Ln, scale=1.0 / 128, bias=epsc[0:NS, :]), reads=["s_ss"], writes=["s_ss"])
    P.op("act", lambda e: e.activation(out=S(48, 52), in_=S(48, 52), func=AF.Exp, scale=-0.5), reads=["s_ss"], writes=["s_ss"])
    P.op("dve", lambda e: e.tensor_tensor(out=v4(tn), in0=v4(tn), in1=bc4(S(48, 52)), op=ALU.mult), reads=["tmpA", "s_ss"], writes=["tmpA"])
    P.op("dve", lambda e: e.tensor_tensor(out=v4(tn), in0=v4(tn), in1=ghbc[0:NS, :].unsqueeze(1).to_broadcast([NS, 4, 128]), op=ALU.mult),
         reads=["tmpA", "ghbc"], writes=["tmpA"])
    P.op("act", lambda e: e.activation(out=tv2, in_=o, func=AF.Exp, scale=-1.0), reads=["zs", "tmpB"], writes=["tmpB"])
    P.op("dve", lambda e: e.tensor_scalar(out=tv2, in0=tv2, scalar1=1.0, scalar2=None, op0=ALU.add), reads=["tmpB"], writes=["tmpB"])
    P.op("dve", lambda e: e.reciprocal(out=tv2, in_=tv2), reads=["tmpB"], writes=["tmpB"])
    P.op("dve", lambda e: e.tensor_tensor(out=tn, in0=tn, in1=tv2, op=ALU.mult), reads=["tmpA", "tmpB"], writes=["tmpA"])
    bk3, bk3n = nextmm()
    def _th(e):
        for h in range(4):
            ins = e.transpose(out=bk3[:, h * NS:(h + 1) * NS], in_=tn[:, h * 128:(h + 1) * 128], identity=ident[0:NS, 0:NS])
        return ins
    P.op("pe", _th, reads=["tmpA", "ident"], writes=[bk3n])
    P.op("act", lambda e: e.activation(out=hT_s[:, 0:4, :], in_=bk3[:, 0:64].rearrange("p (h b) -> p h b", h=4), func=AF.Copy),
         reads=[bk3n], writes=["hT_s"])
    tp = tmpB[0:NS, 0:512]
    for gi in range(4):
        win = 2 ** (gi + 1)
        r0 = [0, 1, 4, 11][gi]
        P.op("dve", lambda e, gi=gi, win=win, r0=r0: e.tensor_reduce(
            out=tp[:, gi * 128:(gi + 1) * 128], in_=hist[:, r0:r0 + win - 1, :].rearrange("p r c -> p c r"), axis=AX.X, op=ALU.add),
            reads=[f"hist{gi}", "tmpB"], writes=["tmpB"])
        P.op("dve", lambda e, gi=gi: e.tensor_tensor(out=tp[:, gi * 128:(gi + 1) * 128], in0=tp[:, gi * 128:(gi + 1) * 128],
                                                     in1=u[:, gi * 128:(gi + 1) * 128], op=ALU.add), reads=["tmpB", "zs"], writes=["tmpB"])
        P.op("dve", lambda e, gi=gi, win=win: e.scalar_tensor_tensor(
            out=tp[:, gi * 128:(gi + 1) * 128], in0=tp[:, gi * 128:(gi + 1) * 128], scalar=1.0 / win,
            in1=u[:, gi * 128:(gi + 1) * 128], op0=ALU.mult, op1=ALU.subtract), reads=["tmpB", "zs"], writes=["tmpB"])
    bk4, bk4n = nextmm()
    def _tp(e):
        for gi in range(4):
            ins = e.transpose(out=bk4[:, gi * NS:(gi + 1) * NS], in_=tp[:, gi * 128:(gi + 1) * 128], identity=ident[0:NS, 0:NS])
        return ins
    P.op("pe", _tp, reads=["tmpB", "ident"], writes=[bk4n])
    P.op("act", lambda e: e.activation(out=pooledT_s[:, :, :], in_=bk4[:, 0:64].rearrange("p (h b) -> p h b", h=4), func=AF.Copy),
         reads=[bk4n], writes=["pooledT_s"])
    for gi in range(4):
        bk5, bk5n = nextmm()
        P.op("pe", lambda e, gi=gi, bk5=bk5: e.matmul(bk5[:, 0:NS], lhsT=wpool_b[:, gi, :], rhs=pooledT_s[:, gi, :], start=True, stop=True),
             reads=["wpool_b", "pooledT_s"], writes=[bk5n])
        P.op("act", lambda e, gi=gi, bk5=bk5: e.activation(out=hT_s[:, 4 + gi, :], in_=bk5[:, 0:NS], func=AF.Identity, scale=pscol[:, gi:gi + 1]),
             reads=[bk5n, "pscol"], writes=["hT_s"])


_NC_CACHE = {}


def kernel(x_prompt, x_sample, c_prompt, c_sample, state_C, state_n, state_m, state_pool,
           w_ada, b_ada, g_pre1, g_post1, w_in, b_ig, b_fg, g_head, w_pool, pool_scale,
           w_out, g_pre2, g_post2, w_up, w_down):
    f = lambda a: np.ascontiguousarray(np.asarray(a, dtype=np.float32))
    if "nc" not in _NC_CACHE:
        _NC_CACHE["nc"] = build_nc()
    nc = _NC_CACHE["nc"]
    x_prompt = f(x_prompt); x_sample = f(x_sample); c_prompt = f(c_prompt); c_sample = f(c_sample)
    state_C = f(state_C); state_n = f(state_n); state_m = f(state_m); state_pool = f(state_pool)
    shared = {"w_ada": f(w_ada)[0], "b_ada": f(b_ada)[0], "g_pre1": f(g_pre1)[0], "g_post1": f(g_post1)[0],
              "g_pre2": f(g_pre2)[0], "g_post2": f(g_post2)[0], "w_in": f(w_in)[0], "w_out": f(w_out)[0],
              "w_up": f(w_up)[0], "w_down": f(w_down)[0], "b_ig": f(b_ig)[0], "b_fg": f(b_fg)[0],
              "g_head": f(g_head)[0], "w_pool": f(w_pool)[0], "pool_scale": f(pool_scale)[0]}
    in_maps = []
    for c in range(NCORES):
        sl = slice(c * NS, (c + 1) * NS)
        m = dict(shared)
        m["x_p"] = x_prompt[c]
        m["x_s"] = np.ascontiguousarray(x_sample[sl, 0, :])
        m["c_all"] = np.ascontiguousarray(np.concatenate([c_sample[sl], c_prompt[c:c + 1]], axis=0))
        m["sC"] = np.ascontiguousarray(state_C[0, sl])
        m["sn"] = np.ascontiguousarray(state_n[0, sl].reshape(NS, 512))
        m["sm"] = np.ascontiguousarray(state_m[0, sl])
        m["spool"] = np.ascontiguousarray(state_pool[0, sl])
        in_maps.append(m)
    res = run_bass_kernel_spmd(nc, in_maps, core_ids=list(range(NCORES)))
    R = res.results
    cat = lambda k: np.concatenate([np.asarray(r[k]) for r in R], axis=0)
    y_p = np.stack([np.asarray(r["y_p"]) for r in R], axis=0)
    y_s = cat("y_s").reshape(NCORES * NS, 1, D)
    C_p = np.stack([np.asarray(r["C_p"]) for r in R], axis=0)[None]
    n_p = np.stack([np.asarray(r["n_p"]) for r in R], axis=0)[None]
    m_p = np.stack([np.asarray(r["m_p"]).reshape(4) for r in R], axis=0)[None]
    pool_p = np.stack([np.asarray(r["pool_p"]) for r in R], axis=0)[None]
    C_s = cat("C_s")[None]
    n_s = cat("n_s").reshape(NCORES * NS, 4, 128)[None]
    m_s = cat("m_s")[None]
    pool_s = cat("pool_s")[None]
    return (y_p.astype(np.float32), y_s.astype(np.float32), C_p.astype(np.float32), n_p.astype(np.float32),
            m_p.astype(np.float32), pool_p.astype(np.float32), C_s.astype(np.float32), n_s.astype(np.float32),
            m_s.astype(np.float32), pool_s.astype(np.float32))
```
